# Optimizing a Trainium2 kernel written in Bass

```python
import math
import jax, jax.numpy as jnp
from jax import lax
import numpy as np

D_MODEL = 1024
BATCH = 32
SEQ = 2048
DEPTH = 4

CTX_LEN = 256
GRID_W = 64
N_MIXERS = 3
LAYER_TYPES = tuple(i % N_MIXERS for i in range(DEPTH))
N_ATTN = LAYER_TYPES.count(0)
N_S5 = LAYER_TYPES.count(1)
N_HG = LAYER_TYPES.count(2)

DA_HEADS = 8
DA_HEAD_DIM = 64
DA_V_DIM = 2 * DA_HEAD_DIM
Q_BLOCK = 128
ROPE_THETA = 10000.0
S5_GROUP = 16
S5_GROUPS = D_MODEL // S5_GROUP
S5_STATE = 64
HG_HEADS = 8
HG_KEY = D_MODEL // HG_HEADS
HG_VAL = D_MODEL // HG_HEADS
HG_CHUNK = 32
PEER_HEADS = 8
PEER_NKEYS = 128
PEER_EXPERTS = PEER_NKEYS * PEER_NKEYS
PEER_QDIM = 256
PEER_TOPK = 16
PEER_BLOCK = 128
LN_EPS = 1e-5
RMS_EPS = 1e-6
DN_ALPHA = (2 * DEPTH) ** 0.25
DN_BETA = (8 * DEPTH) ** -0.25

kernel_name = "hybrid_diffattn_s5_hgrn2_peer_dit"

F32 = jnp.float32


def layer_norm(x, g, b):
    xf = x.astype(F32)
    mu = jnp.mean(xf, -1, keepdims=True)
    var = jnp.mean(jnp.square(xf - mu), -1, keepdims=True)
    return ((xf - mu) * lax.rsqrt(var + LN_EPS)).astype(x.dtype) * g + b


def rms_norm(x):
    xf = x.astype(F32)
    return (xf * lax.rsqrt(jnp.mean(xf * xf, -1, keepdims=True) + RMS_EPS)).astype(x.dtype)


def axial_rope(length, dim):
    rows = length // GRID_W
    row = jnp.repeat(jnp.arange(rows, dtype=F32), GRID_W)
    col = jnp.tile(jnp.arange(GRID_W, dtype=F32), rows)
    n_freq = dim // 4
    inv = ROPE_THETA ** (-jnp.arange(n_freq, dtype=F32) / n_freq)
    ang = jnp.concatenate([row[:, None] * inv, col[:, None] * inv], axis=-1)
    return jnp.cos(ang), jnp.sin(ang)


def apply_rope(x, cos, sin):
    c = cos[None, :, None, None].astype(x.dtype)
    s = sin[None, :, None, None].astype(x.dtype)
    x1, x2 = x[..., 0::2], x[..., 1::2]
    return jnp.stack([x1 * c - x2 * s, x1 * s + x2 * c], axis=-1).reshape(x.shape)


def diff_softmax_mix(q, k, v, lam):
    s = jnp.einsum('bqhcd,bkhcd->bhcqk', q, k).astype(F32)
    p = jax.nn.softmax(s, axis=-1)
    a = p[:, :, 0] - lam * p[:, :, 1]
    return jnp.einsum('bhqk,bkhe->bqhe', a.astype(v.dtype), v)


def diff_attention(u_ctx, u_lat, w_in, w_out, lam_q, lam_k, subln_g, lam_init, cos, sin, need_ctx):
    H, d = DA_HEADS, DA_HEAD_DIM

    def project(u):
        b_, n, _ = u.shape
        q, k, v = jnp.split(u @ w_in, 3, axis=-1)
        return (q.reshape(b_, n, H, 2, d) * d ** -0.5,
                k.reshape(b_, n, H, 2, d),
                v.reshape(b_, n, H, 2 * d))

    qc, kc, vc = project(u_ctx)
    ql, kl, vl = project(u_lat)
    ql, kl = apply_rope(ql, cos, sin), apply_rope(kl, cos, sin)
    lq, lk = lam_q.astype(F32), lam_k.astype(F32)
    lam = jnp.exp(jnp.sum(lq[0] * lk[0])) - jnp.exp(jnp.sum(lq[1] * lk[1])) + lam_init
    k_all = jnp.concatenate([kl, kc], axis=1)
    v_all = jnp.concatenate([vl, vc], axis=1)
    b_, L = ql.shape[:2]
    nb = L // Q_BLOCK
    q_blocks = jnp.moveaxis(ql.reshape(b_, nb, Q_BLOCK, H, 2, d), 1, 0)
    o_lat = lax.map(lambda qb: diff_softmax_mix(qb, k_all, v_all, lam), q_blocks)
    o_lat = jnp.moveaxis(o_lat, 0, 1).reshape(b_, L, H, 2 * d)

    def finish(o):
        b2, n = o.shape[:2]
        o = rms_norm(o) * subln_g * (1.0 - lam_init)
        return o.reshape(b2, n, D_MODEL) @ w_out

    o_ctx = finish(diff_softmax_mix(qc, kc, vc, lam)) if need_ctx else None
    return finish(o_lat), o_ctx


def s5_discretize(lam_re, lam_im, log_dt, b_re, b_im):
    lam_re, lam_im = lam_re.astype(F32), lam_im.astype(F32)
    b_re, b_im = b_re.astype(F32), b_im.astype(F32)
    dt = jnp.exp(log_dt.astype(F32))[:, None]
    mag = jnp.exp(lam_re * dt)
    abar_re, abar_im = mag * jnp.cos(lam_im * dt), mag * jnp.sin(lam_im * dt)
    nr, ni = abar_re - 1.0, abar_im
    den = lam_re * lam_re + lam_im * lam_im
    k_re = (nr * lam_re + ni * lam_im) / den
    k_im = (ni * lam_re - nr * lam_im) / den
    bb_re = k_re[..., None] * b_re - k_im[..., None] * b_im
    bb_im = k_re[..., None] * b_im + k_im[..., None] * b_re
    return abar_re, abar_im, bb_re, bb_im


def complex_affine_combine(e1, e2):
    a1r, a1i, b1r, b1i = e1
    a2r, a2i, b2r, b2i = e2
    return (a2r * a1r - a2i * a1i, a2r * a1i + a2i * a1r,
            a2r * b1r - a2i * b1i + b2r, a2r * b1i + a2i * b1r + b2i)


def s5_scan(u, h0, abar_re, abar_im, bb_re, bb_im, reverse):
    br = jnp.einsum('gpm,bngm->bngp', bb_re, u)
    bi = jnp.einsum('gpm,bngm->bngp', bb_im, u)
    if h0 is not None:
        edge = -1 if reverse else 0
        h0r, h0i = h0
        br = br.at[:, edge].add(abar_re * h0r - abar_im * h0i)
        bi = bi.at[:, edge].add(abar_re * h0i + abar_im * h0r)
    n = u.shape[1]
    ar = jnp.broadcast_to(abar_re, (1, n) + abar_re.shape)
    ai = jnp.broadcast_to(abar_im, (1, n) + abar_im.shape)
    _, _, xr, xi = lax.associative_scan(complex_affine_combine, (ar, ai, br, bi), reverse=reverse, axis=1)
    return xr, xi


def s5_readout(xr, xi, c_re, c_im):
    return (jnp.einsum('gmp,bngp->bngm', c_re.astype(F32), xr)
            - jnp.einsum('gmp,bngp->bngm', c_im.astype(F32), xi))


def s5_mixer(u_ctx, u_lat, lam_re, lam_im, log_dt, b_re, b_im, c_re, c_im, d_skip, w_glu, need_ctx):
    groups = lambda u: u.astype(F32).reshape(u.shape[0], u.shape[1], S5_GROUPS, S5_GROUP)
    uc, ul = groups(u_ctx), groups(u_lat)
    y_ctx = jnp.zeros_like(uc)
    y_lat = jnp.zeros_like(ul)
    for dr, rev in ((0, False), (1, True)):
        ab_re, ab_im, bb_re, bb_im = s5_discretize(lam_re[dr], lam_im[dr], log_dt[dr], b_re[dr], b_im[dr])
        xr_c, xi_c = s5_scan(uc, None, ab_re, ab_im, bb_re, bb_im, rev)
        edge = 0 if rev else -1
        xr_l, xi_l = s5_scan(ul, (xr_c[:, edge], xi_c[:, edge]), ab_re, ab_im, bb_re, bb_im, rev)
        y_lat = y_lat + s5_readout(xr_l, xi_l, c_re[dr], c_im[dr])
        y_ctx = y_ctx + s5_readout(xr_c, xi_c, c_re[dr], c_im[dr])

    def finish(y, u):
        y = y.reshape(u.shape).astype(u.dtype) + d_skip * u
        z = jax.nn.gelu(y)
        val, gate = jnp.split(z @ w_glu, 2, axis=-1)
        return val * jax.nn.sigmoid(gate)

    o_ctx = finish(y_ctx, u_ctx) if need_ctx else None
    return finish(y_lat, u_lat), o_ctx


def hgrn2_chunk_scan(q, k, v, log_f, s0):
    out_dtype = v.dtype
    b_, n, H, K = q.shape
    C = HG_CHUNK
    nc = n // C
    to_chunks = lambda t: jnp.moveaxis(t.astype(F32).reshape(b_, nc, C, H, t.shape[-1]), 1, 0)
    causal = jnp.tril(jnp.ones((C, C), dtype=bool))[None, :, :, None, None]

    def step(S, inp):
        qc, kc, vc, gc = inp
        cum = jnp.cumsum(gc, axis=1)
        diff = cum[:, :, None] - cum[:, None]
        decay = jnp.exp(jnp.where(causal, diff, -jnp.inf))
        att = jnp.sum(qc[:, :, None] * kc[:, None] * decay, axis=-1)
        o = jnp.einsum('btsh,bshv->bthv', att, vc)
        o = o + jnp.einsum('bthk,bhkv->bthv', qc * jnp.exp(cum), S)
        last = cum[:, -1]
        S = jnp.exp(last)[..., None] * S + jnp.einsum('bshk,bshv->bhkv', kc * jnp.exp(last[:, None] - cum), vc)
        return S, o

    S, o = lax.scan(step, s0, (to_chunks(q), to_chunks(k), to_chunks(v), to_chunks(log_f)))
    o = jnp.moveaxis(o, 0, 1).reshape(b_, n, H, v.shape[-1])
    return o.astype(out_dtype), S


def hgrn2_mixer(u_ctx, u_lat, w_in, w_out, norm_g, lb, need_ctx):
    H, K, V = HG_HEADS, HG_KEY, HG_VAL
    log_lb, log_1m_lb = jnp.log(lb), jnp.log1p(-lb)

    def project(u):
        b_, n, _ = u.shape
        q, f_fw, f_bw, inp, gate = jnp.split(u @ w_in, 5, axis=-1)
        q = jax.nn.silu(q).reshape(b_, n, H, K)
        inp = inp.reshape(b_, n, H, V)
        dirs = []
        for f in (f_fw, f_bw):
            f = f.astype(F32)
            log_f = jnp.logaddexp(log_lb, log_1m_lb + jax.nn.log_sigmoid(f))
            one_m_f = (1.0 - lb) * jax.nn.sigmoid(-f)
            dirs.append((one_m_f.reshape(b_, n, H, K), log_f.reshape(b_, n, H, K)))
        return q, inp, dirs, gate

    qc, ic, dc, gc = project(u_ctx)
    ql, il, dl, gl = project(u_lat)
    s0 = jnp.zeros((u_lat.shape[0], H, K, V), F32)
    o_ctx = jnp.zeros(ic.shape, ic.dtype)
    o_lat = jnp.zeros(il.shape, il.dtype)
    for (kc_, lfc), (kl_, lfl), rev in zip(dc, dl, (False, True)):
        fl = (lambda t: jnp.flip(t, axis=1)) if rev else (lambda t: t)
        oc_d, s_ctx = hgrn2_chunk_scan(fl(qc), fl(kc_), fl(ic), fl(lfc), s0)
        ol_d, _ = hgrn2_chunk_scan(fl(ql), fl(kl_), fl(il), fl(lfl), s_ctx)
        o_ctx = o_ctx + fl(oc_d)
        o_lat = o_lat + fl(ol_d)

    def finish(o, gate):
        b2, n = o.shape[:2]
        o = rms_norm(o) * norm_g.reshape(H, V)
        return (o.reshape(b2, n, D_MODEL) * jax.nn.silu(gate)) @ w_out

    out_ctx = finish(o_ctx, gc) if need_ctx else None
    return finish(o_lat, gl), out_ctx


def peer_ffn(u, w_q, sub_keys, u_tab, v_tab):
    b_, n, D = u.shape
    H, KT, NK = PEER_HEADS, PEER_TOPK, PEER_NKEYS
    tok = u.reshape((b_ * n) // PEER_BLOCK, PEER_BLOCK, D)

    def block(xb):
        q = (xb @ w_q).reshape(PEER_BLOCK, H, 2, PEER_QDIM // 2)
        s = jnp.einsum('thcd,ckd->thck', q, sub_keys).astype(F32)
        s1, i1 = lax.top_k(s[:, :, 0], KT)
        s2, i2 = lax.top_k(s[:, :, 1], KT)
        cand_s = (s1[..., :, None] + s2[..., None, :]).reshape(PEER_BLOCK, H, KT * KT)
        cand_i = (i1[..., :, None] * NK + i2[..., None, :]).reshape(PEER_BLOCK, H, KT * KT)
        top_s, pos = lax.top_k(cand_s, KT)
        idx = jnp.take_along_axis(cand_i, pos, axis=-1)
        g = jax.nn.softmax(top_s, axis=-1)
        act = jax.nn.gelu(jnp.einsum('thkd,td->thk', u_tab[idx], xb))
        return jnp.einsum('thk,thkd->td', (g * act).astype(xb.dtype), v_tab[idx])

    return lax.map(block, tok).reshape(b_, n, D)


def setup_inputs(seed: int = 0) -> dict:
    key = jax.random.key(seed)
    ks = iter(jax.random.split(key, 40))
    nrm = lambda shape, scale: scale * jax.random.normal(next(ks), shape, F32)
    D = D_MODEL
    G, P, M = S5_GROUPS, S5_STATE, S5_GROUP
    x = nrm((BATCH, SEQ, D), 1.0)
    c = nrm((BATCH, D), 1.0)
    ctx = nrm((BATCH, CTX_LEN, D), 1.0)
    c_ctx = nrm((D,), 1.0)
    ada_w = nrm((DEPTH, D, 6 * D), 0.5 * D ** -0.5)
    ada_b = nrm((DEPTH, 6 * D), 0.02)
    ln_g = 1.0 + nrm((DEPTH, 2, D), 0.02)
    ln_b = nrm((DEPTH, 2, D), 0.02)
    da_w_in = nrm((N_ATTN, D, 3 * D), D ** -0.5)
    da_w_out = nrm((N_ATTN, D, D), DN_BETA * D ** -0.5)
    da_lam_q = nrm((N_ATTN, 2, DA_HEAD_DIM), 0.1)
    da_lam_k = nrm((N_ATTN, 2, DA_HEAD_DIM), 0.1)
    da_subln = 1.0 + nrm((N_ATTN, DA_V_DIM), 0.02)
    s5_lam_re = -0.5 + nrm((N_S5, 2, G, P), 0.01)
    s5_lam_im = math.pi * jnp.arange(P, dtype=F32) + nrm((N_S5, 2, G, P), 0.01)
    s5_log_dt = jax.random.uniform(next(ks), (N_S5, 2, G), F32, math.log(1e-3), math.log(1e-1))
    s5_b_re = nrm((N_S5, 2, G, P, M), (2 * M) ** -0.5)
    s5_b_im = nrm((N_S5, 2, G, P, M), (2 * M) ** -0.5)
    s5_c_re = nrm((N_S5, 2, G, M, P), (2 * P) ** -0.5)
    s5_c_im = nrm((N_S5, 2, G, M, P), (2 * P) ** -0.5)
    s5_d = nrm((N_S5, D), 1.0)
    s5_w_glu = jnp.concatenate([nrm((N_S5, D, D), DN_BETA * D ** -0.5), nrm((N_S5, D, D), D ** -0.5)], axis=-1)
    hg_w_in = nrm((N_HG, D, 5 * D), D ** -0.5)
    hg_w_out = nrm((N_HG, D, D), DN_BETA * D ** -0.5)
    hg_norm = 1.0 + nrm((N_HG, D), 0.02)
    hg_lb = nrm((DEPTH, D), 0.1)
    peer_wq = nrm((DEPTH, D, PEER_HEADS * PEER_QDIM), D ** -0.5)
    peer_keys = nrm((DEPTH, 2, PEER_NKEYS, PEER_QDIM // 2), (PEER_QDIM // 2) ** -0.5)
    peer_u = nrm((DEPTH, PEER_EXPERTS, D), D ** -0.5)
    peer_v = nrm((DEPTH, PEER_EXPERTS, D), DN_BETA * PEER_HEADS ** -0.5)
    return {"x": x, "c": c, "ctx": ctx, "c_ctx": c_ctx, "ada_w": ada_w, "ada_b": ada_b,
            "ln_g": ln_g, "ln_b": ln_b, "da_w_in": da_w_in, "da_w_out": da_w_out,
            "da_lam_q": da_lam_q, "da_lam_k": da_lam_k, "da_subln": da_subln,
            "s5_lam_re": s5_lam_re, "s5_lam_im": s5_lam_im, "s5_log_dt": s5_log_dt,
            "s5_b_re": s5_b_re, "s5_b_im": s5_b_im, "s5_c_re": s5_c_re, "s5_c_im": s5_c_im,
            "s5_d": s5_d, "s5_w_glu": s5_w_glu, "hg_w_in": hg_w_in, "hg_w_out": hg_w_out,
            "hg_norm": hg_norm, "hg_lb": hg_lb, "peer_wq": peer_wq, "peer_keys": peer_keys,
            "peer_u": peer_u, "peer_v": peer_v}


def reference(x, c, ctx, c_ctx, ada_w, ada_b, ln_g, ln_b, da_w_in, da_w_out, da_lam_q, da_lam_k, da_subln,
              s5_lam_re, s5_lam_im, s5_log_dt, s5_b_re, s5_b_im, s5_c_re, s5_c_im, s5_d, s5_w_glu,
              hg_w_in, hg_w_out, hg_norm, hg_lb, peer_wq, peer_keys, peer_u, peer_v):
    L = x.shape[1]
    cos, sin = axial_rope(L, DA_HEAD_DIM)
    s_c = jax.nn.silu(c)
    s_ctx = jax.nn.silu(c_ctx)
    lb_soft = jax.nn.softmax(hg_lb.astype(F32), axis=0)
    lb_all = jnp.cumsum(lb_soft, axis=0) - lb_soft[0]
    h, hc = x, ctx
    for i in range(DEPTH):
        kind, slot = LAYER_TYPES[i], i // N_MIXERS
        need_ctx = i < DEPTH - 1
        mod = (s_c @ ada_w[i] + ada_b[i])[:, None, :]
        mod_c = s_ctx @ ada_w[i] + ada_b[i]
        sh1, sc1, g1, sh2, sc2, g2 = jnp.split(mod, 6, axis=-1)
        csh1, csc1, cg1, csh2, csc2, cg2 = jnp.split(mod_c, 6, axis=-1)
        u = h * (1.0 + sc1) + sh1
        uc = hc * (1.0 + csc1) + csh1
        if kind == 0:
            lam_init = 0.8 - 0.6 * math.exp(-0.3 * i)
            o, oc = diff_attention(uc, u, da_w_in[slot], da_w_out[slot], da_lam_q[slot], da_lam_k[slot],
                                   da_subln[slot], lam_init, cos, sin, need_ctx)
        elif kind == 1:
            o, oc = s5_mixer(uc, u, s5_lam_re[slot], s5_lam_im[slot], s5_log_dt[slot], s5_b_re[slot],
                             s5_b_im[slot], s5_c_re[slot], s5_c_im[slot], s5_d[slot], s5_w_glu[slot], need_ctx)
        else:
            o, oc = hgrn2_mixer(uc, u, hg_w_in[slot], hg_w_out[slot], hg_norm[slot], lb_all[i], need_ctx)
        h = layer_norm(DN_ALPHA * h + g1 * o, ln_g[i, 0], ln_b[i, 0])
        f = peer_ffn(h * (1.0 + sc2) + sh2, peer_wq[i], peer_keys[i], peer_u[i], peer_v[i])
        h = layer_norm(DN_ALPHA * h + g2 * f, ln_g[i, 1], ln_b[i, 1])
        if need_ctx:
            hc = layer_norm(DN_ALPHA * hc + cg1 * oc, ln_g[i, 0], ln_b[i, 0])
            fc = peer_ffn(hc * (1.0 + csc2) + csh2, peer_wq[i], peer_keys[i], peer_u[i], peer_v[i])
            hc = layer_norm(DN_ALPHA * hc + cg2 * fc, ln_g[i, 1], ln_b[i, 1])
    return h
```

```python
import numpy as np
from contextlib import ExitStack
import concourse.bass as bass
import concourse.mybir as mybir
from concourse.bass_utils import run_bass_kernel_spmd

F32 = mybir.dt.float32
BF16 = mybir.dt.bfloat16
U32 = mybir.dt.uint32
I32 = mybir.dt.int32
AF = mybir.ActivationFunctionType
ALU = mybir.AluOpType
AX = mybir.AxisListType


class Res:
    __slots__ = ("name", "w", "rd")

    def __init__(self, name):
        self.name = name
        self.w = None
        self.rd = []


class Tl:
    def __init__(self, h, name):
        self.h = h
        self.res = Res(name)
        self.name = name

    def __getitem__(self, idx):
        return self.h[idx]


class K:
    ENG = ("pe", "dve", "act", "pool", "sp")
    GEN = 30000
    NDS = 6
    SKIP_SAME = True
    SKIP_RAW = False

    def __init__(self, nc):
        self.nc = nc
        self.root = ExitStack()
        self.stacks = [self.root]
        self.eng = dict(pe=nc.tensor, dve=nc.vector, act=nc.scalar, pool=nc.gpsimd, sp=nc.sync)
        self.allsems = []
        self.cur = {}
        self.cnt = {}
        self.seen = {e: {} for e in self.ENG}
        for e in self.ENG:
            self.cur[e] = self._newsem("c_" + e)
            self.cnt[e] = 0
        self.dq = {}
        for q in ("sp", "act", "pool"):
            self.dq[q] = dict(idx=[self._newsem(f"d_{q}{i}") for i in range(self.NDS)],
                              val=[0] * self.NDS, n=0)
        self.uid = 0
        self.ninst = 0

    def _newsem(self, name):
        s = self.root.enter_context(self.nc.semaphore(name + f"_{len(self.allsems)}"))
        self.allsems.append(s)
        return len(self.allsems) - 1

    def push(self):
        self.stacks.append(ExitStack())

    def pop(self):
        self.barrier()
        self.stacks.pop().close()

    def sb(self, name, shape, dtype=F32):
        self.uid += 1
        nm = f"{name}_{self.uid}"
        h = self.stacks[-1].enter_context(self.nc.sbuf_tensor(nm, list(shape), dtype))
        return Tl(h, nm)

    def ps(self, name, shape, dtype=F32):
        self.uid += 1
        nm = f"{name}_{self.uid}"
        h = self.stacks[-1].enter_context(self.nc.psum_tensor(nm, list(shape), dtype))
        return Tl(h, nm)

    def dram(self, name, shape, dtype=F32, kind="Internal"):
        t = self.nc.dram_tensor(name, list(shape), dtype, kind=kind)
        tl = Tl(t.ap(), name)
        return tl

    def _wait(self, e, tk):
        idx, v, src = tk
        if src == e and e == "pe":
            return
        if self.seen[e].get(idx, 0) >= v:
            return
        self.eng[e].wait_ge(self.allsems[idx], v)
        self.seen[e][idx] = v

    @staticmethod
    def _r(x):
        return x.res if isinstance(x, Tl) else x

    def _deps(self, e, reads, writes):
        for r in reads:
            r = self._r(r)
            if r.w is not None and (r.w[2] != e or not self.SKIP_RAW):
                self._wait(e, r.w)
        for w in writes:
            w = self._r(w)
            if w.w is not None and (w.w[2] != e or not self.SKIP_SAME):
                self._wait(e, w.w)
            for tk in w.rd:
                if tk[2] != e or not self.SKIP_SAME:
                    self._wait(e, tk)

    def _commit(self, tk, reads, writes):
        for r in reads:
            r = self._r(r)
            r.rd.append(tk)
            if len(r.rd) > 64:
                best = {}
                for t in r.rd:
                    if t[0] not in best or best[t[0]][1] < t[1]:
                        best[t[0]] = t
                r.rd = list(best.values())
        for w in writes:
            w = self._r(w)
            w.w = tk
            w.rd = []

    def op(self, e, fn, reads=(), writes=()):
        self._deps(e, reads, writes)
        ins = fn(self.eng[e])
        if self.cnt[e] >= self.GEN:
            self.cur[e] = self._newsem("c_" + e)
            self.cnt[e] = 0
        self.cnt[e] += 1
        ins.then_inc(self.allsems[self.cur[e]], 1)
        tk = (self.cur[e], self.cnt[e], e)
        self._commit(tk, reads, writes)
        self.ninst += 1
        return tk

    def dma(self, q, out, in_, reads=(), writes=(), **kw):
        self._deps(q, reads, writes)
        d = self.dq[q]
        k = d["n"] % self.NDS
        d["n"] += 1
        if d["val"][k] > 0:
            self._wait(q, (d["idx"][k], d["val"][k], "dma"))
        ins = self.eng[q].dma_start(out=out, in_=in_, **kw)
        d["val"][k] += 16
        ins.then_inc(self.allsems[d["idx"][k]], 16)
        tk = (d["idx"][k], d["val"][k], "dma")
        self._commit(tk, reads, writes)
        self.ninst += 1
        return tk

    def barrier(self):
        tks = []
        for e in self.ENG:
            if self.cnt[e] > 0:
                tks.append((self.cur[e], self.cnt[e], "x"))
        for q, d in self.dq.items():
            for k in range(self.NDS):
                if d["val"][k] > 0:
                    tks.append((d["idx"][k], d["val"][k], "dma"))
        for e in self.ENG:
            for tk in tks:
                self._wait(e, tk)

    def finish(self):
        self.barrier()
        while len(self.stacks) > 1:
            self.stacks.pop().close()
        self.root.close()

    def mm(self, out, lhsT, rhs, start, stop, reads, writes, **kw):
        return self.op("pe", lambda e: e.matmul(out, lhsT, rhs, start=start, stop=stop, **kw), reads, writes)

    def tr(self, out, in_, ident, reads, writes):
        return self.op("pe", lambda e: e.transpose(out, in_, ident), reads, writes)

    def actf(self, out, in_, func, reads, writes, bias=None, scale=1.0, accum_out=None, e="act"):
        kw = {}
        if bias is not None:
            kw["bias"] = bias
        if accum_out is not None:
            kw["accum_out"] = accum_out
        return self.op(e, lambda en: en.activation(out, in_, func, scale=scale, **kw), reads, writes)

    def ts(self, e, out, in0, s1, s2, op0, op1=None, reads=(), writes=(), accum_out=None):
        kw = {}
        if op1 is not None:
            kw["op1"] = op1
        if accum_out is not None:
            kw["accum_out"] = accum_out
        return self.op(e, lambda en: en.tensor_scalar(out, in0, s1, s2, op0, **kw), reads, writes)

    def tt(self, e, out, in0, in1, op, reads=(), writes=()):
        return self.op(e, lambda en: en.tensor_tensor(out, in0, in1, op), reads, writes)

    def stt(self, out, in0, scalar, in1, op0, op1, reads=(), writes=(), e="dve"):
        return self.op(e, lambda en: en.scalar_tensor_tensor(out, in0, scalar, in1, op0, op1), reads, writes)

    def copy(self, e, out, in_, reads=(), writes=()):
        if e == "act":
            return self.op(e, lambda en: en.copy(out, in_), reads, writes)
        return self.op(e, lambda en: en.tensor_copy(out, in_), reads, writes)

    def memset(self, e, ap, val, writes=()):
        return self.op(e, lambda en: en.memset(ap, val), (), writes)


D = 1024
CTX = 256
LAT = 2048
NT = CTX + LAT
DEPTH = 4
ALPHA = (2 * DEPTH) ** 0.25
LN_EPS = 1e-5
RMS_EPS = 1e-6
NRP = 8


class GG:
    pass


def seg(s, c):
    return s * 8 + c


def setup_globals(k, NB, dbg=False):
    G = GG()
    skind = "ExternalOutput" if dbg else "Internal"
    G.NB = NB
    ext = lambda n, s, d=F32: k.dram(n, s, d, kind="ExternalInput")
    G.hT0 = ext("hT0", [NB, 128, 8, NT])
    G.cT = ext("cT", [128, 8, NRP])
    G.ada_w = ext("ada_w", [4, 8, 128, 6144])
    G.ada_b = ext("ada_b", [128, 4, 48])
    G.ln_g = ext("ln_g", [128, 4, 2, 8])
    G.ln_b = ext("ln_b", [128, 4, 2, 8])
    G.peer_wq = ext("peer_wq", [4, 1024, 2048])
    G.peer_keysT = ext("peer_keysT", [4, 2, 128, 128])
    G.peer_u = ext("peer_u", [4, 128 * 128, 1024])
    G.peer_v = ext("peer_v", [4, 128 * 128, 1024])
    G.ident_d = ext("ident", [128, 128])
    G.iota3_d = ext("iota3", [128, 16 * 128], BF16)
    G.iota16_d = ext("iota16", [128, 16])
    G.H = [G.hT0, k.dram("H1", [NB, 128, 8, NT], kind=skind), k.dram("H2", [NB, 128, 8, NT], kind=skind)]
    G.RT = k.dram("RT", [NB, NT, 384], kind=skind)
    G.UB = k.dram("UB", [128, 128, 1024], BF16)
    G.VB = k.dram("VB", [128, 128, 1024], BF16)
    G.UB.sub = [Res(f"UB{i}") for i in range(16)]
    G.VB.sub = [Res(f"VB{i}") for i in range(16)]
    G.out = k.dram("outT", [NB, 128, 8, LAT], kind="ExternalOutput")
    G.da_w_in = ext("da_w_in", [2, 1024, 3072])
    G.da_w_sw = ext("da_w_sw", [2, 1024, 2048])
    G.da_w_out = ext("da_w_out", [2, 1024, 1024])
    G.da_lam_q = ext("da_lam_q", [2, 128])
    G.da_lam_k = ext("da_lam_k", [2, 128])
    G.da_subln = ext("da_subln", [2, 128, 1])
    G.rope = ext("rope", [4, 128, NT])
    def wscr(n, r, c):
        t = k.dram(n, [r, c], BF16)
        t.sub = [t.res]
        return t
    G.WinB = wscr("WinB", 1024, 3072)
    G.WswB = wscr("WswB", 1024, 2048)
    G.WoutB = wscr("WoutB", 1024, 1024)
    G.hg_w_in = ext("hg_w_in", [1024, 5120])
    G.hg_w_out = ext("hg_w_out", [1024, 1024])
    G.hg_norm = ext("hg_norm", [128, 8])
    G.hg_lb = ext("hg_lb", [128, 8, 4])
    G.hgmask_d = ext("hgmask", [128, 64 + NT], BF16)
    G.WhgB = wscr("WhgB", 1024, 5120)
    G.WhgoB = wscr("WhgoB", 1024, 1024)
    G.PQ_d = k.dram("PQ_d", [NB, 4, 8, 128, NT])
    G.Vr_d = k.dram("Vr_d", [NB, NT, 1024], BF16)
    G.s5p = ext("s5p", [128, 2, 32, 3])
    G.s5bc = ext("s5bc", [128, 2, 32, 4, 16])
    G.s5_d = ext("s5_d", [128, 8])
    G.s5_w_glu = ext("s5_w_glu", [1024, 2048])
    G.tT_d = ext("tT", [128, NT])
    G.WgluB = wscr("WgluB", 1024, 2048)
    G.UT_d = k.dram("UT_d", [NB, 8, 128, NT], BF16)
    G.ZT_d = k.dram("ZT_d", [NB, 8, 128, NT], BF16)
    G.QT_d = k.dram("QT_d", [NB, 8, 128, NT], BF16)
    G.KT_d = k.dram("KT_d", [NB, 8, 128, NT], BF16)
    G.V_d = k.dram("V_d", [NB, NT, 1024], BF16)
    G.modT = k.sb("modT", [128, 4, 48, NRP])
    G.mod1 = k.sb("mod1", [128, 4, 48, NRP])
    G.lng = k.sb("lng", [128, 4, 2, 8])
    G.lnb = k.sb("lnb", [128, 4, 2, 8])
    G.ident = k.sb("ident", [128, 128])
    G.onesD = k.sb("onesD", [128, 128])
    G.iota3 = k.sb("iota3", [128, 16, 128], BF16)
    G.iota16 = k.sb("iota16", [128, 16])
    k.dma("sp", G.lng[:], G.ln_g[:, :, :, :], reads=[G.ln_g], writes=[G.lng])
    k.dma("sp", G.lnb[:], G.ln_b[:, :, :, :], reads=[G.ln_b], writes=[G.lnb])
    k.dma("sp", G.ident[:], G.ident_d[:, :], reads=[G.ident_d], writes=[G.ident])
    k.dma("sp", G.iota3[:].rearrange("p a b -> p (a b)"), G.iota3_d[:, :], reads=[G.iota3_d], writes=[G.iota3])
    k.dma("sp", G.iota16[:], G.iota16_d[:, :], reads=[G.iota16_d], writes=[G.iota16])
    k.memset("dve", G.onesD[:], 1.0 / D, writes=[G.onesD])
    G.epsln = k.sb("epsln", [128, 1])
    G.epsrms = k.sb("epsrms", [128, 1])
    k.memset("dve", G.epsln[:], LN_EPS, writes=[G.epsln])
    k.memset("dve", G.epsrms[:], RMS_EPS, writes=[G.epsrms])
    return G


def phase_adaln(k, G):
    k.push()
    cT = k.sb("cT", [128, 8, NRP])
    sT = k.sb("sT", [128, 8, NRP])
    adab = k.sb("adab", [128, 4, 48])
    k.dma("sp", cT[:], G.cT[:, :, :], reads=[G.cT], writes=[cT])
    k.dma("sp", adab[:], G.ada_b[:, :, :], reads=[G.ada_b], writes=[adab])
    k.actf(sT[:], cT[:], AF.Silu, [cT], [sT])
    ps = k.ps("adaps", [128, 48 * NRP])
    wb = [k.sb("adaw", [128, 6144]) for _ in range(2)]
    n = 0
    for i in range(4):
        first = True
        for kc in range(8):
            w = wb[n % 2]
            n += 1
            k.dma("sp", w[:], G.ada_w[i, kc, :, :], reads=[G.ada_w], writes=[w])
            for j in range(48):
                k.mm(ps[:, j * NRP:(j + 1) * NRP], w[:, j * 128:(j + 1) * 128], sT[:, kc, :],
                     first, (kc == 7 and j == 47), [w, sT], [ps], skip_group_check=True)
                first = False
        k.tt("dve", G.modT[:, i, :, :], ps[:].rearrange("p (j r) -> p j r", r=NRP),
             adab[:, i, :].unsqueeze(2).to_broadcast([128, 48, NRP]), ALU.add, [ps, adab], [G.modT])
    k.ts("dve", G.mod1[:].rearrange("p a b c -> p (a b c)"), G.modT[:].rearrange("p a b c -> p (a b c)"),
         1.0, None, ALU.add, reads=[G.modT], writes=[G.mod1])
    k.pop()


def cast_bf16(k, dst, src, rows, cols, step=1024):
    n = 0
    for r0 in range(0, rows, step):
        for c0 in range(0, cols, 1024):
            k.dma("pool", dst.h[r0:r0 + step, c0:c0 + 1024], src[r0:r0 + step, c0:c0 + 1024],
                  reads=[], writes=[dst.sub[n] if len(dst.sub) > 1 else dst])
            n += 1


def layer_norm_fm(k, G, r, T, li, which, out, psL, tmp):
    sq, mean, rstd = tmp["sq"], tmp["mean"], tmp["rstd"]
    k.actf(sq[:].rearrange("p c t -> p (c t)"), r[:].rearrange("p c t -> p (c t)"), AF.Square, [r], [sq])
    for c in range(8):
        k.mm(psL[:, 0:T], G.onesD[:], r[:, c, :], c == 0, c == 7, [G.onesD, r], [psL])
    k.copy("act", mean[:, 0:T], psL[:, 0:T], [psL], [mean])
    for c in range(8):
        k.mm(psL[:, T:2 * T], G.onesD[:], sq[:, c, :], c == 0, c == 7, [G.onesD, sq], [psL])
    k.tt("dve", rstd[:, 0:T], mean[:, 0:T], mean[:, 0:T], ALU.mult, [mean], [rstd])
    k.tt("dve", rstd[:, 0:T], psL[:, T:2 * T], rstd[:, 0:T], ALU.subtract, [psL, rstd], [rstd])
    k.actf(rstd[:, 0:T], rstd[:, 0:T], AF.Sqrt, [rstd], [rstd], bias=G.epsln[:, 0:1])
    k.op("dve", lambda e: e.reciprocal(rstd[:, 0:T], rstd[:, 0:T]), [rstd], [rstd])
    for c in range(8):
        k.tt("dve", sq[:, c, :], r[:, c, :], mean[:, 0:T], ALU.subtract, [r, mean], [sq])
        k.tt("dve", sq[:, c, :], sq[:, c, :], rstd[:, 0:T], ALU.mult, [sq, rstd], [sq])
        k.actf(out[:, c, :], sq[:, c, :], AF.Identity, [sq, G.lng, G.lnb], [out],
               scale=G.lng[:, li, which, c:c + 1], bias=G.lnb[:, li, which, c:c + 1])


def blocks_of(G, need_ctx):
    for b in range(G.NB):
        for blk in range(NT // 256):
            if blk == 0 and not need_ctx:
                continue
            yield b, blk, blk * 256, (G.NB if blk == 0 else b)


def peer_route(k, G, li, Hin, need_ctx):
    k.push()
    wq = k.sb("wq", [128, 8, 2048])
    kT = k.sb("kT", [128, 2, 128])
    k.dma("sp", wq[:], G.peer_wq[li].rearrange("(k p) n -> p k n", p=128), reads=[G.peer_wq], writes=[wq])
    k.dma("sp", kT[:], G.peer_keysT[li].rearrange("c d k -> d c k"), reads=[G.peer_keysT], writes=[kT])
    hT = [k.sb("hT", [128, 8, 256]) for _ in range(2)]
    uT2 = [k.sb("uT", [128, 8, 256]) for _ in range(2)]
    qT2 = [k.sb("qT", [128, 16, 256]) for _ in range(2)]
    psq = [k.ps("psq", [128, 512]) for _ in range(2)]
    pss = k.ps("pss", [128, 2048])
    sc2_ = [k.sb("sc", [128, 2048]) for _ in range(2)]
    nsc = 0
    scr = k.sb("scr", [128, 16, 128])
    V = k.sb("V", [128, 16, 16])
    V2 = k.sb("V2", [128, 16, 8])
    TS2 = k.sb("TS2", [128, 8, 8])
    I = k.sb("I", [128, 16, 16], U32)
    If = k.sb("If", [128, 16, 16])
    cand = k.sb("cand", [128, 8, 256])
    scr2 = k.sb("scr2", [128, 8, 256])
    TS = k.sb("TS", [128, 8, 16])
    PI = k.sb("PI", [128, 8, 16], U32)
    PA = k.sb("PA", [128, 8, 16], U32)
    PB = k.sb("PB", [128, 8, 16], U32)
    PAf = k.sb("PAf", [128, 8, 16])
    PBf = k.sb("PBf", [128, 8, 16])
    E = k.sb("E", [128, 8, 16, 16])
    ssum = k.sb("ssum", [128, 8])
    Rt = [k.sb("Rt", [128, 3, 128]) for _ in range(2)]
    Ifv = If[:].rearrange("p (h c) k -> p h c k", c=2)
    Vv = V[:].rearrange("p (h c) k -> p h c k", c=2)
    nb = 0
    for b, blk, tok0, row in blocks_of(G, need_ctx):
        h = hT[nb % 2]
        uT = uT2[nb % 2]
        qT = qT2[nb % 2]
        nb += 1
        k.dma("sp", h[:], Hin[b, :, :, tok0:tok0 + 256], reads=[Hin], writes=[h])
        for c in range(8):
            k.actf(uT[:, c, :], h[:, c, :], AF.Identity, [h, G.mod1, G.modT], [uT],
                   scale=G.mod1[:, li, seg(4, c), row:row + 1], bias=G.modT[:, li, seg(3, c), row:row + 1])
        for hc in range(16):
            ps = psq[(hc // 2) % 2]
            half = hc % 2
            for kc in range(8):
                k.mm(ps[:, half * 256:(half + 1) * 256], wq[:, kc, hc * 128:(hc + 1) * 128], uT[:, kc, :],
                     kc == 0, kc == 7, [wq, uT], [ps])
            if half == 1:
                k.copy("act", qT[:, hc - 1:hc + 1, :].rearrange("p a t -> p (a t)"), ps[:], [ps], [qT])
        for tt in range(2):
            for hc in range(16):
                k.mm(pss[:, hc * 128:(hc + 1) * 128], qT[:, hc, tt * 128:(tt + 1) * 128], kT[:, hc % 2, :],
                     True, True, [qT, kT], [pss])
            sc = sc2_[nsc % 2]
            nsc += 1
            k.copy("act", sc[:], pss[:], [pss], [sc])
            sl_ = lambda hc: sc[:, hc * 128:(hc + 1) * 128]
            for hc in range(16):
                k.op("dve", lambda e: e.max(V[:, hc, 0:8], sl_(hc)), [sc], [V])
            for hc in range(16):
                k.op("dve", lambda e: e.max_index(I[:, hc, 0:8], V[:, hc, 0:8], sl_(hc)), [sc, V], [I])
            for hc in range(16):
                k.op("dve", lambda e: e.match_replace(scr[:, hc, :], V[:, hc, 0:8], sl_(hc), -1e30), [sc, V], [scr])
            for hc in range(16):
                k.op("dve", lambda e: e.max(V2[:, hc, :], scr[:, hc, :]), [scr], [V2])
            for hc in range(16):
                k.op("dve", lambda e: e.max_index(I[:, hc, 8:16], V2[:, hc, :], scr[:, hc, :]), [scr, V2], [I])
            k.copy("dve", V[:, :, 8:16], V2[:], [V2], [V])
            k.copy("dve", If[:], I[:], [I], [If])
            k.tt("dve", cand[:].rearrange("p h (a b) -> p h a b", b=16),
                 Vv[:, :, 0, :].unsqueeze(3).to_broadcast([128, 8, 16, 16]),
                 Vv[:, :, 1, :].unsqueeze(2).to_broadcast([128, 8, 16, 16]), ALU.add, [V], [cand])
            for hh in range(8):
                k.op("dve", lambda e: e.max(TS[:, hh, 0:8], cand[:, hh, :]), [cand], [TS])
            for hh in range(8):
                k.op("dve", lambda e: e.max_index(PI[:, hh, 0:8], TS[:, hh, 0:8], cand[:, hh, :]), [cand, TS], [PI])
            for hh in range(8):
                k.op("dve", lambda e: e.match_replace(scr2[:, hh, :], TS[:, hh, 0:8], cand[:, hh, :], -1e30), [cand, TS], [scr2])
            for hh in range(8):
                k.op("dve", lambda e: e.max(TS2[:, hh, :], scr2[:, hh, :]), [scr2], [TS2])
            for hh in range(8):
                k.op("dve", lambda e: e.max_index(PI[:, hh, 8:16], TS2[:, hh, :], scr2[:, hh, :]), [scr2, TS2], [PI])
            k.copy("dve", TS[:, :, 8:16], TS2[:], [TS2], [TS])
            k.op("dve", lambda e: e.tensor_single_scalar(PA[:], PI[:], 4, ALU.logical_shift_right), [PI], [PA])
            k.op("dve", lambda e: e.tensor_single_scalar(PB[:], PI[:], 15, ALU.bitwise_and), [PI], [PB])
            k.copy("dve", PAf[:], PA[:], [PA], [PAf])
            k.copy("dve", PBf[:], PB[:], [PB], [PBf])
            R = Rt[tt]
            for which, Pf in ((0, PAf), (1, PBf)):
                k.tt("dve", E[:], G.iota16[:].unsqueeze(1).unsqueeze(1).to_broadcast([128, 8, 16, 16]),
                     Pf[:].unsqueeze(3).to_broadcast([128, 8, 16, 16]), ALU.is_equal, [G.iota16, Pf], [E])
                k.tt("dve", E[:], E[:], Ifv[:, :, which, :].unsqueeze(2).to_broadcast([128, 8, 16, 16]),
                     ALU.mult, [E, If], [E])
                k.op("dve", lambda e: e.tensor_reduce(R[:, which, :], E[:].rearrange("p h k a -> p (h k) a"),
                                                        AX.X, ALU.add), [E], [R])
            g3 = R[:, 2, :].rearrange("p (h k) -> p h k", k=16)
            k.tt("dve", g3, TS[:], TS[:, :, 0:1].to_broadcast([128, 8, 16]), ALU.subtract, [TS], [R])
            k.actf(g3, g3, AF.Exp, [R], [R])
            k.op("dve", lambda e: e.tensor_reduce(ssum[:], g3, AX.X, ALU.add), [R], [ssum])
            k.op("dve", lambda e: e.reciprocal(ssum[:], ssum[:]), [ssum], [ssum])
            k.tt("dve", g3, g3, ssum[:].unsqueeze(2).to_broadcast([128, 8, 16]), ALU.mult, [R, ssum], [R])
            t0 = tok0 + tt * 128
            k.dma("act", G.RT[b, t0:t0 + 128, :], R[:].rearrange("p a k -> p (a k)"), reads=[R], writes=[G.RT])
    k.pop()


def peer_experts(k, G, li, Hin, Hout, need_ctx, out_final=False):
    k.push()
    NBUF = 4
    hT2 = [k.sb("hT", [128, 8, 256]) for _ in range(2)]
    uTb2 = [k.sb("uTb", [128, 8, 256], BF16) for _ in range(2)]
    rr = k.sb("rr", [128, 8, 256])
    oo = rr
    nblk_ = 0
    tmp = dict(sq=k.sb("sq", [128, 8, 256]), mean=k.sb("mean", [128, 256]), rstd=k.sb("rstd", [128, 256]))
    Rt = [k.sb("Rt", [128, 384]) for _ in range(2)]
    RTt = k.sb("RTt", [128, 3, 256])
    SB_ = 16
    Q = [k.sb("Q", [128, SB_, 128], BF16) for _ in range(2)]
    Pg = [k.sb("Pg", [128, SB_, 128], BF16) for _ in range(2)]
    GT = k.sb("GT", [128, 256, 128], BF16)
    ut = [k.sb("ut", [128, 2, 8, 128], BF16) for _ in range(NBUF)]
    vt = [k.sb("vt", [128, 2, 1024], BF16) for _ in range(NBUF)]
    ga = [k.sb("ga", [128, 256], BF16) for _ in range(3)]
    gam = [k.sb("gam", [128, 256], BF16) for _ in range(3)]
    psO = [k.ps("psO", [128, 512]) for _ in range(4)]
    psG = [k.ps("psG", [128, 512]) for _ in range(2)]
    psA2 = [psG[0], psG[1], k.ps("psA", [128, 512])]
    psA_r = psA2
    psL = psG[0]
    NA = 3
    ng = 0
    nev = 0
    for b, blk, tok0, row in blocks_of(G, need_ctx):
        hT = hT2[0]
        uTb = uTb2[0]
        nblk_ += 1
        k.dma("sp", hT[:], Hin[b, :, :, tok0:tok0 + 256], reads=[Hin], writes=[hT])
        for c in range(8):
            k.actf(uTb[:, c, :], hT[:, c, :], AF.Identity, [hT, G.mod1, G.modT], [uTb],
                   scale=G.mod1[:, li, seg(4, c), row:row + 1], bias=G.modT[:, li, seg(3, c), row:row + 1])
        for tt in range(2):
            t0 = tok0 + tt * 128
            k.dma("sp", Rt[tt][:], G.RT[b, t0:t0 + 128, :], reads=[G.RT], writes=[Rt[tt]])
            pg = psG[ng % 2]
            ng += 1
            for a in range(3):
                k.tr(pg[:, a * 128:(a + 1) * 128], Rt[tt][:, a * 128:(a + 1) * 128], G.ident[:], [Rt[tt], G.ident], [pg])
            k.copy("act", RTt[:, :, tt * 128:(tt + 1) * 128], pg[:, 0:384].rearrange("p (a t) -> p a t", a=3), [pg], [RTt])
        def load_tab(ii):
            k.dma("sp", ut[ii % NBUF][:].rearrange("p a c j -> p a (c j)"), G.UB.h[:, 2 * ii:2 * ii + 2, :],
                  reads=[G.UB.sub[ii // 4]], writes=[ut[ii % NBUF]])
            k.dma("sp", vt[ii % NBUF][:], G.VB.h[:, 2 * ii:2 * ii + 2, :], reads=[G.VB.sub[ii // 4]], writes=[vt[ii % NBUF]])
        for ii in range(NBUF - 1):
            load_tab(ii)
        for sub in range(256 // SB_):
            s = sub % 2
            tsl = slice(sub * SB_, (sub + 1) * SB_)
            k.tt("dve", Q[s][:], G.iota3[:], RTt[:, 1, tsl].unsqueeze(2).to_broadcast([128, SB_, 128]),
                 ALU.is_equal, [G.iota3, RTt], [Q[s]])
            for t in range(SB_):
                tok = sub * SB_ + t
                k.ts("dve", Pg[s][:, t, :], G.iota3[:, 0, :], RTt[:, 0, tok:tok + 1], RTt[:, 2, tok:tok + 1],
                     ALU.is_equal, ALU.mult, reads=[G.iota3, RTt], writes=[Pg[s]])
            for t in range(SB_):
                tok = sub * SB_ + t
                pg = psG[ng % 2]
                k.mm(pg[:, (t % 4) * 128:(t % 4 + 1) * 128], Q[s][:, t, :], Pg[s][:, t, :], True, True,
                     [Q[s], Pg[s]], [pg])
                if t % 4 == 3:
                    ng += 1
                    k.copy("act" if nev % 2 == 0 else "dve", GT[:, tok - 3:tok + 1, :].rearrange("p t i -> p (t i)"),
                           pg[:], [pg], [GT])
                    nev += 1
        def amm(i):
            u_ = ut[(i // 2) % NBUF]
            pa = psA2[i % NA][:, 0:256]
            for c in range(8):
                k.mm(pa, u_[:, i % 2, c, :], uTb[:, c, :], c == 0, c == 7, [u_, uTb], [psA_r[i % NA]])
        amm(0)
        amm(1)
        for i in range(128):
            if i % 2 == 0 and i // 2 + NBUF - 1 < 64:
                load_tab(i // 2 + NBUF - 1)
            if i + 2 < 128:
                amm(i + 2)
            v_ = vt[(i // 2) % NBUF]
            pa = psA2[i % NA][:, 0:256]
            par = psA_r[i % NA]
            k.actf(ga[i % NA][:], pa, AF.Gelu_apprx_tanh, [par], [ga[i % NA]])
            k.tt("dve", gam[i % NA][:], ga[i % NA][:], GT[:, :, i], ALU.mult, [ga[i % NA], GT], [gam[i % NA]])
            for dc in range(8):
                k.mm(psO[dc // 2][:, (dc % 2) * 256:(dc % 2 + 1) * 256], v_[:, i % 2, dc * 128:(dc + 1) * 128], gam[i % NA][:],
                     (i == 0 and dc % 2 == 0), i == 127, [v_, gam[i % NA]], [psO[dc // 2]], skip_group_check=True)
        k.actf(hT[:].rearrange("p c t -> p (c t)"), hT[:].rearrange("p c t -> p (c t)"), AF.Copy, [hT], [hT], scale=ALPHA)
        for c in range(8):
            k.stt(rr[:, c, :], psO[c // 2][:, (c % 2) * 256:(c % 2 + 1) * 256], G.modT[:, li, seg(5, c), row:row + 1],
                  hT[:, c, :], ALU.mult, ALU.add, [psO[c // 2], G.modT, hT], [rr])
        layer_norm_fm(k, G, rr, 256, li, 1, oo, psL, tmp)
        if out_final:
            k.dma("act", G.out[b, :, :, tok0 - CTX:tok0 - CTX + 256], oo[:], reads=[oo], writes=[G.out])
        else:
            k.dma("act", Hout[b, :, :, tok0:tok0 + 256], oo[:], reads=[oo], writes=[Hout])
    k.pop()


def peer_cast(k, G, li):
    for dst, src in ((G.UB, G.peer_u), (G.VB, G.peer_v)):
        sv = src[li].rearrange("(i p) n -> p i n", p=128)
        for n in range(16):
            k.dma("pool", dst.h[:, 8 * n:8 * n + 8, :], sv[:, 8 * n:8 * n + 8, :], reads=[], writes=[dst.sub[n]])


import ml_dtypes


def fm(a):
    T = a.shape[-2]
    x = a.reshape(a.shape[:-2] + (T, 8, 128))
    nd = x.ndim
    perm = tuple(range(nd - 3)) + (nd - 1, nd - 2, nd - 3)
    return np.ascontiguousarray(np.transpose(x, perm))


def unfm(a):
    nd = a.ndim
    perm = tuple(range(nd - 3)) + (nd - 1, nd - 2, nd - 3)
    x = np.transpose(a, perm)
    return np.ascontiguousarray(x).reshape(x.shape[:-2] + (D,))


def host_consts():
    ident = np.eye(128, dtype=np.float32)
    iota3 = np.tile(np.arange(128, dtype=np.float32)[None, None, :], (128, 16, 1)).reshape(128, 16 * 128)
    iota16 = np.tile(np.arange(16, dtype=np.float32)[None, :], (128, 1))
    return dict(ident=ident, iota3=iota3.astype(ml_dtypes.bfloat16), iota16=iota16)


def host_weights(inp):
    w = {}
    w["ada_w"] = np.ascontiguousarray(inp["ada_w"]).reshape(4, 8, 128, 6144)
    w["ada_b"] = np.ascontiguousarray(inp["ada_b"].reshape(4, 48, 128).transpose(2, 0, 1))
    w["ln_g"] = np.ascontiguousarray(inp["ln_g"].reshape(4, 2, 8, 128).transpose(3, 0, 1, 2))
    w["ln_b"] = np.ascontiguousarray(inp["ln_b"].reshape(4, 2, 8, 128).transpose(3, 0, 1, 2))
    w["peer_wq"] = np.ascontiguousarray(inp["peer_wq"])
    w["peer_keysT"] = np.ascontiguousarray(inp["peer_keys"].transpose(0, 1, 3, 2))
    pu = inp["peer_u"].reshape(4, 128, 128, 8, 128).transpose(0, 1, 4, 3, 2)
    w["peer_u"] = np.ascontiguousarray(pu).reshape(4, 128 * 128, 1024)
    w["peer_v"] = np.ascontiguousarray(inp["peer_v"])
    w.update(host_consts())
    return w


def host_core_inputs(inp, b0, NB):
    cT = np.zeros((NRP, D), np.float32)
    cT[:NB] = inp["c"][b0:b0 + NB]
    cT[NB] = inp["c_ctx"]
    cT = np.ascontiguousarray(cT.reshape(NRP, 8, 128).transpose(2, 1, 0))
    tok = np.concatenate([inp["ctx"][b0:b0 + NB], inp["x"][b0:b0 + NB]], axis=1)
    return dict(cT=cT, hT0=fm(tok))


def rope_tables():
    rows = LAT // 64
    row = np.repeat(np.arange(rows, dtype=np.float32), 64)
    col = np.tile(np.arange(64, dtype=np.float32), rows)
    inv = (np.float32(10000.0) ** (-np.arange(16, dtype=np.float32) / np.float32(16))).astype(np.float32)
    ang = np.concatenate([row[:, None] * inv, col[:, None] * inv], axis=-1).astype(np.float32)
    cos, sin = np.cos(ang).astype(np.float32), np.sin(ang).astype(np.float32)
    p = np.arange(128)
    r = p % 64
    i = r // 2
    sign = np.where(r % 2 == 0, -1.0, 1.0).astype(np.float32)
    C = np.ones((128, NT), np.float32)
    S = np.zeros((128, NT), np.float32)
    C[:, CTX:] = cos[:, i].T
    S[:, CTX:] = sin[:, i].T * sign[:, None]
    return np.stack([C * np.float32(0.125), S * np.float32(0.125), C, S]).astype(np.float32)


def host_weights_attn(inp):
    w = {}
    w["da_w_in"] = np.ascontiguousarray(inp["da_w_in"])
    qk = inp["da_w_in"][:, :, :2048]
    w["da_w_sw"] = np.ascontiguousarray(qk.reshape(2, 1024, 1024, 2)[:, :, :, ::-1].reshape(2, 1024, 2048))
    w["da_w_out"] = np.ascontiguousarray(inp["da_w_out"])
    w["da_lam_q"] = np.ascontiguousarray(inp["da_lam_q"].reshape(2, 128))
    w["da_lam_k"] = np.ascontiguousarray(inp["da_lam_k"].reshape(2, 128))
    w["da_subln"] = np.ascontiguousarray(inp["da_subln"].reshape(2, 128, 1))
    w["rope"] = rope_tables()
    return w


def qblocks(need_ctx):
    out = []
    if need_ctx:
        out.append((0, 256, 2))
    for i in range(4):
        out.append((CTX + i * 512, 512, 18))
    return out


def attention(k, G, li, Hin, Hout, need_ctx):
    import math
    slot = li // 3
    lam_init = 0.8 - 0.6 * math.exp(-0.3 * li)
    NB = G.NB
    cast_bf16(k, G.WinB, G.da_w_in[slot], 1024, 3072)
    cast_bf16(k, G.WswB, G.da_w_sw[slot], 1024, 2048)
    cast_bf16(k, G.WoutB, G.da_w_out[slot], 1024, 1024)
    k.push()
    lq = k.sb("lq", [128, 128])
    lk = k.sb("lk", [128, 128])
    l2 = k.sb("l2", [128, 2])
    neglam = k.sb("neglam", [128, 1])
    subg = k.sb("subg", [128, 1])
    ones_b = k.sb("ones_b", [128, 128], BF16)
    ones128 = k.sb("ones128", [128, 128])
    k.memset("dve", ones_b[:], 1.0, writes=[ones_b])
    k.memset("dve", ones128[:], 1.0 / 128, writes=[ones128])
    k.dma("sp", lq[:], G.da_lam_q.h[slot:slot + 1, :].to_broadcast([128, 128]), reads=[G.da_lam_q], writes=[lq])
    k.dma("sp", lk[:], G.da_lam_k.h[slot:slot + 1, :].to_broadcast([128, 128]), reads=[G.da_lam_k], writes=[lk])
    k.dma("sp", subg[:], G.da_subln[slot], reads=[G.da_subln], writes=[subg])
    k.tt("dve", lq[:], lq[:], lk[:], ALU.mult, [lq, lk], [lq])
    k.op("dve", lambda e: e.tensor_reduce(l2[:], lq[:].rearrange("p (c d) -> p c d", c=2), AX.X, ALU.add), [lq], [l2])
    k.actf(l2[:], l2[:], AF.Exp, [l2], [l2])
    k.tt("dve", neglam[:], l2[:, 1:2], l2[:, 0:1], ALU.subtract, [l2], [neglam])
    k.ts("dve", neglam[:], neglam[:], -lam_init, None, ALU.add, reads=[neglam], writes=[neglam])
    k.ts("dve", subg[:], subg[:], 1.0 - lam_init, None, ALU.mult, reads=[subg], writes=[subg])
    OnT = k.sb("OnT", [128, 8, NT], BF16)
    for b in range(NB):
        k.push()
        uTb = k.sb("uTb", [128, 8, NT], BF16)
        hblk = [k.sb("hblk", [128, 8, 256]) for _ in range(2)]
        for blk in range(NT // 256):
            hb = hblk[blk % 2]
            row = NB if blk == 0 else b
            k.dma("sp", hb[:], Hin[b, :, :, blk * 256:(blk + 1) * 256], reads=[Hin], writes=[hb])
            for c in range(8):
                k.actf(uTb[:, c, blk * 256:(blk + 1) * 256], hb[:, c, :], AF.Identity, [hb, G.modT, G.mod1], [uTb],
                       scale=G.mod1[:, li, seg(1, c), row:row + 1], bias=G.modT[:, li, seg(0, c), row:row + 1])
        Ct = k.sb("Ct", [128, NT])
        St = k.sb("St", [128, NT])
        wch = [k.sb("wch", [128, 8, 128], BF16) for _ in range(2)]
        wsc = [k.sb("wsc", [128, 8, 128], BF16) for _ in range(2)]
        orow = [k.sb("orow", [128, NT], BF16) for _ in range(2)]
        t1 = [k.sb("t1", [128, 512]) for _ in range(2)]
        t2 = [k.sb("t2", [128, 512]) for _ in range(2)]
        psa = [k.ps("psa", [128, 512]) for _ in range(2)]
        psb = [k.ps("psb", [128, 512]) for _ in range(2)]
        psv = [k.ps("psv", [128, 512]) for _ in range(2)]
        WinV = G.WinB.h.rearrange("(k p) n -> p k n", p=128)
        WswV = G.WswB.h.rearrange("(k p) n -> p k n", p=128)
        n = 0
        for grp in range(2):
            k.dma("sp", Ct[:], G.rope[2 * grp], reads=[G.rope], writes=[Ct])
            k.dma("sp", St[:], G.rope[2 * grp + 1], reads=[G.rope], writes=[St])
            for ch in range(8):
                col0 = grp * 1024 + ch * 128
                w_, ws_ = wch[ch % 2], wsc[ch % 2]
                k.dma("sp", w_[:], WinV[:, :, col0:col0 + 128], reads=[G.WinB], writes=[w_])
                k.dma("sp", ws_[:], WswV[:, :, col0:col0 + 128], reads=[G.WswB], writes=[ws_])
                orw = orow[ch % 2]
                for (t0, N) in [(0, 256)] + [(CTX + i * 512, 512) for i in range(4)]:
                    pa, pb = psa[n % 2], psb[n % 2]
                    a1, a2 = t1[n % 2], t2[n % 2]
                    n += 1
                    for kc in range(8):
                        k.mm(pa[:, 0:N], w_[:, kc, :], uTb[:, kc, t0:t0 + N], kc == 0, kc == 7, [w_, uTb], [pa])
                    for kc in range(8):
                        k.mm(pb[:, 0:N], ws_[:, kc, :], uTb[:, kc, t0:t0 + N], kc == 0, kc == 7, [ws_, uTb], [pb])
                    k.tt("dve", a1[:, 0:N], pa[:, 0:N], Ct[:, t0:t0 + N], ALU.mult, [pa, Ct], [a1])
                    k.tt("dve", a2[:, 0:N], pb[:, 0:N], St[:, t0:t0 + N], ALU.mult, [pb, St], [a2])
                    k.tt("pool", orw[:, t0:t0 + N], a1[:, 0:N], a2[:, 0:N], ALU.add, [a1, a2], [orw])
                dst = G.QT_d if grp == 0 else G.KT_d
                k.dma("act", dst[b, ch, :, :], orw[:], reads=[orw], writes=[dst])
        wv = k.sb("wv", [128, 8, 1024], BF16)
        k.dma("sp", wv[:], WinV[:, :, 2048:3072], reads=[G.WinB], writes=[wv])
        vrow = [k.sb("vrow", [128, 1024], BF16) for _ in range(2)]
        for tl in range(NT // 128):
            vr = vrow[tl % 2]
            for cb in range(2):
                pv = psv[cb]
                for kc in range(8):
                    k.mm(pv[:], uTb[:, kc, tl * 128:(tl + 1) * 128], wv[:, kc, cb * 512:(cb + 1) * 512], kc == 0, kc == 7,
                         [uTb, wv], [pv])
                k.copy("act" if cb == 0 else "dve", vr[:, cb * 512:(cb + 1) * 512], pv[:], [pv], [vr])
            k.dma("act", G.V_d[b, tl * 128:(tl + 1) * 128, :], vr[:], reads=[vr], writes=[G.V_d])
        k.pop()
        k.push()
        Qh = [k.sb("Qh", [128, NT], BF16) for _ in range(2)]
        Kh = [k.sb("Kh", [128, NT], BF16) for _ in range(2)]
        Vh = [k.sb("Vh", [128, 18, 128], BF16) for _ in range(2)]
        PT = [k.sb("PT", [128, 512], BF16) for _ in range(3)]
        psS = [k.ps("psS", [128, 512]) for _ in range(2)]
        psOc = [k.ps("psOc", [128, 512]) for _ in range(2)]
        psZ = [k.ps("psZ", [128, 512]) for _ in range(2)]
        psM = k.ps("psM", [128, 512])
        rz = [k.sb("rz", [128, 512]) for _ in range(2)]
        tO = [k.sb("tO", [128, 512]) for _ in range(2)]
        osb = k.sb("osb", [128, 512])
        sqb = k.sb("sqb", [128, 512])
        rinv = k.sb("rinv", [128, 512])
        ns = 0
        for h in range(8):
            q_, k_, v_ = Qh[h % 2], Kh[h % 2], Vh[h % 2]
            k.dma("sp", q_[:], G.QT_d[b, h, :, :], reads=[G.QT_d], writes=[q_])
            k.dma("sp", k_[:], G.KT_d[b, h, :, :], reads=[G.KT_d], writes=[k_])
            k.dma("sp", v_[:], G.V_d[b, :, h * 128:(h + 1) * 128].rearrange("(t p) e -> p t e", p=128),
                  reads=[G.V_d], writes=[v_])
            for (t0, N, nkt) in qblocks(need_ctx):
                for c in range(2):
                    def smm(kt_, n_):
                        k.mm(psS[n_ % 2][:, 0:N], k_[c * 64:(c + 1) * 64, kt_ * 128:(kt_ + 1) * 128], q_[c * 64:(c + 1) * 64, t0:t0 + N],
                             True, True, [k_, q_], [psS[n_ % 2]])
                    smm(0, ns)
                    for kt in range(nkt):
                        pS = psS[ns % 2]
                        pt = PT[ns % 3]
                        ns += 1
                        if kt + 1 < nkt:
                            smm(kt + 1, ns)
                        k.actf(pt[:, 0:N], pS[:, 0:N], AF.Exp, [pS], [pt])
                        k.mm(psOc[c][:, 0:N], v_[:, kt, :], pt[:, 0:N], kt == 0, kt == nkt - 1, [v_, pt], [psOc[c]])
                        k.mm(psZ[c][:, 0:N], ones_b[:], pt[:, 0:N], kt == 0, kt == nkt - 1, [ones_b, pt], [psZ[c]])
                    k.op("dve", lambda e: e.reciprocal(rz[c][:, 0:N], psZ[c][:, 0:N]), [psZ[c]], [rz[c]])
                    k.tt("dve", tO[c][:, 0:N], psOc[c][:, 0:N], rz[c][:, 0:N], ALU.mult, [psOc[c], rz[c]], [tO[c]])
                k.stt(osb[:, 0:N], tO[1][:, 0:N], neglam[:, 0:1], tO[0][:, 0:N], ALU.mult, ALU.add, [tO[0], tO[1], neglam], [osb])
                k.actf(sqb[:, 0:N], osb[:, 0:N], AF.Square, [osb], [sqb])
                k.mm(psM[:, 0:N], ones128[:], sqb[:, 0:N], True, True, [ones128, sqb], [psM])
                k.actf(rinv[:, 0:N], psM[:, 0:N], AF.Sqrt, [psM], [rinv], bias=G.epsrms[:, 0:1])
                k.op("dve", lambda e: e.reciprocal(rinv[:, 0:N], rinv[:, 0:N]), [rinv], [rinv])
                k.tt("dve", osb[:, 0:N], osb[:, 0:N], rinv[:, 0:N], ALU.mult, [osb, rinv], [osb])
                k.ts("dve", OnT[:, h, t0:t0 + N], osb[:, 0:N], subg[:, 0:1], None, ALU.mult, reads=[osb, subg], writes=[OnT])
        k.pop()
        k.push()
        wo = k.sb("wo", [128, 8, 1024], BF16)
        k.dma("sp", wo[:], G.WoutB.h.rearrange("(k p) n -> p k n", p=128), reads=[G.WoutB], writes=[wo])
        hT = k.sb("hT", [128, 8, 256])
        rr = k.sb("rr", [128, 8, 256])
        tmp = dict(sq=k.sb("sq", [128, 8, 256]), mean=k.sb("mean", [128, 256]), rstd=k.sb("rstd", [128, 256]))
        psO = [k.ps("psO", [128, 512]) for _ in range(4)]
        psL = k.ps("psL", [128, 512])
        for blk in range(NT // 256):
            if blk == 0 and not need_ctx:
                continue
            row = NB if blk == 0 else b
            t0 = blk * 256
            k.dma("sp", hT[:], Hin[b, :, :, t0:t0 + 256], reads=[Hin], writes=[hT])
            for dc in range(8):
                po = psO[dc // 2][:, (dc % 2) * 256:(dc % 2 + 1) * 256]
                for h in range(8):
                    k.mm(po, wo[:, h, dc * 128:(dc + 1) * 128], OnT[:, h, t0:t0 + 256], h == 0, h == 7, [wo, OnT], [psO[dc // 2]])
            k.actf(hT[:].rearrange("p c t -> p (c t)"), hT[:].rearrange("p c t -> p (c t)"), AF.Copy, [hT], [hT], scale=ALPHA)
            for c in range(8):
                k.stt(rr[:, c, :], psO[c // 2][:, (c % 2) * 256:(c % 2 + 1) * 256], G.modT[:, li, seg(2, c), row:row + 1],
                      hT[:, c, :], ALU.mult, ALU.add, [psO[c // 2], G.modT, hT], [rr])
            layer_norm_fm(k, G, rr, 256, li, 0, rr, psL, tmp)
            k.dma("act", Hout[b, :, :, t0:t0 + 256], rr[:], reads=[rr], writes=[Hout])
        k.pop()
    k.pop()


def host_weights_s5(inp):
    w = {}

    def t_gp(a):
        return a.reshape(2, 32, 2, 64).transpose(2, 3, 0, 1).reshape(128, 2, 32)
    lr = t_gp(inp["s5_lam_re"][0])
    li_ = t_gp(inp["s5_lam_im"][0])
    ld = inp["s5_log_dt"][0].reshape(2, 32, 2).transpose(2, 0, 1)
    ld = np.broadcast_to(ld[:, None], (2, 64, 2, 32)).reshape(128, 2, 32)
    w["s5p"] = np.ascontiguousarray(np.stack([lr, li_, ld], axis=-1)).astype(np.float32)

    def t_b(a):
        return a.reshape(2, 32, 2, 64, 16).transpose(2, 3, 0, 1, 4).reshape(128, 2, 32, 16)

    def t_c(a):
        return a.reshape(2, 32, 2, 16, 64).transpose(2, 4, 0, 1, 3).reshape(128, 2, 32, 16)
    w["s5bc"] = np.ascontiguousarray(np.stack([t_b(inp["s5_b_re"][0]), t_b(inp["s5_b_im"][0]),
                                               t_c(inp["s5_c_re"][0]), t_c(inp["s5_c_im"][0])], axis=3)).astype(np.float32)
    w["s5_d"] = np.ascontiguousarray(inp["s5_d"][0].reshape(8, 128).T)
    w["s5_w_glu"] = np.ascontiguousarray(inp["s5_w_glu"][0])
    w["tT"] = np.tile(np.arange(NT, dtype=np.float32)[None, :], (128, 1))
    return w


TWO_PI = 6.283185307179586
CW1 = 6.28125
CW2 = TWO_PI - CW1
MAGIC = 12582912.0


def range_reduce(k, out, ang, nt, reads, e="dve"):
    k.ts(e, nt, ang, 1.0 / TWO_PI, MAGIC, ALU.mult, ALU.add, reads=reads[0], writes=reads[1])
    k.ts(e, nt, nt, MAGIC, None, ALU.subtract, reads=reads[1], writes=reads[1])
    k.stt(out, nt, -CW1, ang, ALU.mult, ALU.add, reads[0] + reads[1], reads[2])
    k.stt(out, nt, -CW2, out, ALU.mult, ALU.add, reads[1] + reads[2], reads[2])


def rev_tprime(lo, n):
    if lo < CTX:
        return (CTX - lo - n, CTX - lo)
    return (2 * CTX + LAT - lo - n, 2 * CTX + LAT - lo)


def s5_layer(k, G, li, Hin, Hout, need_ctx):
    NB = G.NB
    cast_bf16(k, G.WgluB, G.s5_w_glu.h, 1024, 2048)
    k.push()
    hb = [k.sb("hb", [128, 8, 256]) for _ in range(2)]
    ub = [k.sb("ub", [128, 8, 256], BF16) for _ in range(2)]
    n = 0
    for b in range(NB):
        for blk in range(NT // 256):
            row = NB if blk == 0 else b
            h_, u_ = hb[n % 2], ub[n % 2]
            n += 1
            k.dma("sp", h_[:], Hin[b, :, :, blk * 256:(blk + 1) * 256], reads=[Hin], writes=[h_])
            for c in range(8):
                k.actf(u_[:, c, :], h_[:, c, :], AF.Identity, [h_, G.modT, G.mod1], [u_],
                       scale=G.mod1[:, li, seg(1, c), row:row + 1], bias=G.modT[:, li, seg(0, c), row:row + 1])
            k.dma("act", G.UT_d[b, :, :, blk * 256:(blk + 1) * 256].rearrange("c p t -> p c t"), u_[:], reads=[u_], writes=[G.UT_d])
    k.pop()
    k.push()
    P3 = k.sb("P3", [128, 2, 32, 3])
    BC = k.sb("BC", [128, 2, 32, 4, 16])
    k.dma("sp", P3[:], G.s5p[:, :, :, :], reads=[G.s5p], writes=[P3])
    k.dma("sp", BC[:], G.s5bc[:, :, :, :, :], reads=[G.s5bc], writes=[BC])
    sd = k.sb("sd", [128, 8])
    k.dma("sp", sd[:], G.s5_d[:, :], reads=[G.s5_d], writes=[sd])
    tT = k.sb("tT", [128, NT])
    k.dma("sp", tT[:], G.tT_d[:, :], reads=[G.tT_d], writes=[tT])
    halfpi = k.sb("halfpi", [128, 1])
    k.memset("dve", halfpi[:], TWO_PI / 4, writes=[halfpi])
    sh = [128, 2, 32]
    nm = ["dt", "mag", "th", "nt", "thp", "sn", "cs", "are", "aim", "nr", "den", "kre", "kim", "t1", "t2"]
    V = {x: k.sb("p_" + x, sh) for x in nm}
    lr, lim, ld = P3[:, :, :, 0], P3[:, :, :, 1], P3[:, :, :, 2]
    k.actf(V["dt"][:], ld, AF.Exp, [P3], [V["dt"]])
    k.tt("dve", V["t1"][:], lr, V["dt"][:], ALU.mult, [P3, V["dt"]], [V["t1"]])
    k.actf(V["mag"][:], V["t1"][:], AF.Exp, [V["t1"]], [V["mag"]])
    k.tt("dve", V["th"][:], lim, V["dt"][:], ALU.mult, [P3, V["dt"]], [V["th"]])
    range_reduce(k, V["thp"][:], V["th"][:], V["nt"][:], ([V["th"]], [V["nt"]], [V["thp"]]))
    k.actf(V["sn"][:], V["thp"][:], AF.Sin, [V["thp"]], [V["sn"]])
    k.actf(V["t2"][:], V["thp"][:], AF.Abs, [V["thp"]], [V["t2"]])
    k.actf(V["cs"][:], V["t2"][:], AF.Sin, [V["t2"], halfpi], [V["cs"]], scale=-1.0, bias=halfpi[:, 0:1])
    k.tt("dve", V["are"][:], V["mag"][:], V["cs"][:], ALU.mult, [V["mag"], V["cs"]], [V["are"]])
    k.tt("dve", V["aim"][:], V["mag"][:], V["sn"][:], ALU.mult, [V["mag"], V["sn"]], [V["aim"]])
    k.ts("dve", V["nr"][:], V["are"][:], -1.0, None, ALU.add, reads=[V["are"]], writes=[V["nr"]])
    k.tt("dve", V["den"][:], lr, lr, ALU.mult, [P3], [V["den"]])
    k.tt("dve", V["t1"][:], lim, lim, ALU.mult, [P3], [V["t1"]])
    k.tt("dve", V["den"][:], V["den"][:], V["t1"][:], ALU.add, [V["den"], V["t1"]], [V["den"]])
    k.op("dve", lambda e: e.reciprocal(V["den"][:], V["den"][:]), [V["den"]], [V["den"]])
    k.tt("dve", V["t1"][:], V["nr"][:], lr, ALU.mult, [V["nr"], P3], [V["t1"]])
    k.tt("dve", V["t2"][:], V["aim"][:], lim, ALU.mult, [V["aim"], P3], [V["t2"]])
    k.tt("dve", V["kre"][:], V["t1"][:], V["t2"][:], ALU.add, [V["t1"], V["t2"]], [V["kre"]])
    k.tt("dve", V["kre"][:], V["kre"][:], V["den"][:], ALU.mult, [V["kre"], V["den"]], [V["kre"]])
    k.tt("dve", V["t1"][:], V["aim"][:], lr, ALU.mult, [V["aim"], P3], [V["t1"]])
    k.tt("dve", V["t2"][:], V["nr"][:], lim, ALU.mult, [V["nr"], P3], [V["t2"]])
    k.tt("dve", V["kim"][:], V["t1"][:], V["t2"][:], ALU.subtract, [V["t1"], V["t2"]], [V["kim"]])
    k.tt("dve", V["kim"][:], V["kim"][:], V["den"][:], ALU.mult, [V["kim"], V["den"]], [V["kim"]])
    sh4 = [128, 2, 32, 16]
    Bre = k.sb("Bre", sh4)
    Bim = k.sb("Bim", sh4)
    t4a = k.sb("t4a", sh4)
    nCim = k.sb("nCim", sh4)
    kreb = V["kre"][:].unsqueeze(3).to_broadcast(sh4)
    kimb = V["kim"][:].unsqueeze(3).to_broadcast(sh4)
    bre, bim, cre, cim = BC[:, :, :, 0, :], BC[:, :, :, 1, :], BC[:, :, :, 2, :], BC[:, :, :, 3, :]
    k.tt("dve", Bre[:], bre, kreb, ALU.mult, [BC, V["kre"]], [Bre])
    k.tt("dve", t4a[:], bim, kimb, ALU.mult, [BC, V["kim"]], [t4a])
    k.tt("dve", Bre[:], Bre[:], t4a[:], ALU.subtract, [Bre, t4a], [Bre])
    k.tt("dve", Bim[:], bim, kreb, ALU.mult, [BC, V["kre"]], [Bim])
    k.tt("dve", t4a[:], bre, kimb, ALU.mult, [BC, V["kim"]], [t4a])
    k.tt("dve", Bim[:], Bim[:], t4a[:], ALU.add, [Bim, t4a], [Bim])
    k.ts("dve", nCim[:], cim, -1.0, None, ALU.mult, reads=[BC], writes=[nCim])
    BD = [k.sb("BD", [128, 128]) for _ in range(2)]
    WB = [k.sb("WB", [128, 128], BF16) for _ in range(2)]
    WC = [k.sb("WC", [128, 128], BF16) for _ in range(2)]
    cosT = k.sb("cosT", [128, NT])
    sinT = k.sb("sinT", [128, NT])
    ntT = k.sb("ntT", [128, NT])
    rT = k.sb("rT", [128, 512])
    uch = [k.sb("uch", [128, NT], BF16) for _ in range(NB)]
    yacc = [k.sb("yacc", [128, NT]) for _ in range(NB)]
    zout = k.sb("zout", [128, NT], BF16)
    xb = [[k.sb("xre", [128, 512], BF16), k.sb("xim", [128, 512], BF16)] for _ in range(NB)]
    Wab = [{x: k.sb("w5" + x, [128, 512]) for x in ("a", "b")} for _ in range(2)]
    Wst = [{x: k.sb("w5" + x, [128, 512]) for x in ("cr", "ci", "zr", "zi")} for _ in range(NB)]
    stt_ = [{x: k.sb("st" + x, [128, 1]) for x in ("zr", "zi")} for _ in range(NB)]
    psB = [[k.ps("psBr", [128, 512]), k.ps("psBi", [128, 512])] for _ in range(2)]
    psY = [k.ps("psY", [128, 512]) for _ in range(2)]
    blocks = [(0, 256)] + [(CTX + i * 512, 512) for i in range(4)]
    nblk = 0
    for c in range(8):
        for b in range(NB):
            k.dma("sp", uch[b][:], G.UT_d[b, c, :, :], reads=[G.UT_d], writes=[uch[b]])
            k.memset("pool", yacc[b][:], 0.0, writes=[yacc[b]])
        for s_ in range(4):
            unit = 4 * c + s_
            for dr in range(2):
                for ri, Bsrc in ((0, Bre), (1, Bim)):
                    bd = BD[ri]
                    k.memset("pool", bd[:], 0.0, writes=[bd])
                    k.copy("pool", bd[0:64, 32 * s_:32 * s_ + 16], Bsrc[0:64, dr, unit, :], [Bsrc], [bd])
                    k.copy("pool", bd[64:128, 32 * s_ + 16:32 * s_ + 32], Bsrc[64:128, dr, unit, :], [Bsrc], [bd])
                    pt = psY[ri]
                    k.tr(pt[:, 0:128], bd[:], G.ident[:], [bd, G.ident], [pt])
                    k.copy("act", WB[ri][:], pt[:, 0:128], [pt], [WB[ri]])
                for ri, Csrc, Cap in ((0, BC, cre), (1, nCim, nCim[:])):
                    wc = WC[ri]
                    k.memset("pool", wc[:], 0.0, writes=[wc])
                    k.copy("pool", wc[0:64, 32 * s_:32 * s_ + 16], Cap[0:64, dr, unit, :], [Csrc], [wc])
                    k.copy("pool", wc[64:128, 32 * s_ + 16:32 * s_ + 32], Cap[64:128, dr, unit, :], [Csrc], [wc])
                k.ts("dve", cosT[:], tT[:], V["thp"][:, dr, unit:unit + 1], None, ALU.mult, reads=[tT, V["thp"]], writes=[cosT])
                range_reduce(k, sinT[:], cosT[:], ntT[:], ([cosT], [ntT], [sinT]))
                k.actf(ntT[:], sinT[:], AF.Abs, [sinT], [ntT])
                k.actf(cosT[:], ntT[:], AF.Sin, [ntT, halfpi], [cosT], scale=-1.0, bias=halfpi[:, 0:1])
                k.actf(sinT[:], sinT[:], AF.Sin, [sinT], [sinT])
                k.ts("dve", rT[:], tT[:, 0:512], 0.0, V["mag"][:, dr, unit:unit + 1], ALU.mult, ALU.add, reads=[tT, V["mag"]], writes=[rT])
                for bi_, (lo, N) in enumerate(blocks):
                    for b in range(NB):
                        w = dict(Wst[b])
                        w.update(Wab[nblk % 2])
                        pB = psB[nblk % 2]
                        pY = psY[nblk % 2]
                        xre, xim = xb[b]
                        nblk += 1
                        if dr == 0:
                            a0, a1 = lo, lo + N
                            usl = uch[b][:, a0:a1]
                            ysl = yacc[b][:, a0:a1]
                        else:
                            a0, a1 = rev_tprime(lo, N)
                            usl = uch[b][:, a0:a1][:, ::-1]
                            ysl = yacc[b][:, a0:a1][:, ::-1]
                        k.mm(pB[0][:, 0:N], WB[0][:], usl, True, True, [WB[0], uch[b]], [pB[0]])
                        k.mm(pB[1][:, 0:N], WB[1][:], usl, True, True, [WB[1], uch[b]], [pB[1]])
                        cs_, sn_ = cosT[:, lo:lo + N], sinT[:, lo:lo + N]
                        k.tt("dve", w["a"][:, 0:N], pB[0][:, 0:N], cs_, ALU.mult, [pB[0], cosT], [w["a"]])
                        k.tt("dve", w["b"][:, 0:N], pB[1][:, 0:N], sn_, ALU.mult, [pB[1], sinT], [w["b"]])
                        k.tt("pool", w["cr"][:, 0:N], w["a"][:, 0:N], w["b"][:, 0:N], ALU.add, [w["a"], w["b"]], [w["cr"]])
                        k.tt("dve", w["a"][:, 0:N], pB[1][:, 0:N], cs_, ALU.mult, [pB[1], cosT], [w["a"]])
                        k.tt("dve", w["b"][:, 0:N], pB[0][:, 0:N], sn_, ALU.mult, [pB[0], sinT], [w["b"]])
                        k.tt("pool", w["ci"][:, 0:N], w["a"][:, 0:N], w["b"][:, 0:N], ALU.subtract, [w["a"], w["b"]], [w["ci"]])
                        for src, dst in (("cr", "zr"), ("ci", "zi")):
                            st = stt_[b][dst]
                            init = 0.0 if bi_ == 0 else st[:, 0:1]
                            rd = [rT, w[src]] + ([] if bi_ == 0 else [st])
                            k.op("dve", lambda e: e.tensor_tensor_scan(w[dst][:, 0:N], rT[:, 0:N], w[src][:, 0:N], init,
                                                                         ALU.mult, ALU.add), rd, [w[dst]])
                            k.copy("act", st[:, 0:1], w[dst][:, N - 1:N], [w[dst]], [st])
                        k.tt("dve", w["a"][:, 0:N], w["zr"][:, 0:N], cs_, ALU.mult, [w["zr"], cosT], [w["a"]])
                        k.tt("dve", w["b"][:, 0:N], w["zi"][:, 0:N], sn_, ALU.mult, [w["zi"], sinT], [w["b"]])
                        k.tt("pool", xre[:, 0:N], w["a"][:, 0:N], w["b"][:, 0:N], ALU.subtract, [w["a"], w["b"]], [xre])
                        k.tt("dve", w["a"][:, 0:N], w["zr"][:, 0:N], sn_, ALU.mult, [w["zr"], sinT], [w["a"]])
                        k.tt("dve", w["b"][:, 0:N], w["zi"][:, 0:N], cs_, ALU.mult, [w["zi"], cosT], [w["b"]])
                        k.tt("pool", xim[:, 0:N], w["a"][:, 0:N], w["b"][:, 0:N], ALU.add, [w["a"], w["b"]], [xim])
                        k.mm(pY[:, 0:N], WC[0][:], xre[:, 0:N], True, False, [WC[0], xre], [pY])
                        k.mm(pY[:, 0:N], WC[1][:], xim[:, 0:N], False, True, [WC[1], xim], [pY])
                        k.tt("dve", ysl, pY[:, 0:N], ysl, ALU.add, [pY, yacc[b]], [yacc[b]])
        for b in range(NB):
            k.stt(yacc[b][:], uch[b][:], sd[:, c:c + 1], yacc[b][:], ALU.mult, ALU.add, [uch[b], sd, yacc[b]], [yacc[b]])
            k.actf(zout[:], yacc[b][:], AF.Gelu_apprx_tanh, [yacc[b]], [zout])
            k.dma("act", G.ZT_d[b, c, :, :], zout[:], reads=[zout], writes=[G.ZT_d])
    k.pop()
    k.push()
    wg = k.sb("wg", [128, 8, 2048], BF16)
    k.dma("sp", wg[:], G.WgluB.h.rearrange("(k p) n -> p k n", p=128), reads=[G.WgluB], writes=[wg])
    zT = [k.sb("zT", [128, 8, 256], BF16) for _ in range(2)]
    hT = k.sb("hT", [128, 8, 256])
    rr = k.sb("rr", [128, 8, 256])
    sg = [k.sb("sg", [128, 256]) for _ in range(2)]
    oc = [k.sb("oc", [128, 256]) for _ in range(2)]
    tmp = dict(sq=k.sb("sq", [128, 8, 256]), mean=k.sb("mean", [128, 256]), rstd=k.sb("rstd", [128, 256]))
    psV = [k.ps("psV", [128, 512]) for _ in range(2)]
    psL = k.ps("psL", [128, 512])
    n = 0
    for b, blk, t0, row in blocks_of(G, need_ctx):
        z_ = zT[n % 2]
        n += 1
        k.dma("sp", z_[:], G.ZT_d[b, :, :, t0:t0 + 256].rearrange("c p t -> p c t"), reads=[G.ZT_d], writes=[z_])
        k.dma("sp", hT[:], Hin[b, :, :, t0:t0 + 256], reads=[Hin], writes=[hT])
        k.actf(hT[:].rearrange("p c t -> p (c t)"), hT[:].rearrange("p c t -> p (c t)"), AF.Copy, [hT], [hT], scale=ALPHA)
        for dc in range(8):
            pv = psV[dc % 2]
            for kc in range(8):
                k.mm(pv[:, 0:256], wg[:, kc, dc * 128:(dc + 1) * 128], z_[:, kc, :], kc == 0, kc == 7, [wg, z_], [pv])
            for kc in range(8):
                k.mm(pv[:, 256:512], wg[:, kc, 1024 + dc * 128:1024 + (dc + 1) * 128], z_[:, kc, :], kc == 0, kc == 7, [wg, z_], [pv])
            s_, o_ = sg[dc % 2], oc[dc % 2]
            k.actf(s_[:], pv[:, 256:512], AF.Sigmoid, [pv], [s_])
            k.tt("dve", o_[:], pv[:, 0:256], s_[:], ALU.mult, [pv, s_], [o_])
            k.stt(rr[:, dc, :], o_[:], G.modT[:, li, seg(2, dc), row:row + 1], hT[:, dc, :], ALU.mult, ALU.add,
                  [o_, G.modT, hT], [rr])
        layer_norm_fm(k, G, rr, 256, li, 0, rr, psL, tmp)
        k.dma("act", Hout[b, :, :, t0:t0 + 256], rr[:], reads=[rr], writes=[Hout])
    k.pop()


def host_weights_hg(inp):
    w = {}
    w["hg_w_in"] = np.ascontiguousarray(inp["hg_w_in"][0])
    w["hg_w_out"] = np.ascontiguousarray(inp["hg_w_out"][0])
    w["hg_norm"] = np.ascontiguousarray(inp["hg_norm"][0].reshape(8, 128).T)
    w["hg_lb"] = np.ascontiguousarray(inp["hg_lb"].reshape(4, 8, 128).transpose(2, 1, 0))
    m = np.zeros((128, 64 + NT), np.float32)
    s_ = (np.arange(128) % 64)[:, None]
    t_ = np.arange(64)[None, :]
    m[:, 0:64] = (s_ <= t_).astype(np.float32)
    m[:, 64:] = (np.arange(NT) % 64 != 0).astype(np.float32)[None, :]
    w["hgmask"] = m.astype(ml_dtypes.bfloat16)
    return w


def out_proj_ln(k, G, li, b, Hin, Hout, OnT, WB_, need_ctx):
    NB = G.NB
    k.push()
    wo = k.sb("wo", [128, 8, 1024], BF16)
    k.dma("sp", wo[:], WB_.h.rearrange("(k p) n -> p k n", p=128), reads=[WB_], writes=[wo])
    hT = k.sb("hT", [128, 8, 256])
    rr = k.sb("rr", [128, 8, 256])
    tmp = dict(sq=k.sb("sq", [128, 8, 256]), mean=k.sb("mean", [128, 256]), rstd=k.sb("rstd", [128, 256]))
    psO = [k.ps("psO", [128, 512]) for _ in range(4)]
    psL = k.ps("psL", [128, 512])
    for blk in range(NT // 256):
        if blk == 0 and not need_ctx:
            continue
        row = NB if blk == 0 else b
        t0 = blk * 256
        k.dma("sp", hT[:], Hin[b, :, :, t0:t0 + 256], reads=[Hin], writes=[hT])
        for dc in range(8):
            po = psO[dc // 2][:, (dc % 2) * 256:(dc % 2 + 1) * 256]
            for h in range(8):
                k.mm(po, wo[:, h, dc * 128:(dc + 1) * 128], OnT[:, h, t0:t0 + 256], h == 0, h == 7, [wo, OnT], [psO[dc // 2]])
        k.actf(hT[:].rearrange("p c t -> p (c t)"), hT[:].rearrange("p c t -> p (c t)"), AF.Copy, [hT], [hT], scale=ALPHA)
        for c in range(8):
            k.stt(rr[:, c, :], psO[c // 2][:, (c % 2) * 256:(c % 2 + 1) * 256], G.modT[:, li, seg(2, c), row:row + 1],
                  hT[:, c, :], ALU.mult, ALU.add, [psO[c // 2], G.modT, hT], [rr])
        layer_norm_fm(k, G, rr, 256, li, 0, rr, psL, tmp)
        k.dma("act", Hout[b, :, :, t0:t0 + 256], rr[:], reads=[rr], writes=[Hout])
    k.pop()


def tp_blocks(dr, step=512):
    out = []
    los = [(0, 256)] if step >= 256 else [(i, step) for i in range(0, 256, step)]
    los = los + [(CTX + i, step) for i in range(0, LAT, step)]
    for lo, N in los:
        if dr == 0:
            out.append((lo, N, lo, lo + N))
        else:
            a0, a1 = rev_tprime(lo, N)
            out.append((lo, N, a0, a1))
    return out


def hgrn2_layer(k, G, li, Hin, Hout, need_ctx):
    NB = G.NB
    CH = 64
    NCH = NT // CH
    cast_bf16(k, G.WhgB, G.hg_w_in.h, 1024, 5120)
    cast_bf16(k, G.WhgoB, G.hg_w_out.h, 1024, 1024)
    k.push()
    lbe = k.sb("lbe", [128, 8, 4])
    lbs = k.sb("lbs", [128, 8])
    lb = k.sb("lb", [128, 8])
    oml = k.sb("oml", [128, 8])
    ng = k.sb("ng", [128, 8])
    k.dma("sp", lbe[:], G.hg_lb[:, :, :], reads=[G.hg_lb], writes=[lbe])
    k.dma("sp", ng[:], G.hg_norm[:, :], reads=[G.hg_norm], writes=[ng])
    k.actf(lbe[:], lbe[:], AF.Exp, [lbe], [lbe])
    k.op("dve", lambda e: e.tensor_reduce(lbs[:], lbe[:], AX.X, ALU.add), [lbe], [lbs])
    k.op("dve", lambda e: e.reciprocal(lbs[:], lbs[:]), [lbs], [lbs])
    k.op("dve", lambda e: e.tensor_reduce(lb[:], lbe[:, :, 1:li + 1], AX.X, ALU.add), [lbe], [lb])
    k.tt("dve", lb[:], lb[:], lbs[:], ALU.mult, [lb, lbs], [lb])
    k.ts("dve", oml[:], lb[:], -1.0, 1.0, ALU.mult, ALU.add, reads=[lb], writes=[oml])
    msk = k.sb("msk", [128, 64 + NT], BF16)
    k.dma("sp", msk[:], G.hgmask_d[:, :], reads=[G.hgmask_d], writes=[msk])
    ones128 = k.sb("ones128", [128, 128])
    k.memset("dve", ones128[:], 1.0 / 128, writes=[ones128])
    identb = k.sb("identb", [128, 128], BF16)
    k.copy("dve", identb[:], G.ident[:], [G.ident], [identb])
    Jb = k.sb("Jb", [128, 128], BF16)
    k.copy("dve", Jb[:], G.ident[:, ::-1], [G.ident], [Jb])
    WV = G.WhgB.h.rearrange("(k p) n -> p k n", p=128)
    for b in range(NB):
        k.push()
        OgT = k.sb("OgT", [128, 8, NT], BF16)
        k.push()
        uTb = k.sb("uTb", [128, 8, NT], BF16)
        hblk = [k.sb("hblk", [128, 8, 256]) for _ in range(2)]
        for blk in range(NT // 256):
            hb = hblk[blk % 2]
            row = NB if blk == 0 else b
            k.dma("sp", hb[:], Hin[b, :, :, blk * 256:(blk + 1) * 256], reads=[Hin], writes=[hb])
            for c in range(8):
                k.actf(uTb[:, c, blk * 256:(blk + 1) * 256], hb[:, c, :], AF.Identity, [hb, G.modT, G.mod1], [uTb],
                       scale=G.mod1[:, li, seg(1, c), row:row + 1], bias=G.modT[:, li, seg(0, c), row:row + 1])
        wch = [k.sb("wch", [128, 8, 128], BF16) for _ in range(2)]
        orow = [k.sb("orow", [128, NT]) for _ in range(2)]
        psa = [k.ps("psa", [128, 512]) for _ in range(2)]
        psv = [k.ps("psv", [128, 512]) for _ in range(2)]
        n = 0
        nw = 0
        for kind, cbase in ((0, 0), (1, 1024), (2, 2048), (3, 4096)):
            for h in range(8):
                w_ = wch[nw % 2]
                orw = orow[nw % 2]
                nw += 1
                k.dma("sp", w_[:], WV[:, :, cbase + h * 128:cbase + (h + 1) * 128], reads=[G.WhgB], writes=[w_])
                for (t0, N) in [(0, 256)] + [(CTX + i * 512, 512) for i in range(4)]:
                    pa = psa[n % 2]
                    n += 1
                    for kc in range(8):
                        k.mm(pa[:, 0:N], w_[:, kc, :], uTb[:, kc, t0:t0 + N], kc == 0, kc == 7, [w_, uTb], [pa])
                    k.copy("act" if n % 2 == 0 else "dve", orw[:, t0:t0 + N], pa[:, 0:N], [pa], [orw])
                k.dma("act", G.PQ_d[b, kind, h, :, :], orw[:], reads=[orw], writes=[G.PQ_d])
        wv = k.sb("wv", [128, 8, 1024], BF16)
        k.dma("sp", wv[:], WV[:, :, 3072:4096], reads=[G.WhgB], writes=[wv])
        vrow = [k.sb("vrow", [128, 1024], BF16) for _ in range(2)]
        nv = 0
        for dr in range(1):
            for (lo, N, a0, a1) in tp_blocks(dr, 128):
                vr = vrow[nv % 2]
                nv += 1
                for cb in range(2):
                    pv = psv[cb]
                    for kc in range(8):
                        lhs = uTb[:, kc, a0:a1] if dr == 0 else uTb[:, kc, a0:a1][:, ::-1]
                        k.mm(pv[:], lhs, wv[:, kc, cb * 512:(cb + 1) * 512], kc == 0, kc == 7, [uTb, wv], [pv])
                    k.copy("act" if cb == 0 else "dve", vr[:, cb * 512:(cb + 1) * 512], pv[:], [pv], [vr])
                dst = G.V_d if dr == 0 else G.Vr_d
                k.dma("act", dst[b, lo:lo + 128, :], vr[:], reads=[vr], writes=[dst])
        k.pop()
        k.push()
        qs = k.sb("qs", [128, NT])
        raw = k.sb("raw", [128, NT])
        tA = k.sb("tA", [128, NT])
        tB = k.sb("tB", [128, NT])
        tC = k.sb("tC", [128, NT])
        oacc = k.sb("oacc", [128, NT])
        qtb = [k.sb("qtb", [128, NT], BF16) for _ in range(2)]
        ktb = [k.sb("ktb", [128, NT], BF16) for _ in range(2)]
        khb = [k.sb("khb", [128, NT], BF16) for _ in range(2)]
        ecl = [k.sb("ecl", [128, NCH]) for _ in range(2)]
        khtok = [k.sb("khtok", [128, NT // 128, 128], BF16) for _ in range(2)]
        Vh = [k.sb("Vh", [128, NT // 128, 128], BF16) for _ in range(2)]
        S = [k.sb("S", [128, 128]) for _ in range(2)]
        Sb = [k.sb("Sb", [128, 128], BF16) for _ in range(2)]
        Am = [k.sb("Am", [128, 64], BF16) for _ in range(4)]
        psT = k.ps("psT", [128, 512], BF16)
        psAd = [k.ps("psA", [128, 512]) for _ in range(2)]
        psOo = [k.ps("psOo", [128, 512]) for _ in range(2)]
        psSd = [k.ps("psS", [128, 512]) for _ in range(2)]
        psM = k.ps("psM", [128, 512])
        sqb = k.sb("sqb", [128, 512])
        rinv = k.sb("rinv", [128, 512])
        psA_r = psAd
        psS_r = psSd
        for h in range(8):
            k.dma("sp", raw[:], G.PQ_d[b, 0, h, :, :], reads=[G.PQ_d], writes=[raw])
            k.actf(qs[:], raw[:], AF.Silu, [raw], [qs])
            k.memset("pool", oacc[:], 0.0, writes=[oacc])
            for dr in range(2):
                if dr == 0:
                    k.dma("sp", Vh[dr][:], G.V_d[b, :, h * 128:(h + 1) * 128].rearrange("(t p) e -> p t e", p=128),
                          reads=[G.V_d], writes=[Vh[dr]])
                else:
                    for tg in range(0, 18, 4):
                        nn = min(4, 18 - tg)
                        for j in range(nn):
                            tau = tg + j
                            src = 1 - tau if tau < 2 else 19 - tau
                            k.mm(psM[:, j * 128:(j + 1) * 128], Jb[:], Vh[0][:, src, :], True, True, [Jb, Vh[0]], [psM])
                        k.copy("act", Vh[1][:, tg:tg + nn, :].rearrange("p a b -> p (a b)"), psM[:, 0:nn * 128], [psM], [Vh[1]])
                k.dma("sp", raw[:], G.PQ_d[b, 1 + dr, h, :, :], reads=[G.PQ_d], writes=[raw])
                if dr == 0:
                    k.actf(tA[:], raw[:], AF.Sigmoid, [raw], [tA])
                else:
                    k.actf(tA[:, 0:CTX], raw[:, 0:CTX][:, ::-1], AF.Sigmoid, [raw], [tA])
                    k.actf(tA[:, CTX:NT], raw[:, CTX:NT][:, ::-1], AF.Sigmoid, [raw], [tA])
                k.ts("dve", tA[:], tA[:], oml[:, h:h + 1], lb[:, h:h + 1], ALU.mult, ALU.add, reads=[tA, oml, lb], writes=[tA])
                k.actf(tB[:], tA[:], AF.Ln, [tA], [tB])
                k.ts("dve", tA[:], tA[:], -1.0, 1.0, ALU.mult, ALU.add, reads=[tA], writes=[tA])
                k.op("dve", lambda e: e.tensor_tensor_scan(tC[:], msk[:, 64:64 + NT], tB[:], 0.0, ALU.mult, ALU.add), [msk, tB], [tC])
                k.actf(tB[:], tC[:], AF.Exp, [tC], [tB])
                k.actf(tC[:], tC[:], AF.Exp, [tC], [tC], scale=-1.0)
                if dr == 0:
                    k.tt("dve", qtb[dr][:], qs[:], tB[:], ALU.mult, [qs, tB], [qtb[dr]])
                else:
                    k.tt("dve", qtb[dr][:, 0:CTX], qs[:, 0:CTX][:, ::-1], tB[:, 0:CTX], ALU.mult, [qs, tB], [qtb[dr]])
                    k.tt("dve", qtb[dr][:, CTX:NT], qs[:, CTX:NT][:, ::-1], tB[:, CTX:NT], ALU.mult, [qs, tB], [qtb[dr]])
                k.copy("act", ecl[dr][:], tB[:, CH - 1::CH], [tB], [ecl[dr]])
                k.tt("dve", tA[:], tA[:], tC[:], ALU.mult, [tA, tC], [tA])
                k.copy("pool", ktb[dr][:], tA[:], [tA], [ktb[dr]])
                k.tt("dve", khb[dr][:].rearrange("p (c t) -> p c t", t=CH), tA[:].rearrange("p (c t) -> p c t", t=CH),
                     ecl[dr][:].unsqueeze(2).to_broadcast([128, NCH, CH]), ALU.mult, [tA, ecl[dr]], [khb[dr]])
                for tl in range(NT // 128):
                    k.tr(psT[:, (tl % 4) * 128:(tl % 4 + 1) * 128], khb[dr][:, tl * 128:(tl + 1) * 128], identb[:], [khb[dr], identb], [psT])
                    if tl % 4 == 3 or tl == NT // 128 - 1:
                        t0_ = (tl // 4) * 4
                        nn = tl - t0_ + 1
                        k.copy("act", khtok[dr][:, t0_:tl + 1, :].rearrange("p a b -> p (a b)"), psT[:, 0:nn * 128], [psT], [khtok[dr]])
            for ci in range(NCH):
                par = ci % 2
                tl = ci // 2
                pr = slice(64 * par, 64 * par + 64)
                tsl = slice(ci * CH, (ci + 1) * CH)
                bank_pos = ci % 8
                for dr in range(2):
                    am = Am[(ci % 2) * 2 + dr]
                    pa = psAd[dr][pr, (ci % 8) * 64:(ci % 8 + 1) * 64]
                    k.mm(pa, ktb[dr][:, tsl], qtb[dr][:, tsl], True, True, [ktb[dr], qtb[dr]], [psA_r[dr]])
                    k.tt("dve", am[pr, :], pa, msk[pr, 0:64], ALU.mult, [psA_r[dr], msk], [am])
                    po = psOo[dr][:, bank_pos * 64:(bank_pos + 1) * 64]
                    k.mm(po, Vh[dr][pr, tl, :], am[pr, :], True, ci == 0, [Vh[dr], am], [psOo[dr]])
                    if ci > 0:
                        k.mm(po, Sb[dr][:], qtb[dr][:, tsl], False, True, [Sb[dr], qtb[dr]], [psOo[dr]])
                    ps_ = psSd[dr][:, (ci % 4) * 128:(ci % 4 + 1) * 128]
                    k.mm(ps_, khtok[dr][pr, tl, :], Vh[dr][pr, tl, :], True, True, [khtok[dr], Vh[dr]], [psS_r[dr]])
                    if ci == 0:
                        k.copy("dve", S[dr][:], ps_, [psS_r[dr]], [S[dr]])
                    else:
                        k.stt(S[dr][:], S[dr][:], ecl[dr][:, ci:ci + 1], ps_, ALU.mult, ALU.add, [S[dr], ecl[dr], psS_r[dr]], [S[dr]])
                    k.copy("act", Sb[dr][:], S[dr][:], [S[dr]], [Sb[dr]])
                    if bank_pos == 7 or (ci == 3):
                        pass
                done = None
                if ci == 3:
                    done = (0, 256)
                elif ci > 3 and (ci - 4) % 8 == 7:
                    done = ((ci - 7) * CH, 512)
                if done is not None:
                    lo, N = done
                    for dr in range(2):
                        c0 = ((lo // CH) % 8) * 64
                        if c0 + N <= 512:
                            srcs = [(psOo[dr][:, c0:c0 + N], lo, N)]
                        else:
                            n1 = 512 - c0
                            srcs = [(psOo[dr][:, c0:512], lo, n1), (psOo[dr][:, 0:N - n1], lo + n1, N - n1)]
                        for (sp_, l0, n_) in srcs:
                            if dr == 0:
                                dst = oacc[:, l0:l0 + n_]
                            else:
                                a0, a1 = rev_tprime(l0, n_)
                                dst = oacc[:, a0:a1][:, ::-1]
                            k.tt("dve", dst, sp_, dst, ALU.add, [psOo[dr], oacc], [oacc])
            k.dma("sp", raw[:], G.PQ_d[b, 3, h, :, :], reads=[G.PQ_d], writes=[raw])
            k.actf(tA[:], raw[:], AF.Silu, [raw], [tA])
            for (t0, N) in [(0, 256)] + [(CTX + i * 512, 512) for i in range(4)]:
                k.actf(sqb[:, 0:N], oacc[:, t0:t0 + N], AF.Square, [oacc], [sqb])
                k.mm(psM[:, 0:N], ones128[:], sqb[:, 0:N], True, True, [ones128, sqb], [psM])
                k.actf(rinv[:, 0:N], psM[:, 0:N], AF.Sqrt, [psM], [rinv], bias=G.epsrms[:, 0:1])
                k.op("dve", lambda e: e.reciprocal(rinv[:, 0:N], rinv[:, 0:N]), [rinv], [rinv])
                k.tt("dve", rinv[:, 0:N], rinv[:, 0:N], oacc[:, t0:t0 + N], ALU.mult, [rinv, oacc], [rinv])
                k.stt(OgT[:, h, t0:t0 + N], rinv[:, 0:N], ng[:, h:h + 1], tA[:, t0:t0 + N], ALU.mult, ALU.mult, [rinv, ng, tA], [OgT])
        k.pop()
        out_proj_ln(k, G, li, b, Hin, Hout, OgT, G.WhgoB, need_ctx)
        k.pop()
    k.pop()


def build_program(NB):
    nc = bass.Bass("TRN2", target_bir_lowering=False)
    k = K(nc)
    G = setup_globals(k, NB)
    phase_adaln(k, G)
    cur = G.H[0]
    for li in range(DEPTH):
        need_ctx = li < DEPTH - 1
        mid = G.H[1]
        nxt = G.H[2]
        kind = li % 3
        if kind == 0:
            attention(k, G, li, cur, mid, need_ctx)
        elif kind == 1:
            s5_layer(k, G, li, cur, mid, need_ctx)
        else:
            hgrn2_layer(k, G, li, cur, mid, need_ctx)
        peer_cast(k, G, li)
        peer_route(k, G, li, mid, need_ctx)
        peer_experts(k, G, li, mid, nxt, need_ctx, out_final=(li == DEPTH - 1))
        cur = nxt
    k.finish()
    return nc, k


def kernel(**inp):
    inp = {k_: np.asarray(v) for k_, v in inp.items()}
    NCORES = 8
    B = inp["x"].shape[0]
    NB = B // NCORES
    nc, k = build_program(NB)
    w = host_weights(inp)
    w.update(host_weights_attn(inp))
    w.update(host_weights_s5(inp))
    w.update(host_weights_hg(inp))
    in_maps = []
    for c in range(NCORES):
        m = dict(w)
        m.update(host_core_inputs(inp, c * NB, NB))
        in_maps.append(m)
    res = run_bass_kernel_spmd(nc, in_maps, core_ids=list(range(NCORES)))
    outs = [unfm(r["outT"]) for r in res.results]
    return np.ascontiguousarray(np.concatenate(outs, axis=0)).astype(np.float32)
```

```python
import numpy as np
from contextlib import ExitStack
import concourse.bass as bass
import concourse.mybir as mybir
from concourse.bass_utils import run_bass_kernel_spmd

F32 = mybir.dt.float32
BF16 = mybir.dt.bfloat16
U32 = mybir.dt.uint32
I32 = mybir.dt.int32
AF = mybir.ActivationFunctionType
ALU = mybir.AluOpType
AX = mybir.AxisListType


class Res:
    __slots__ = ("name", "w", "rd")

    def __init__(self, name):
        self.name = name
        self.w = None
        self.rd = []


class Tl:
    def __init__(self, h, name):
        self.h = h
        self.res = Res(name)
        self.name = name

    def __getitem__(self, idx):
        return self.h[idx]


class K:
    ENG = ("pe", "dve", "act", "pool", "sp")
    GEN = 30000
    NDS = 6
    SKIP_SAME = True
    SKIP_RAW = False

    def __init__(self, nc):
        self.nc = nc
        self.root = ExitStack()
        self.stacks = [self.root]
        self.eng = dict(pe=nc.tensor, dve=nc.vector, act=nc.scalar, pool=nc.gpsimd, sp=nc.sync)
        self.allsems = []
        self.cur = {}
        self.cnt = {}
        self.seen = {e: {} for e in self.ENG}
        for e in self.ENG:
            self.cur[e] = self._newsem("c_" + e)
            self.cnt[e] = 0
        self.dq = {}
        for q in ("sp", "act", "pool"):
            self.dq[q] = dict(idx=[self._newsem(f"d_{q}{i}") for i in range(self.NDS)],
                              val=[0] * self.NDS, n=0)
        self.uid = 0
        self.ninst = 0

    def _newsem(self, name):
        s = self.root.enter_context(self.nc.semaphore(name + f"_{len(self.allsems)}"))
        self.allsems.append(s)
        return len(self.allsems) - 1

    def push(self):
        self.stacks.append(ExitStack())

    def pop(self):
        self.barrier()
        self.stacks.pop().close()

    def sb(self, name, shape, dtype=F32):
        self.uid += 1
        nm = f"{name}_{self.uid}"
        h = self.stacks[-1].enter_context(self.nc.sbuf_tensor(nm, list(shape), dtype))
        return Tl(h, nm)

    def ps(self, name, shape, dtype=F32):
        self.uid += 1
        nm = f"{name}_{self.uid}"
        h = self.stacks[-1].enter_context(self.nc.psum_tensor(nm, list(shape), dtype))
        return Tl(h, nm)

    def dram(self, name, shape, dtype=F32, kind="Internal"):
        t = self.nc.dram_tensor(name, list(shape), dtype, kind=kind)
        tl = Tl(t.ap(), name)
        return tl

    def _wait(self, e, tk):
        idx, v, src = tk
        if src == e and e == "pe":
            return
        if self.seen[e].get(idx, 0) >= v:
            return
        self.eng[e].wait_ge(self.allsems[idx], v)
        self.seen[e][idx] = v

    @staticmethod
    def _r(x):
        return x.res if isinstance(x, Tl) else x

    def _deps(self, e, reads, writes):
        for r in reads:
            r = self._r(r)
            if r.w is not None and (r.w[2] != e or not self.SKIP_RAW):
                self._wait(e, r.w)
        for w in writes:
            w = self._r(w)
            if w.w is not None and (w.w[2] != e or not self.SKIP_SAME):
                self._wait(e, w.w)
            for tk in w.rd:
                if tk[2] != e or not self.SKIP_SAME:
                    self._wait(e, tk)

    def _commit(self, tk, reads, writes):
        for r in reads:
            r = self._r(r)
            r.rd.append(tk)
            if len(r.rd) > 64:
                best = {}
                for t in r.rd:
                    if t[0] not in best or best[t[0]][1] < t[1]:
                        best[t[0]] = t
                r.rd = list(best.values())
        for w in writes:
            w = self._r(w)
            w.w = tk
            w.rd = []

    def op(self, e, fn, reads=(), writes=()):
        self._deps(e, reads, writes)
        ins = fn(self.eng[e])
        if self.cnt[e] >= self.GEN:
            self.cur[e] = self._newsem("c_" + e)
            self.cnt[e] = 0
        self.cnt[e] += 1
        ins.then_inc(self.allsems[self.cur[e]], 1)
        tk = (self.cur[e], self.cnt[e], e)
        self._commit(tk, reads, writes)
        self.ninst += 1
        return tk

    def dma(self, q, out, in_, reads=(), writes=(), **kw):
        self._deps(q, reads, writes)
        d = self.dq[q]
        k = d["n"] % self.NDS
        d["n"] += 1
        if d["val"][k] > 0:
            self._wait(q, (d["idx"][k], d["val"][k], "dma"))
        ins = self.eng[q].dma_start(out=out, in_=in_, **kw)
        d["val"][k] += 16
        ins.then_inc(self.allsems[d["idx"][k]], 16)
        tk = (d["idx"][k], d["val"][k], "dma")
        self._commit(tk, reads, writes)
        self.ninst += 1
        return tk

    def barrier(self):
        tks = []
        for e in self.ENG:
            if self.cnt[e] > 0:
                tks.append((self.cur[e], self.cnt[e], "x"))
        for q, d in self.dq.items():
            for k in range(self.NDS):
                if d["val"][k] > 0:
                    tks.append((d["idx"][k], d["val"][k], "dma"))
        for e in self.ENG:
            for tk in tks:
                self._wait(e, tk)

    def finish(self):
        self.barrier()
        while len(self.stacks) > 1:
            self.stacks.pop().close()
        self.root.close()

    def mm(self, out, lhsT, rhs, start, stop, reads, writes, **kw):
        return self.op("pe", lambda e: e.matmul(out, lhsT, rhs, start=start, stop=stop, **kw), reads, writes)

    def tr(self, out, in_, ident, reads, writes):
        return self.op("pe", lambda e: e.transpose(out, in_, ident), reads, writes)

    def actf(self, out, in_, func, reads, writes, bias=None, scale=1.0, accum_out=None, e="act"):
        kw = {}
        if bias is not None:
            kw["bias"] = bias
        if accum_out is not None:
            kw["accum_out"] = accum_out
        return self.op(e, lambda en: en.activation(out, in_, func, scale=scale, **kw), reads, writes)

    def ts(self, e, out, in0, s1, s2, op0, op1=None, reads=(), writes=(), accum_out=None):
        kw = {}
        if op1 is not None:
            kw["op1"] = op1
        if accum_out is not None:
            kw["accum_out"] = accum_out
        return self.op(e, lambda en: en.tensor_scalar(out, in0, s1, s2, op0, **kw), reads, writes)

    def tt(self, e, out, in0, in1, op, reads=(), writes=()):
        return self.op(e, lambda en: en.tensor_tensor(out, in0, in1, op), reads, writes)

    def stt(self, out, in0, scalar, in1, op0, op1, reads=(), writes=(), e="dve"):
        return self.op(e, lambda en: en.scalar_tensor_tensor(out, in0, scalar, in1, op0, op1), reads, writes)

    def copy(self, e, out, in_, reads=(), writes=()):
        if e == "act":
            return self.op(e, lambda en: en.copy(out, in_), reads, writes)
        return self.op(e, lambda en: en.tensor_copy(out, in_), reads, writes)

    def memset(self, e, ap, val, writes=()):
        return self.op(e, lambda en: en.memset(ap, val), (), writes)


D = 1024
CTX = 256
LAT = 2048
NT = CTX + LAT
DEPTH = 4
ALPHA = (2 * DEPTH) ** 0.25
LN_EPS = 1e-5
RMS_EPS = 1e-6
NRP = 8


class GG:
    pass


def seg(s, c):
    return s * 8 + c


def setup_globals(k, NB, dbg=False):
    G = GG()
    skind = "ExternalOutput" if dbg else "Internal"
    G.NB = NB
    ext = lambda n, s, d=F32: k.dram(n, s, d, kind="ExternalInput")
    G.hT0 = ext("hT0", [NB, 128, 8, NT])
    G.cT = ext("cT", [128, 8, NRP])
    G.ada_w = ext("ada_w", [4, 8, 128, 6144])
    G.ada_b = ext("ada_b", [128, 4, 48])
    G.ln_g = ext("ln_g", [128, 4, 2, 8])
    G.ln_b = ext("ln_b", [128, 4, 2, 8])
    G.peer_wq = ext("peer_wq", [4, 1024, 2048])
    G.peer_keysT = ext("peer_keysT", [4, 2, 128, 128])
    G.peer_u = ext("peer_u", [4, 128 * 128, 1024])
    G.peer_v = ext("peer_v", [4, 128 * 128, 1024])
    G.ident_d = ext("ident", [128, 128])
    G.iota3_d = ext("iota3", [128, 16 * 128], BF16)
    G.iota16_d = ext("iota16", [128, 16])
    G.H = [G.hT0, k.dram("H1", [NB, 128, 8, NT], kind=skind), k.dram("H2", [NB, 128, 8, NT], kind=skind)]
    G.RT = k.dram("RT", [NB, NT, 384], kind=skind)
    G.UB = k.dram("UB", [128, 128, 1024], BF16)
    G.VB = k.dram("VB", [128, 128, 1024], BF16)
    G.UB.sub = [Res(f"UB{i}") for i in range(16)]
    G.VB.sub = [Res(f"VB{i}") for i in range(16)]
    G.out = k.dram("outT", [NB, 128, 8, LAT], kind="ExternalOutput")
    G.da_w_in = ext("da_w_in", [2, 1024, 3072])
    G.da_w_sw = ext("da_w_sw", [2, 1024, 2048])
    G.da_w_out = ext("da_w_out", [2, 1024, 1024])
    G.da_lam_q = ext("da_lam_q", [2, 128])
    G.da_lam_k = ext("da_lam_k", [2, 128])
    G.da_subln = ext("da_subln", [2, 128, 1])
    G.rope = ext("rope", [4, 128, NT])
    def wscr(n, r, c):
        t = k.dram(n, [r, c], BF16)
        t.sub = [t.res]
        return t
    G.WinB = wscr("WinB", 1024, 3072)
    G.WswB = wscr("WswB", 1024, 2048)
    G.WoutB = wscr("WoutB", 1024, 1024)
    G.hg_w_in = ext("hg_w_in", [1024, 5120])
    G.hg_w_out = ext("hg_w_out", [1024, 1024])
    G.hg_norm = ext("hg_norm", [128, 8])
    G.hg_lb = ext("hg_lb", [128, 8, 4])
    G.hgmask_d = ext("hgmask", [128, 64 + NT], BF16)
    G.WhgB = wscr("WhgB", 1024, 5120)
    G.WhgoB = wscr("WhgoB", 1024, 1024)
    G.PQ_d = k.dram("PQ_d", [NB, 4, 8, 128, NT])
    G.Vr_d = k.dram("Vr_d", [NB, NT, 1024], BF16)
    G.s5p = ext("s5p", [128, 2, 32, 3])
    G.s5bc = ext("s5bc", [128, 2, 32, 4, 16])
    G.s5_d = ext("s5_d", [128, 8])
    G.s5_w_glu = ext("s5_w_glu", [1024, 2048])
    G.tT_d = ext("tT", [128, NT])
    G.WgluB = wscr("WgluB", 1024, 2048)
    G.UT_d = k.dram("UT_d", [NB, 8, 128, NT], BF16)
    G.ZT_d = k.dram("ZT_d", [NB, 8, 128, NT], BF16)
    G.QT_d = k.dram("QT_d", [NB, 8, 128, NT], BF16)
    G.KT_d = k.dram("KT_d", [NB, 8, 128, NT], BF16)
    G.V_d = k.dram("V_d", [NB, NT, 1024], BF16)
    G.modT = k.sb("modT", [128, 4, 48, NRP])
    G.mod1 = k.sb("mod1", [128, 4, 48, NRP])
    G.lng = k.sb("lng", [128, 4, 2, 8])
    G.lnb = k.sb("lnb", [128, 4, 2, 8])
    G.ident = k.sb("ident", [128, 128])
    G.onesD = k.sb("onesD", [128, 128])
    G.iota3 = k.sb("iota3", [128, 16, 128], BF16)
    G.iota16 = k.sb("iota16", [128, 16])
    k.dma("sp", G.lng[:], G.ln_g[:, :, :, :], reads=[G.ln_g], writes=[G.lng])
    k.dma("sp", G.lnb[:], G.ln_b[:, :, :, :], reads=[G.ln_b], writes=[G.lnb])
    k.dma("sp", G.ident[:], G.ident_d[:, :], reads=[G.ident_d], writes=[G.ident])
    k.dma("sp", G.iota3[:].rearrange("p a b -> p (a b)"), G.iota3_d[:, :], reads=[G.iota3_d], writes=[G.iota3])
    k.dma("sp", G.iota16[:], G.iota16_d[:, :], reads=[G.iota16_d], writes=[G.iota16])
    k.memset("dve", G.onesD[:], 1.0 / D, writes=[G.onesD])
    G.epsln = k.sb("epsln", [128, 1])
    G.epsrms = k.sb("epsrms", [128, 1])
    k.memset("dve", G.epsln[:], LN_EPS, writes=[G.epsln])
    k.memset("dve", G.epsrms[:], RMS_EPS, writes=[G.epsrms])
    return G


def phase_adaln(k, G):
    k.push()
    cT = k.sb("cT", [128, 8, NRP])
    sT = k.sb("sT", [128, 8, NRP])
    adab = k.sb("adab", [128, 4, 48])
    k.dma("sp", cT[:], G.cT[:, :, :], reads=[G.cT], writes=[cT])
    k.dma("sp", adab[:], G.ada_b[:, :, :], reads=[G.ada_b], writes=[adab])
    k.actf(sT[:], cT[:], AF.Silu, [cT], [sT])
    ps = k.ps("adaps", [128, 48 * NRP])
    wb = [k.sb("adaw", [128, 6144]) for _ in range(2)]
    n = 0
    for i in range(4):
        first = True
        for kc in range(8):
            w = wb[n % 2]
            n += 1
            k.dma("sp", w[:], G.ada_w[i, kc, :, :], reads=[G.ada_w], writes=[w])
            for j in range(48):
                k.mm(ps[:, j * NRP:(j + 1) * NRP], w[:, j * 128:(j + 1) * 128], sT[:, kc, :],
                     first, (kc == 7 and j == 47), [w, sT], [ps], skip_group_check=True)
                first = False
        k.tt("dve", G.modT[:, i, :, :], ps[:].rearrange("p (j r) -> p j r", r=NRP),
             adab[:, i, :].unsqueeze(2).to_broadcast([128, 48, NRP]), ALU.add, [ps, adab], [G.modT])
    k.ts("dve", G.mod1[:].rearrange("p a b c -> p (a b c)"), G.modT[:].rearrange("p a b c -> p (a b c)"),
         1.0, None, ALU.add, reads=[G.modT], writes=[G.mod1])
    k.pop()


def cast_bf16(k, dst, src, rows, cols, step=1024):
    n = 0
    for r0 in range(0, rows, step):
        for c0 in range(0, cols, 1024):
            k.dma("pool", dst.h[r0:r0 + step, c0:c0 + 1024], src[r0:r0 + step, c0:c0 + 1024],
                  reads=[], writes=[dst.sub[n] if len(dst.sub) > 1 else dst])
            n += 1


def layer_norm_fm(k, G, r, T, li, which, out, psL, tmp):
    sq, mean, rstd = tmp["sq"], tmp["mean"], tmp["rstd"]
    k.actf(sq[:].rearrange("p c t -> p (c t)"), r[:].rearrange("p c t -> p (c t)"), AF.Square, [r], [sq])
    for c in range(8):
        k.mm(psL[:, 0:T], G.onesD[:], r[:, c, :], c == 0, c == 7, [G.onesD, r], [psL])
    k.copy("act", mean[:, 0:T], psL[:, 0:T], [psL], [mean])
    for c in range(8):
        k.mm(psL[:, T:2 * T], G.onesD[:], sq[:, c, :], c == 0, c == 7, [G.onesD, sq], [psL])
    k.tt("dve", rstd[:, 0:T], mean[:, 0:T], mean[:, 0:T], ALU.mult, [mean], [rstd])
    k.tt("dve", rstd[:, 0:T], psL[:, T:2 * T], rstd[:, 0:T], ALU.subtract, [psL, rstd], [rstd])
    k.actf(rstd[:, 0:T], rstd[:, 0:T], AF.Sqrt, [rstd], [rstd], bias=G.epsln[:, 0:1])
    k.op("dve", lambda e: e.reciprocal(rstd[:, 0:T], rstd[:, 0:T]), [rstd], [rstd])
    for c in range(8):
        k.tt("dve", sq[:, c, :], r[:, c, :], mean[:, 0:T], ALU.subtract, [r, mean], [sq])
        k.tt("dve", sq[:, c, :], sq[:, c, :], rstd[:, 0:T], ALU.mult, [sq, rstd], [sq])
        k.actf(out[:, c, :], sq[:, c, :], AF.Identity, [sq, G.lng, G.lnb], [out],
               scale=G.lng[:, li, which, c:c + 1], bias=G.lnb[:, li, which, c:c + 1])


def blocks_of(G, need_ctx):
    for b in range(G.NB):
        for blk in range(NT // 256):
            if blk == 0 and not need_ctx:
                continue
            yield b, blk, blk * 256, (G.NB if blk == 0 else b)


def peer_route(k, G, li, Hin, need_ctx):
    k.push()
    wq = k.sb("wq", [128, 8, 2048])
    kT = k.sb("kT", [128, 2, 128])
    k.dma("sp", wq[:], G.peer_wq[li].rearrange("(k p) n -> p k n", p=128), reads=[G.peer_wq], writes=[wq])
    k.dma("sp", kT[:], G.peer_keysT[li].rearrange("c d k -> d c k"), reads=[G.peer_keysT], writes=[kT])
    hT = [k.sb("hT", [128, 8, 256]) for _ in range(2)]
    uT2 = [k.sb("uT", [128, 8, 256]) for _ in range(2)]
    qT2 = [k.sb("qT", [128, 16, 256]) for _ in range(2)]
    psq = [k.ps("psq", [128, 512]) for _ in range(2)]
    pss = k.ps("pss", [128, 2048])
    sc2_ = [k.sb("sc", [128, 2048]) for _ in range(2)]
    nsc = 0
    scr = k.sb("scr", [128, 16, 128])
    V = k.sb("V", [128, 16, 16])
    V2 = k.sb("V2", [128, 16, 8])
    TS2 = k.sb("TS2", [128, 8, 8])
    I = k.sb("I", [128, 16, 16], U32)
    If = k.sb("If", [128, 16, 16])
    cand = k.sb("cand", [128, 8, 256])
    scr2 = k.sb("scr2", [128, 8, 256])
    TS = k.sb("TS", [128, 8, 16])
    PI = k.sb("PI", [128, 8, 16], U32)
    PA = k.sb("PA", [128, 8, 16], U32)
    PB = k.sb("PB", [128, 8, 16], U32)
    PAf = k.sb("PAf", [128, 8, 16])
    PBf = k.sb("PBf", [128, 8, 16])
    E = k.sb("E", [128, 8, 16, 16])
    ssum = k.sb("ssum", [128, 8])
    Rt = [k.sb("Rt", [128, 3, 128]) for _ in range(2)]
    Ifv = If[:].rearrange("p (h c) k -> p h c k", c=2)
    Vv = V[:].rearrange("p (h c) k -> p h c k", c=2)
    nb = 0
    for b, blk, tok0, row in blocks_of(G, need_ctx):
        h = hT[nb % 2]
        uT = uT2[nb % 2]
        qT = qT2[nb % 2]
        nb += 1
        k.dma("sp", h[:], Hin[b, :, :, tok0:tok0 + 256], reads=[Hin], writes=[h])
        for c in range(8):
            k.actf(uT[:, c, :], h[:, c, :], AF.Identity, [h, G.mod1, G.modT], [uT],
                   scale=G.mod1[:, li, seg(4, c), row:row + 1], bias=G.modT[:, li, seg(3, c), row:row + 1])
        for hc in range(16):
            ps = psq[(hc // 2) % 2]
            half = hc % 2
            for kc in range(8):
                k.mm(ps[:, half * 256:(half + 1) * 256], wq[:, kc, hc * 128:(hc + 1) * 128], uT[:, kc, :],
                     kc == 0, kc == 7, [wq, uT], [ps])
            if half == 1:
                k.copy("act", qT[:, hc - 1:hc + 1, :].rearrange("p a t -> p (a t)"), ps[:], [ps], [qT])
        for tt in range(2):
            for hc in range(16):
                k.mm(pss[:, hc * 128:(hc + 1) * 128], qT[:, hc, tt * 128:(tt + 1) * 128], kT[:, hc % 2, :],
                     True, True, [qT, kT], [pss])
            sc = sc2_[nsc % 2]
            nsc += 1
            k.copy("act", sc[:], pss[:], [pss], [sc])
            sl_ = lambda hc: sc[:, hc * 128:(hc + 1) * 128]
            for hc in range(16):
                k.op("dve", lambda e: e.max(V[:, hc, 0:8], sl_(hc)), [sc], [V])
            for hc in range(16):
                k.op("dve", lambda e: e.max_index(I[:, hc, 0:8], V[:, hc, 0:8], sl_(hc)), [sc, V], [I])
            for hc in range(16):
                k.op("dve", lambda e: e.match_replace(scr[:, hc, :], V[:, hc, 0:8], sl_(hc), -1e30), [sc, V], [scr])
            for hc in range(16):
                k.op("dve", lambda e: e.max(V2[:, hc, :], scr[:, hc, :]), [scr], [V2])
            for hc in range(16):
                k.op("dve", lambda e: e.max_index(I[:, hc, 8:16], V2[:, hc, :], scr[:, hc, :]), [scr, V2], [I])
            k.copy("dve", V[:, :, 8:16], V2[:], [V2], [V])
            k.copy("dve", If[:], I[:], [I], [If])
            k.tt("dve", cand[:].rearrange("p h (a b) -> p h a b", b=16),
                 Vv[:, :, 0, :].unsqueeze(3).to_broadcast([128, 8, 16, 16]),
                 Vv[:, :, 1, :].unsqueeze(2).to_broadcast([128, 8, 16, 16]), ALU.add, [V], [cand])
            for hh in range(8):
                k.op("dve", lambda e: e.max(TS[:, hh, 0:8], cand[:, hh, :]), [cand], [TS])
            for hh in range(8):
                k.op("dve", lambda e: e.max_index(PI[:, hh, 0:8], TS[:, hh, 0:8], cand[:, hh, :]), [cand, TS], [PI])
            for hh in range(8):
                k.op("dve", lambda e: e.match_replace(scr2[:, hh, :], TS[:, hh, 0:8], cand[:, hh, :], -1e30), [cand, TS], [scr2])
            for hh in range(8):
                k.op("dve", lambda e: e.max(TS2[:, hh, :], scr2[:, hh, :]), [scr2], [TS2])
            for hh in range(8):
                k.op("dve", lambda e: e.max_index(PI[:, hh, 8:16], TS2[:, hh, :], scr2[:, hh, :]), [scr2, TS2], [PI])
            k.copy("dve", TS[:, :, 8:16], TS2[:], [TS2], [TS])
            k.op("dve", lambda e: e.tensor_single_scalar(PA[:], PI[:], 4, ALU.logical_shift_right), [PI], [PA])
            k.op("dve", lambda e: e.tensor_single_scalar(PB[:], PI[:], 15, ALU.bitwise_and), [PI], [PB])
            k.copy("dve", PAf[:], PA[:], [PA], [PAf])
            k.copy("dve", PBf[:], PB[:], [PB], [PBf])
            R = Rt[tt]
            for which, Pf in ((0, PAf), (1, PBf)):
                k.tt("dve", E[:], G.iota16[:].unsqueeze(1).unsqueeze(1).to_broadcast([128, 8, 16, 16]),
                     Pf[:].unsqueeze(3).to_broadcast([128, 8, 16, 16]), ALU.is_equal, [G.iota16, Pf], [E])
                k.tt("dve", E[:], E[:], Ifv[:, :, which, :].unsqueeze(2).to_broadcast([128, 8, 16, 16]),
                     ALU.mult, [E, If], [E])
                k.op("dve", lambda e: e.tensor_reduce(R[:, which, :], E[:].rearrange("p h k a -> p (h k) a"),
                                                        AX.X, ALU.add), [E], [R])
            g3 = R[:, 2, :].rearrange("p (h k) -> p h k", k=16)
            k.tt("dve", g3, TS[:], TS[:, :, 0:1].to_broadcast([128, 8, 16]), ALU.subtract, [TS], [R])
            k.actf(g3, g3, AF.Exp, [R], [R])
            k.op("dve", lambda e: e.tensor_reduce(ssum[:], g3, AX.X, ALU.add), [R], [ssum])
            k.op("dve", lambda e: e.reciprocal(ssum[:], ssum[:]), [ssum], [ssum])
            k.tt("dve", g3, g3, ssum[:].unsqueeze(2).to_broadcast([128, 8, 16]), ALU.mult, [R, ssum], [R])
            t0 = tok0 + tt * 128
            k.dma("act", G.RT[b, t0:t0 + 128, :], R[:].rearrange("p a k -> p (a k)"), reads=[R], writes=[G.RT])
    k.pop()


def peer_experts(k, G, li, Hin, Hout, need_ctx, out_final=False):
    k.push()
    NBUF = 4
    hT2 = [k.sb("hT", [128, 8, 256]) for _ in range(2)]
    uTb2 = [k.sb("uTb", [128, 8, 256], BF16) for _ in range(2)]
    rr = k.sb("rr", [128, 8, 256])
    oo = rr
    nblk_ = 0
    tmp = dict(sq=k.sb("sq", [128, 8, 256]), mean=k.sb("mean", [128, 256]), rstd=k.sb("rstd", [128, 256]))
    Rt = [k.sb("Rt", [128, 384]) for _ in range(2)]
    RTt = k.sb("RTt", [128, 3, 256])
    SB_ = 16
    Q = [k.sb("Q", [128, SB_, 128], BF16) for _ in range(2)]
    Pg = [k.sb("Pg", [128, SB_, 128], BF16) for _ in range(2)]
    GT = k.sb("GT", [128, 256, 128], BF16)
    ut = [k.sb("ut", [128, 2, 8, 128], BF16) for _ in range(NBUF)]
    vt = [k.sb("vt", [128, 2, 1024], BF16) for _ in range(NBUF)]
    ga = [k.sb("ga", [128, 256], BF16) for _ in range(3)]
    gam = [k.sb("gam", [128, 256], BF16) for _ in range(3)]
    psO = [k.ps("psO", [128, 512]) for _ in range(4)]
    psG = [k.ps("psG", [128, 512]) for _ in range(2)]
    psA2 = [psG[0], psG[1], k.ps("psA", [128, 512])]
    psA_r = psA2
    psL = psG[0]
    NA = 3
    ng = 0
    nev = 0
    for b, blk, tok0, row in blocks_of(G, need_ctx):
        hT = hT2[0]
        uTb = uTb2[0]
        nblk_ += 1
        k.dma("sp", hT[:], Hin[b, :, :, tok0:tok0 + 256], reads=[Hin], writes=[hT])
        for c in range(8):
            k.actf(uTb[:, c, :], hT[:, c, :], AF.Identity, [hT, G.mod1, G.modT], [uTb],
                   scale=G.mod1[:, li, seg(4, c), row:row + 1], bias=G.modT[:, li, seg(3, c), row:row + 1])
        for tt in range(2):
            t0 = tok0 + tt * 128
            k.dma("sp", Rt[tt][:], G.RT[b, t0:t0 + 128, :], reads=[G.RT], writes=[Rt[tt]])
            pg = psG[ng % 2]
            ng += 1
            for a in range(3):
                k.tr(pg[:, a * 128:(a + 1) * 128], Rt[tt][:, a * 128:(a + 1) * 128], G.ident[:], [Rt[tt], G.ident], [pg])
            k.copy("act", RTt[:, :, tt * 128:(tt + 1) * 128], pg[:, 0:384].rearrange("p (a t) -> p a t", a=3), [pg], [RTt])
        def load_tab(ii):
            k.dma("sp", ut[ii % NBUF][:].rearrange("p a c j -> p a (c j)"), G.UB.h[:, 2 * ii:2 * ii + 2, :],
                  reads=[G.UB.sub[ii // 4]], writes=[ut[ii % NBUF]])
            k.dma("sp", vt[ii % NBUF][:], G.VB.h[:, 2 * ii:2 * ii + 2, :], reads=[G.VB.sub[ii // 4]], writes=[vt[ii % NBUF]])
        for ii in range(NBUF - 1):
            load_tab(ii)
        for sub in range(256 // SB_):
            s = sub % 2
            tsl = slice(sub * SB_, (sub + 1) * SB_)
            k.tt("dve", Q[s][:], G.iota3[:], RTt[:, 1, tsl].unsqueeze(2).to_broadcast([128, SB_, 128]),
                 ALU.is_equal, [G.iota3, RTt], [Q[s]])
            for t in range(SB_):
                tok = sub * SB_ + t
                k.ts("dve", Pg[s][:, t, :], G.iota3[:, 0, :], RTt[:, 0, tok:tok + 1], RTt[:, 2, tok:tok + 1],
                     ALU.is_equal, ALU.mult, reads=[G.iota3, RTt], writes=[Pg[s]])
            for t in range(SB_):
                tok = sub * SB_ + t
                pg = psG[ng % 2]
                k.mm(pg[:, (t % 4) * 128:(t % 4 + 1) * 128], Q[s][:, t, :], Pg[s][:, t, :], True, True,
                     [Q[s], Pg[s]], [pg])
                if t % 4 == 3:
                    ng += 1
                    k.copy("act" if nev % 2 == 0 else "dve", GT[:, tok - 3:tok + 1, :].rearrange("p t i -> p (t i)"),
                           pg[:], [pg], [GT])
                    nev += 1
        def amm(i):
            u_ = ut[(i // 2) % NBUF]
            pa = psA2[i % NA][:, 0:256]
            for c in range(8):
                k.mm(pa, u_[:, i % 2, c, :], uTb[:, c, :], c == 0, c == 7, [u_, uTb], [psA_r[i % NA]])
        amm(0)
        amm(1)
        for i in range(128):
            if i % 2 == 0 and i // 2 + NBUF - 1 < 64:
                load_tab(i // 2 + NBUF - 1)
            if i + 2 < 128:
                amm(i + 2)
            v_ = vt[(i // 2) % NBUF]
            pa = psA2[i % NA][:, 0:256]
            par = psA_r[i % NA]
            k.actf(ga[i % NA][:], pa, AF.Gelu_apprx_tanh, [par], [ga[i % NA]])
            k.tt("dve", gam[i % NA][:], ga[i % NA][:], GT[:, :, i], ALU.mult, [ga[i % NA], GT], [gam[i % NA]])
            for dc in range(8):
                k.mm(psO[dc // 2][:, (dc % 2) * 256:(dc % 2 + 1) * 256], v_[:, i % 2, dc * 128:(dc + 1) * 128], gam[i % NA][:],
                     (i == 0 and dc % 2 == 0), i == 127, [v_, gam[i % NA]], [psO[dc // 2]], skip_group_check=True)
        k.actf(hT[:].rearrange("p c t -> p (c t)"), hT[:].rearrange("p c t -> p (c t)"), AF.Copy, [hT], [hT], scale=ALPHA)
        for c in range(8):
            k.stt(rr[:, c, :], psO[c // 2][:, (c % 2) * 256:(c % 2 + 1) * 256], G.modT[:, li, seg(5, c), row:row + 1],
                  hT[:, c, :], ALU.mult, ALU.add, [psO[c // 2], G.modT, hT], [rr])
        layer_norm_fm(k, G, rr, 256, li, 1, oo, psL, tmp)
        if out_final:
            k.dma("act", G.out[b, :, :, tok0 - CTX:tok0 - CTX + 256], oo[:], reads=[oo], writes=[G.out])
        else:
            k.dma("act", Hout[b, :, :, tok0:tok0 + 256], oo[:], reads=[oo], writes=[Hout])
    k.pop()


def peer_cast(k, G, li):
    for dst, src in ((G.UB, G.peer_u), (G.VB, G.peer_v)):
        sv = src[li].rearrange("(i p) n -> p i n", p=128)
        for n in range(16):
            k.dma("pool", dst.h[:, 8 * n:8 * n + 8, :], sv[:, 8 * n:8 * n + 8, :], reads=[], writes=[dst.sub[n]])


import ml_dtypes


def fm(a):
    T = a.shape[-2]
    x = a.reshape(a.shape[:-2] + (T, 8, 128))
    nd = x.ndim
    perm = tuple(range(nd - 3)) + (nd - 1, nd - 2, nd - 3)
    return np.ascontiguousarray(np.transpose(x, perm))


def unfm(a):
    nd = a.ndim
    perm = tuple(range(nd - 3)) + (nd - 1, nd - 2, nd - 3)
    x = np.transpose(a, perm)
    return np.ascontiguousarray(x).reshape(x.shape[:-2] + (D,))


def host_consts():
    ident = np.eye(128, dtype=np.float32)
    iota3 = np.tile(np.arange(128, dtype=np.float32)[None, None, :], (128, 16, 1)).reshape(128, 16 * 128)
    iota16 = np.tile(np.arange(16, dtype=np.float32)[None, :], (128, 1))
    return dict(ident=ident, iota3=iota3.astype(ml_dtypes.bfloat16), iota16=iota16)


def host_weights(inp):
    w = {}
    w["ada_w"] = np.ascontiguousarray(inp["ada_w"]).reshape(4, 8, 128, 6144)
    w["ada_b"] = np.ascontiguousarray(inp["ada_b"].reshape(4, 48, 128).transpose(2, 0, 1))
    w["ln_g"] = np.ascontiguousarray(inp["ln_g"].reshape(4, 2, 8, 128).transpose(3, 0, 1, 2))
    w["ln_b"] = np.ascontiguousarray(inp["ln_b"].reshape(4, 2, 8, 128).transpose(3, 0, 1, 2))
    w["peer_wq"] = np.ascontiguousarray(inp["peer_wq"])
    w["peer_keysT"] = np.ascontiguousarray(inp["peer_keys"].transpose(0, 1, 3, 2))
    pu = inp["peer_u"].reshape(4, 128, 128, 8, 128).transpose(0, 1, 4, 3, 2)
    w["peer_u"] = np.ascontiguousarray(pu).reshape(4, 128 * 128, 1024)
    w["peer_v"] = np.ascontiguousarray(inp["peer_v"])
    w.update(host_consts())
    return w


def host_core_inputs(inp, b0, NB):
    cT = np.zeros((NRP, D), np.float32)
    cT[:NB] = inp["c"][b0:b0 + NB]
    cT[NB] = inp["c_ctx"]
    cT = np.ascontiguousarray(cT.reshape(NRP, 8, 128).transpose(2, 1, 0))
    tok = np.concatenate([inp["ctx"][b0:b0 + NB], inp["x"][b0:b0 + NB]], axis=1)
    return dict(cT=cT, hT0=fm(tok))


def rope_tables():
    rows = LAT // 64
    row = np.repeat(np.arange(rows, dtype=np.float32), 64)
    col = np.tile(np.arange(64, dtype=np.float32), rows)
    inv = (np.float32(10000.0) ** (-np.arange(16, dtype=np.float32) / np.float32(16))).astype(np.float32)
    ang = np.concatenate([row[:, None] * inv, col[:, None] * inv], axis=-1).astype(np.float32)
    cos, sin = np.cos(ang).astype(np.float32), np.sin(ang).astype(np.float32)
    p = np.arange(128)
    r = p % 64
    i = r // 2
    sign = np.where(r % 2 == 0, -1.0, 1.0).astype(np.float32)
    C = np.ones((128, NT), np.float32)
    S = np.zeros((128, NT), np.float32)
    C[:, CTX:] = cos[:, i].T
    S[:, CTX:] = sin[:, i].T * sign[:, None]
    return np.stack([C * np.float32(0.125), S * np.float32(0.125), C, S]).astype(np.float32)


def host_weights_attn(inp):
    w = {}
    w["da_w_in"] = np.ascontiguousarray(inp["da_w_in"])
    qk = inp["da_w_in"][:, :, :2048]
    w["da_w_sw"] = np.ascontiguousarray(qk.reshape(2, 1024, 1024, 2)[:, :, :, ::-1].reshape(2, 1024, 2048))
    w["da_w_out"] = np.ascontiguousarray(inp["da_w_out"])
    w["da_lam_q"] = np.ascontiguousarray(inp["da_lam_q"].reshape(2, 128))
    w["da_lam_k"] = np.ascontiguousarray(inp["da_lam_k"].reshape(2, 128))
    w["da_subln"] = np.ascontiguousarray(inp["da_subln"].reshape(2, 128, 1))
    w["rope"] = rope_tables()
    return w


def qblocks(need_ctx):
    out = []
    if need_ctx:
        out.append((0, 256, 2))
    for i in range(4):
        out.append((CTX + i * 512, 512, 18))
    return out


def attention(k, G, li, Hin, Hout, need_ctx):
    import math
    slot = li // 3
    lam_init = 0.8 - 0.6 * math.exp(-0.3 * li)
    NB = G.NB
    cast_bf16(k, G.WinB, G.da_w_in[slot], 1024, 3072)
    cast_bf16(k, G.WswB, G.da_w_sw[slot], 1024, 2048)
    cast_bf16(k, G.WoutB, G.da_w_out[slot], 1024, 1024)
    k.push()
    lq = k.sb("lq", [128, 128])
    lk = k.sb("lk", [128, 128])
    l2 = k.sb("l2", [128, 2])
    neglam = k.sb("neglam", [128, 1])
    subg = k.sb("subg", [128, 1])
    ones_b = k.sb("ones_b", [128, 128], BF16)
    ones128 = k.sb("ones128", [128, 128])
    k.memset("dve", ones_b[:], 1.0, writes=[ones_b])
    k.memset("dve", ones128[:], 1.0 / 128, writes=[ones128])
    k.dma("sp", lq[:], G.da_lam_q.h[slot:slot + 1, :].to_broadcast([128, 128]), reads=[G.da_lam_q], writes=[lq])
    k.dma("sp", lk[:], G.da_lam_k.h[slot:slot + 1, :].to_broadcast([128, 128]), reads=[G.da_lam_k], writes=[lk])
    k.dma("sp", subg[:], G.da_subln[slot], reads=[G.da_subln], writes=[subg])
    k.tt("dve", lq[:], lq[:], lk[:], ALU.mult, [lq, lk], [lq])
    k.op("dve", lambda e: e.tensor_reduce(l2[:], lq[:].rearrange("p (c d) -> p c d", c=2), AX.X, ALU.add), [lq], [l2])
    k.actf(l2[:], l2[:], AF.Exp, [l2], [l2])
    k.tt("dve", neglam[:], l2[:, 1:2], l2[:, 0:1], ALU.subtract, [l2], [neglam])
    k.ts("dve", neglam[:], neglam[:], -lam_init, None, ALU.add, reads=[neglam], writes=[neglam])
    k.ts("dve", subg[:], subg[:], 1.0 - lam_init, None, ALU.mult, reads=[subg], writes=[subg])
    OnT = k.sb("OnT", [128, 8, NT], BF16)
    for b in range(NB):
        k.push()
        uTb = k.sb("uTb", [128, 8, NT], BF16)
        hblk = [k.sb("hblk", [128, 8, 256]) for _ in range(2)]
        for blk in range(NT // 256):
            hb = hblk[blk % 2]
            row = NB if blk == 0 else b
            k.dma("sp", hb[:], Hin[b, :, :, blk * 256:(blk + 1) * 256], reads=[Hin], writes=[hb])
            for c in range(8):
                k.actf(uTb[:, c, blk * 256:(blk + 1) * 256], hb[:, c, :], AF.Identity, [hb, G.modT, G.mod1], [uTb],
                       scale=G.mod1[:, li, seg(1, c), row:row + 1], bias=G.modT[:, li, seg(0, c), row:row + 1])
        Ct = k.sb("Ct", [128, NT])
        St = k.sb("St", [128, NT])
        wch = [k.sb("wch", [128, 8, 128], BF16) for _ in range(2)]
        wsc = [k.sb("wsc", [128, 8, 128], BF16) for _ in range(2)]
        orow = [k.sb("orow", [128, NT], BF16) for _ in range(2)]
        t1 = [k.sb("t1", [128, 512]) for _ in range(2)]
        t2 = [k.sb("t2", [128, 512]) for _ in range(2)]
        psa = [k.ps("psa", [128, 512]) for _ in range(2)]
        psb = [k.ps("psb", [128, 512]) for _ in range(2)]
        psv = [k.ps("psv", [128, 512]) for _ in range(2)]
        WinV = G.WinB.h.rearrange("(k p) n -> p k n", p=128)
        WswV = G.WswB.h.rearrange("(k p) n -> p k n", p=128)
        n = 0
        for grp in range(2):
            k.dma("sp", Ct[:], G.rope[2 * grp], reads=[G.rope], writes=[Ct])
            k.dma("sp", St[:], G.rope[2 * grp + 1], reads=[G.rope], writes=[St])
            for ch in range(8):
                col0 = grp * 1024 + ch * 128
                w_, ws_ = wch[ch % 2], wsc[ch % 2]
                k.dma("sp", w_[:], WinV[:, :, col0:col0 + 128], reads=[G.WinB], writes=[w_])
                k.dma("sp", ws_[:], WswV[:, :, col0:col0 + 128], reads=[G.WswB], writes=[ws_])
                orw = orow[ch % 2]
                for (t0, N) in [(0, 256)] + [(CTX + i * 512, 512) for i in range(4)]:
                    pa, pb = psa[n % 2], psb[n % 2]
                    a1, a2 = t1[n % 2], t2[n % 2]
                    n += 1
                    for kc in range(8):
                        k.mm(pa[:, 0:N], w_[:, kc, :], uTb[:, kc, t0:t0 + N], kc == 0, kc == 7, [w_, uTb], [pa])
                    for kc in range(8):
                        k.mm(pb[:, 0:N], ws_[:, kc, :], uTb[:, kc, t0:t0 + N], kc == 0, kc == 7, [ws_, uTb], [pb])
                    k.tt("dve", a1[:, 0:N], pa[:, 0:N], Ct[:, t0:t0 + N], ALU.mult, [pa, Ct], [a1])
                    k.tt("dve", a2[:, 0:N], pb[:, 0:N], St[:, t0:t0 + N], ALU.mult, [pb, St], [a2])
                    k.tt("pool", orw[:, t0:t0 + N], a1[:, 0:N], a2[:, 0:N], ALU.add, [a1, a2], [orw])
                dst = G.QT_d if grp == 0 else G.KT_d
                k.dma("act", dst[b, ch, :, :], orw[:], reads=[orw], writes=[dst])
        wv = k.sb("wv", [128, 8, 1024], BF16)
        k.dma("sp", wv[:], WinV[:, :, 2048:3072], reads=[G.WinB], writes=[wv])
        vrow = [k.sb("vrow", [128, 1024], BF16) for _ in range(2)]
        for tl in range(NT // 128):
            vr = vrow[tl % 2]
            for cb in range(2):
                pv = psv[cb]
                for kc in range(8):
                    k.mm(pv[:], uTb[:, kc, tl * 128:(tl + 1) * 128], wv[:, kc, cb * 512:(cb + 1) * 512], kc == 0, kc == 7,
                         [uTb, wv], [pv])
                k.copy("act" if cb == 0 else "dve", vr[:, cb * 512:(cb + 1) * 512], pv[:], [pv], [vr])
            k.dma("act", G.V_d[b, tl * 128:(tl + 1) * 128, :], vr[:], reads=[vr], writes=[G.V_d])
        k.pop()
        k.push()
        Qh = [k.sb("Qh", [128, NT], BF16) for _ in range(2)]
        Kh = [k.sb("Kh", [128, NT], BF16) for _ in range(2)]
        Vh = [k.sb("Vh", [128, 18, 128], BF16) for _ in range(2)]
        PT = [k.sb("PT", [128, 512], BF16) for _ in range(3)]
        psS = [k.ps("psS", [128, 512]) for _ in range(2)]
        psOc = [k.ps("psOc", [128, 512]) for _ in range(2)]
        psZ = [k.ps("psZ", [128, 512]) for _ in range(2)]
        psM = k.ps("psM", [128, 512])
        rz = [k.sb("rz", [128, 512]) for _ in range(2)]
        tO = [k.sb("tO", [128, 512]) for _ in range(2)]
        osb = k.sb("osb", [128, 512])
        sqb = k.sb("sqb", [128, 512])
        rinv = k.sb("rinv", [128, 512])
        ns = 0
        for h in range(8):
            q_, k_, v_ = Qh[h % 2], Kh[h % 2], Vh[h % 2]
            k.dma("sp", q_[:], G.QT_d[b, h, :, :], reads=[G.QT_d], writes=[q_])
            k.dma("sp", k_[:], G.KT_d[b, h, :, :], reads=[G.KT_d], writes=[k_])
            k.dma("sp", v_[:], G.V_d[b, :, h * 128:(h + 1) * 128].rearrange("(t p) e -> p t e", p=128),
                  reads=[G.V_d], writes=[v_])
            for (t0, N, nkt) in qblocks(need_ctx):
                for c in range(2):
                    def smm(kt_, n_):
                        k.mm(psS[n_ % 2][:, 0:N], k_[c * 64:(c + 1) * 64, kt_ * 128:(kt_ + 1) * 128], q_[c * 64:(c + 1) * 64, t0:t0 + N],
                             True, True, [k_, q_], [psS[n_ % 2]])
                    smm(0, ns)
                    for kt in range(nkt):
                        pS = psS[ns % 2]
                        pt = PT[ns % 3]
                        ns += 1
                        if kt + 1 < nkt:
                            smm(kt + 1, ns)
                        k.actf(pt[:, 0:N], pS[:, 0:N], AF.Exp, [pS], [pt])
                        k.mm(psOc[c][:, 0:N], v_[:, kt, :], pt[:, 0:N], kt == 0, kt == nkt - 1, [v_, pt], [psOc[c]])
                        k.mm(psZ[c][:, 0:N], ones_b[:], pt[:, 0:N], kt == 0, kt == nkt - 1, [ones_b, pt], [psZ[c]])
                    k.op("dve", lambda e: e.reciprocal(rz[c][:, 0:N], psZ[c][:, 0:N]), [psZ[c]], [rz[c]])
                    k.tt("dve", tO[c][:, 0:N], psOc[c][:, 0:N], rz[c][:, 0:N], ALU.mult, [psOc[c], rz[c]], [tO[c]])
                k.stt(osb[:, 0:N], tO[1][:, 0:N], neglam[:, 0:1], tO[0][:, 0:N], ALU.mult, ALU.add, [tO[0], tO[1], neglam], [osb])
                k.actf(sqb[:, 0:N], osb[:, 0:N], AF.Square, [osb], [sqb])
                k.mm(psM[:, 0:N], ones128[:], sqb[:, 0:N], True, True, [ones128, sqb], [psM])
                k.actf(rinv[:, 0:N], psM[:, 0:N], AF.Sqrt, [psM], [rinv], bias=G.epsrms[:, 0:1])
                k.op("dve", lambda e: e.reciprocal(rinv[:, 0:N], rinv[:, 0:N]), [rinv], [rinv])
                k.tt("dve", osb[:, 0:N], osb[:, 0:N], rinv[:, 0:N], ALU.mult, [osb, rinv], [osb])
                k.ts("dve", OnT[:, h, t0:t0 + N], osb[:, 0:N], subg[:, 0:1], None, ALU.mult, reads=[osb, subg], writes=[OnT])
        k.pop()
        k.push()
        wo = k.sb("wo", [128, 8, 1024], BF16)
        k.dma("sp", wo[:], G.WoutB.h.rearrange("(k p) n -> p k n", p=128), reads=[G.WoutB], writes=[wo])
        hT = k.sb("hT", [128, 8, 256])
        rr = k.sb("rr", [128, 8, 256])
        tmp = dict(sq=k.sb("sq", [128, 8, 256]), mean=k.sb("mean", [128, 256]), rstd=k.sb("rstd", [128, 256]))
        psO = [k.ps("psO", [128, 512]) for _ in range(4)]
        psL = k.ps("psL", [128, 512])
        for blk in range(NT // 256):
            if blk == 0 and not need_ctx:
                continue
            row = NB if blk == 0 else b
            t0 = blk * 256
            k.dma("sp", hT[:], Hin[b, :, :, t0:t0 + 256], reads=[Hin], writes=[hT])
            for dc in range(8):
                po = psO[dc // 2][:, (dc % 2) * 256:(dc % 2 + 1) * 256]
                for h in range(8):
                    k.mm(po, wo[:, h, dc * 128:(dc + 1) * 128], OnT[:, h, t0:t0 + 256], h == 0, h == 7, [wo, OnT], [psO[dc // 2]])
            k.actf(hT[:].rearrange("p c t -> p (c t)"), hT[:].rearrange("p c t -> p (c t)"), AF.Copy, [hT], [hT], scale=ALPHA)
            for c in range(8):
                k.stt(rr[:, c, :], psO[c // 2][:, (c % 2) * 256:(c % 2 + 1) * 256], G.modT[:, li, seg(2, c), row:row + 1],
                      hT[:, c, :], ALU.mult, ALU.add, [psO[c // 2], G.modT, hT], [rr])
            layer_norm_fm(k, G, rr, 256, li, 0, rr, psL, tmp)
            k.dma("act", Hout[b, :, :, t0:t0 + 256], rr[:], reads=[rr], writes=[Hout])
        k.pop()
    k.pop()


def host_weights_s5(inp):
    w = {}

    def t_gp(a):
        return a.reshape(2, 32, 2, 64).transpose(2, 3, 0, 1).reshape(128, 2, 32)
    lr = t_gp(inp["s5_lam_re"][0])
    li_ = t_gp(inp["s5_lam_im"][0])
    ld = inp["s5_log_dt"][0].reshape(2, 32, 2).transpose(2, 0, 1)
    ld = np.broadcast_to(ld[:, None], (2, 64, 2, 32)).reshape(128, 2, 32)
    w["s5p"] = np.ascontiguousarray(np.stack([lr, li_, ld], axis=-1)).astype(np.float32)

    def t_b(a):
        return a.reshape(2, 32, 2, 64, 16).transpose(2, 3, 0, 1, 4).reshape(128, 2, 32, 16)

    def t_c(a):
        return a.reshape(2, 32, 2, 16, 64).transpose(2, 4, 0, 1, 3).reshape(128, 2, 32, 16)
    w["s5bc"] = np.ascontiguousarray(np.stack([t_b(inp["s5_b_re"][0]), t_b(inp["s5_b_im"][0]),
                                               t_c(inp["s5_c_re"][0]), t_c(inp["s5_c_im"][0])], axis=3)).astype(np.float32)
    w["s5_d"] = np.ascontiguousarray(inp["s5_d"][0].reshape(8, 128).T)
    w["s5_w_glu"] = np.ascontiguousarray(inp["s5_w_glu"][0])
    w["tT"] = np.tile(np.arange(NT, dtype=np.float32)[None, :], (128, 1))
    return w


TWO_PI = 6.283185307179586
CW1 = 6.28125
CW2 = TWO_PI - CW1
MAGIC = 12582912.0


def range_reduce(k, out, ang, nt, reads, e="dve"):
    k.ts(e, nt, ang, 1.0 / TWO_PI, MAGIC, ALU.mult, ALU.add, reads=reads[0], writes=reads[1])
    k.ts(e, nt, nt, MAGIC, None, ALU.subtract, reads=reads[1], writes=reads[1])
    k.stt(out, nt, -CW1, ang, ALU.mult, ALU.add, reads[0] + reads[1], reads[2])
    k.stt(out, nt, -CW2, out, ALU.mult, ALU.add, reads[1] + reads[2], reads[2])


def rev_tprime(lo, n):
    if lo < CTX:
        return (CTX - lo - n, CTX - lo)
    return (2 * CTX + LAT - lo - n, 2 * CTX + LAT - lo)


def s5_layer(k, G, li, Hin, Hout, need_ctx):
    NB = G.NB
    cast_bf16(k, G.WgluB, G.s5_w_glu.h, 1024, 2048)
    k.push()
    hb = [k.sb("hb", [128, 8, 256]) for _ in range(2)]
    ub = [k.sb("ub", [128, 8, 256], BF16) for _ in range(2)]
    n = 0
    for b in range(NB):
        for blk in range(NT // 256):
            row = NB if blk == 0 else b
            h_, u_ = hb[n % 2], ub[n % 2]
            n += 1
            k.dma("sp", h_[:], Hin[b, :, :, blk * 256:(blk + 1) * 256], reads=[Hin], writes=[h_])
            for c in range(8):
                k.actf(u_[:, c, :], h_[:, c, :], AF.Identity, [h_, G.modT, G.mod1], [u_],
                       scale=G.mod1[:, li, seg(1, c), row:row + 1], bias=G.modT[:, li, seg(0, c), row:row + 1])
            k.dma("act", G.UT_d[b, :, :, blk * 256:(blk + 1) * 256].rearrange("c p t -> p c t"), u_[:], reads=[u_], writes=[G.UT_d])
    k.pop()
    k.push()
    P3 = k.sb("P3", [128, 2, 32, 3])
    BC = k.sb("BC", [128, 2, 32, 4, 16])
    k.dma("sp", P3[:], G.s5p[:, :, :, :], reads=[G.s5p], writes=[P3])
    k.dma("sp", BC[:], G.s5bc[:, :, :, :, :], reads=[G.s5bc], writes=[BC])
    sd = k.sb("sd", [128, 8])
    k.dma("sp", sd[:], G.s5_d[:, :], reads=[G.s5_d], writes=[sd])
    tT = k.sb("tT", [128, NT])
    k.dma("sp", tT[:], G.tT_d[:, :], reads=[G.tT_d], writes=[tT])
    halfpi = k.sb("halfpi", [128, 1])
    k.memset("dve", halfpi[:], TWO_PI / 4, writes=[halfpi])
    sh = [128, 2, 32]
    nm = ["dt", "mag", "th", "nt", "thp", "sn", "cs", "are", "aim", "nr", "den", "kre", "kim", "t1", "t2"]
    V = {x: k.sb("p_" + x, sh) for x in nm}
    lr, lim, ld = P3[:, :, :, 0], P3[:, :, :, 1], P3[:, :, :, 2]
    k.actf(V["dt"][:], ld, AF.Exp, [P3], [V["dt"]])
    k.tt("dve", V["t1"][:], lr, V["dt"][:], ALU.mult, [P3, V["dt"]], [V["t1"]])
    k.actf(V["mag"][:], V["t1"][:], AF.Exp, [V["t1"]], [V["mag"]])
    k.tt("dve", V["th"][:], lim, V["dt"][:], ALU.mult, [P3, V["dt"]], [V["th"]])
    range_reduce(k, V["thp"][:], V["th"][:], V["nt"][:], ([V["th"]], [V["nt"]], [V["thp"]]))
    k.actf(V["sn"][:], V["thp"][:], AF.Sin, [V["thp"]], [V["sn"]])
    k.actf(V["t2"][:], V["thp"][:], AF.Abs, [V["thp"]], [V["t2"]])
    k.actf(V["cs"][:], V["t2"][:], AF.Sin, [V["t2"], halfpi], [V["cs"]], scale=-1.0, bias=halfpi[:, 0:1])
    k.tt("dve", V["are"][:], V["mag"][:], V["cs"][:], ALU.mult, [V["mag"], V["cs"]], [V["are"]])
    k.tt("dve", V["aim"][:], V["mag"][:], V["sn"][:], ALU.mult, [V["mag"], V["sn"]], [V["aim"]])
    k.ts("dve", V["nr"][:], V["are"][:], -1.0, None, ALU.add, reads=[V["are"]], writes=[V["nr"]])
    k.tt("dve", V["den"][:], lr, lr, ALU.mult, [P3], [V["den"]])
    k.tt("dve", V["t1"][:], lim, lim, ALU.mult, [P3], [V["t1"]])
    k.tt("dve", V["den"][:], V["den"][:], V["t1"][:], ALU.add, [V["den"], V["t1"]], [V["den"]])
    k.op("dve", lambda e: e.reciprocal(V["den"][:], V["den"][:]), [V["den"]], [V["den"]])
    k.tt("dve", V["t1"][:], V["nr"][:], lr, ALU.mult, [V["nr"], P3], [V["t1"]])
    k.tt("dve", V["t2"][:], V["aim"][:], lim, ALU.mult, [V["aim"], P3], [V["t2"]])
    k.tt("dve", V["kre"][:], V["t1"][:], V["t2"][:], ALU.add, [V["t1"], V["t2"]], [V["kre"]])
    k.tt("dve", V["kre"][:], V["kre"][:], V["den"][:], ALU.mult, [V["kre"], V["den"]], [V["kre"]])
    k.tt("dve", V["t1"][:], V["aim"][:], lr, ALU.mult, [V["aim"], P3], [V["t1"]])
    k.tt("dve", V["t2"][:], V["nr"][:], lim, ALU.mult, [V["nr"], P3], [V["t2"]])
    k.tt("dve", V["kim"][:], V["t1"][:], V["t2"][:], ALU.subtract, [V["t1"], V["t2"]], [V["kim"]])
    k.tt("dve", V["kim"][:], V["kim"][:], V["den"][:], ALU.mult, [V["kim"], V["den"]], [V["kim"]])
    sh4 = [128, 2, 32, 16]
    Bre = k.sb("Bre", sh4)
    Bim = k.sb("Bim", sh4)
    t4a = k.sb("t4a", sh4)
    nCim = k.sb("nCim", sh4)
    kreb = V["kre"][:].unsqueeze(3).to_broadcast(sh4)
    kimb = V["kim"][:].unsqueeze(3).to_broadcast(sh4)
    bre, bim, cre, cim = BC[:, :, :, 0, :], BC[:, :, :, 1, :], BC[:, :, :, 2, :], BC[:, :, :, 3, :]
    k.tt("dve", Bre[:], bre, kreb, ALU.mult, [BC, V["kre"]], [Bre])
    k.tt("dve", t4a[:], bim, kimb, ALU.mult, [BC, V["kim"]], [t4a])
    k.tt("dve", Bre[:], Bre[:], t4a[:], ALU.subtract, [Bre, t4a], [Bre])
    k.tt("dve", Bim[:], bim, kreb, ALU.mult, [BC, V["kre"]], [Bim])
    k.tt("dve", t4a[:], bre, kimb, ALU.mult, [BC, V["kim"]], [t4a])
    k.tt("dve", Bim[:], Bim[:], t4a[:], ALU.add, [Bim, t4a], [Bim])
    k.ts("dve", nCim[:], cim, -1.0, None, ALU.mult, reads=[BC], writes=[nCim])
    BD = [k.sb("BD", [128, 128]) for _ in range(2)]
    WB = [k.sb("WB", [128, 128], BF16) for _ in range(2)]
    WC = [k.sb("WC", [128, 128], BF16) for _ in range(2)]
    cosT = k.sb("cosT", [128, NT])
    sinT = k.sb("sinT", [128, NT])
    ntT = k.sb("ntT", [128, NT])
    rT = k.sb("rT", [128, 512])
    uch = [k.sb("uch", [128, NT], BF16) for _ in range(NB)]
    yacc = [k.sb("yacc", [128, NT]) for _ in range(NB)]
    zout = k.sb("zout", [128, NT], BF16)
    xb = [[k.sb("xre", [128, 512], BF16), k.sb("xim", [128, 512], BF16)] for _ in range(NB)]
    Wab = [{x: k.sb("w5" + x, [128, 512]) for x in ("a", "b")} for _ in range(2)]
    Wst = [{x: k.sb("w5" + x, [128, 512]) for x in ("cr", "ci", "zr", "zi")} for _ in range(NB)]
    stt_ = [{x: k.sb("st" + x, [128, 1]) for x in ("zr", "zi")} for _ in range(NB)]
    psB = [[k.ps("psBr", [128, 512]), k.ps("psBi", [128, 512])] for _ in range(2)]
    psY = [k.ps("psY", [128, 512]) for _ in range(2)]
    blocks = [(0, 256)] + [(CTX + i * 512, 512) for i in range(4)]
    nblk = 0
    for c in range(8):
        for b in range(NB):
            k.dma("sp", uch[b][:], G.UT_d[b, c, :, :], reads=[G.UT_d], writes=[uch[b]])
            k.memset("pool", yacc[b][:], 0.0, writes=[yacc[b]])
        for s_ in range(4):
            unit = 4 * c + s_
            for dr in range(2):
                for ri, Bsrc in ((0, Bre), (1, Bim)):
                    bd = BD[ri]
                    k.memset("pool", bd[:], 0.0, writes=[bd])
                    k.copy("pool", bd[0:64, 32 * s_:32 * s_ + 16], Bsrc[0:64, dr, unit, :], [Bsrc], [bd])
                    k.copy("pool", bd[64:128, 32 * s_ + 16:32 * s_ + 32], Bsrc[64:128, dr, unit, :], [Bsrc], [bd])
                    pt = psY[ri]
                    k.tr(pt[:, 0:128], bd[:], G.ident[:], [bd, G.ident], [pt])
                    k.copy("act", WB[ri][:], pt[:, 0:128], [pt], [WB[ri]])
                for ri, Csrc, Cap in ((0, BC, cre), (1, nCim, nCim[:])):
                    wc = WC[ri]
                    k.memset("pool", wc[:], 0.0, writes=[wc])
                    k.copy("pool", wc[0:64, 32 * s_:32 * s_ + 16], Cap[0:64, dr, unit, :], [Csrc], [wc])
                    k.copy("pool", wc[64:128, 32 * s_ + 16:32 * s_ + 32], Cap[64:128, dr, unit, :], [Csrc], [wc])
                k.ts("dve", cosT[:], tT[:], V["thp"][:, dr, unit:unit + 1], None, ALU.mult, reads=[tT, V["thp"]], writes=[cosT])
                range_reduce(k, sinT[:], cosT[:], ntT[:], ([cosT], [ntT], [sinT]))
                k.actf(ntT[:], sinT[:], AF.Abs, [sinT], [ntT])
                k.actf(cosT[:], ntT[:], AF.Sin, [ntT, halfpi], [cosT], scale=-1.0, bias=halfpi[:, 0:1])
                k.actf(sinT[:], sinT[:], AF.Sin, [sinT], [sinT])
                k.ts("dve", rT[:], tT[:, 0:512], 0.0, V["mag"][:, dr, unit:unit + 1], ALU.mult, ALU.add, reads=[tT, V["mag"]], writes=[rT])
                for bi_, (lo, N) in enumerate(blocks):
                    for b in range(NB):
                        w = dict(Wst[b])
                        w.update(Wab[nblk % 2])
                        pB = psB[nblk % 2]
                        pY = psY[nblk % 2]
                        xre, xim = xb[b]
                        nblk += 1
                        if dr == 0:
                            a0, a1 = lo, lo + N
                            usl = uch[b][:, a0:a1]
                            ysl = yacc[b][:, a0:a1]
                        else:
                            a0, a1 = rev_tprime(lo, N)
                            usl = uch[b][:, a0:a1][:, ::-1]
                            ysl = yacc[b][:, a0:a1][:, ::-1]
                        k.mm(pB[0][:, 0:N], WB[0][:], usl, True, True, [WB[0], uch[b]], [pB[0]])
                        k.mm(pB[1][:, 0:N], WB[1][:], usl, True, True, [WB[1], uch[b]], [pB[1]])
                        cs_, sn_ = cosT[:, lo:lo + N], sinT[:, lo:lo + N]
                        k.tt("dve", w["a"][:, 0:N], pB[0][:, 0:N], cs_, ALU.mult, [pB[0], cosT], [w["a"]])
                        k.tt("dve", w["b"][:, 0:N], pB[1][:, 0:N], sn_, ALU.mult, [pB[1], sinT], [w["b"]])
                        k.tt("pool", w["cr"][:, 0:N], w["a"][:, 0:N], w["b"][:, 0:N], ALU.add, [w["a"], w["b"]], [w["cr"]])
                        k.tt("dve", w["a"][:, 0:N], pB[1][:, 0:N], cs_, ALU.mult, [pB[1], cosT], [w["a"]])
                        k.tt("dve", w["b"][:, 0:N], pB[0][:, 0:N], sn_, ALU.mult, [pB[0], sinT], [w["b"]])
                        k.tt("pool", w["ci"][:, 0:N], w["a"][:, 0:N], w["b"][:, 0:N], ALU.subtract, [w["a"], w["b"]], [w["ci"]])
                        for src, dst in (("cr", "zr"), ("ci", "zi")):
                            st = stt_[b][dst]
                            init = 0.0 if bi_ == 0 else st[:, 0:1]
                            rd = [rT, w[src]] + ([] if bi_ == 0 else [st])
                            k.op("dve", lambda e: e.tensor_tensor_scan(w[dst][:, 0:N], rT[:, 0:N], w[src][:, 0:N], init,
                                                                         ALU.mult, ALU.add), rd, [w[dst]])
                            k.copy("act", st[:, 0:1], w[dst][:, N - 1:N], [w[dst]], [st])
                        k.tt("dve", w["a"][:, 0:N], w["zr"][:, 0:N], cs_, ALU.mult, [w["zr"], cosT], [w["a"]])
                        k.tt("dve", w["b"][:, 0:N], w["zi"][:, 0:N], sn_, ALU.mult, [w["zi"], sinT], [w["b"]])
                        k.tt("pool", xre[:, 0:N], w["a"][:, 0:N], w["b"][:, 0:N], ALU.subtract, [w["a"], w["b"]], [xre])
                        k.tt("dve", w["a"][:, 0:N], w["zr"][:, 0:N], sn_, ALU.mult, [w["zr"], sinT], [w["a"]])
                        k.tt("dve", w["b"][:, 0:N], w["zi"][:, 0:N], cs_, ALU.mult, [w["zi"], cosT], [w["b"]])
                        k.tt("pool", xim[:, 0:N], w["a"][:, 0:N], w["b"][:, 0:N], ALU.add, [w["a"], w["b"]], [xim])
                        k.mm(pY[:, 0:N], WC[0][:], xre[:, 0:N], True, False, [WC[0], xre], [pY])
                        k.mm(pY[:, 0:N], WC[1][:], xim[:, 0:N], False, True, [WC[1], xim], [pY])
                        k.tt("dve", ysl, pY[:, 0:N], ysl, ALU.add, [pY, yacc[b]], [yacc[b]])
        for b in range(NB):
            k.stt(yacc[b][:], uch[b][:], sd[:, c:c + 1], yacc[b][:], ALU.mult, ALU.add, [uch[b], sd, yacc[b]], [yacc[b]])
            k.actf(zout[:], yacc[b][:], AF.Gelu_apprx_tanh, [yacc[b]], [zout])
            k.dma("act", G.ZT_d[b, c, :, :], zout[:], reads=[zout], writes=[G.ZT_d])
    k.pop()
    k.push()
    wg = k.sb("wg", [128, 8, 2048], BF16)
    k.dma("sp", wg[:], G.WgluB.h.rearrange("(k p) n -> p k n", p=128), reads=[G.WgluB], writes=[wg])
    zT = [k.sb("zT", [128, 8, 256], BF16) for _ in range(2)]
    hT = k.sb("hT", [128, 8, 256])
    rr = k.sb("rr", [128, 8, 256])
    sg = [k.sb("sg", [128, 256]) for _ in range(2)]
    oc = [k.sb("oc", [128, 256]) for _ in range(2)]
    tmp = dict(sq=k.sb("sq", [128, 8, 256]), mean=k.sb("mean", [128, 256]), rstd=k.sb("rstd", [128, 256]))
    psV = [k.ps("psV", [128, 512]) for _ in range(2)]
    psL = k.ps("psL", [128, 512])
    n = 0
    for b, blk, t0, row in blocks_of(G, need_ctx):
        z_ = zT[n % 2]
        n += 1
        k.dma("sp", z_[:], G.ZT_d[b, :, :, t0:t0 + 256].rearrange("c p t -> p c t"), reads=[G.ZT_d], writes=[z_])
        k.dma("sp", hT[:], Hin[b, :, :, t0:t0 + 256], reads=[Hin], writes=[hT])
        k.actf(hT[:].rearrange("p c t -> p (c t)"), hT[:].rearrange("p c t -> p (c t)"), AF.Copy, [hT], [hT], scale=ALPHA)
        for dc in range(8):
            pv = psV[dc % 2]
            for kc in range(8):
                k.mm(pv[:, 0:256], wg[:, kc, dc * 128:(dc + 1) * 128], z_[:, kc, :], kc == 0, kc == 7, [wg, z_], [pv])
            for kc in range(8):
                k.mm(pv[:, 256:512], wg[:, kc, 1024 + dc * 128:1024 + (dc + 1) * 128], z_[:, kc, :], kc == 0, kc == 7, [wg, z_], [pv])
            s_, o_ = sg[dc % 2], oc[dc % 2]
            k.actf(s_[:], pv[:, 256:512], AF.Sigmoid, [pv], [s_])
            k.tt("dve", o_[:], pv[:, 0:256], s_[:], ALU.mult, [pv, s_], [o_])
            k.stt(rr[:, dc, :], o_[:], G.modT[:, li, seg(2, dc), row:row + 1], hT[:, dc, :], ALU.mult, ALU.add,
                  [o_, G.modT, hT], [rr])
        layer_norm_fm(k, G, rr, 256, li, 0, rr, psL, tmp)
        k.dma("act", Hout[b, :, :, t0:t0 + 256], rr[:], reads=[rr], writes=[Hout])
    k.pop()


def host_weights_hg(inp):
    w = {}
    w["hg_w_in"] = np.ascontiguousarray(inp["hg_w_in"][0])
    w["hg_w_out"] = np.ascontiguousarray(inp["hg_w_out"][0])
    w["hg_norm"] = np.ascontiguousarray(inp["hg_norm"][0].reshape(8, 128).T)
    w["hg_lb"] = np.ascontiguousarray(inp["hg_lb"].reshape(4, 8, 128).transpose(2, 1, 0))
    m = np.zeros((128, 64 + NT), np.float32)
    s_ = (np.arange(128) % 64)[:, None]
    t_ = np.arange(64)[None, :]
    m[:, 0:64] = (s_ <= t_).astype(np.float32)
    m[:, 64:] = (np.arange(NT) % 64 != 0).astype(np.float32)[None, :]
    w["hgmask"] = m.astype(ml_dtypes.bfloat16)
    return w


def out_proj_ln(k, G, li, b, Hin, Hout, OnT, WB_, need_ctx):
    NB = G.NB
    k.push()
    wo = k.sb("wo", [128, 8, 1024], BF16)
    k.dma("sp", wo[:], WB_.h.rearrange("(k p) n -> p k n", p=128), reads=[WB_], writes=[wo])
    hT = k.sb("hT", [128, 8, 256])
    rr = k.sb("rr", [128, 8, 256])
    tmp = dict(sq=k.sb("sq", [128, 8, 256]), mean=k.sb("mean", [128, 256]), rstd=k.sb("rstd", [128, 256]))
    psO = [k.ps("psO", [128, 512]) for _ in range(4)]
    psL = k.ps("psL", [128, 512])
    for blk in range(NT // 256):
        if blk == 0 and not need_ctx:
            continue
        row = NB if blk == 0 else b
        t0 = blk * 256
        k.dma("sp", hT[:], Hin[b, :, :, t0:t0 + 256], reads=[Hin], writes=[hT])
        for dc in range(8):
            po = psO[dc // 2][:, (dc % 2) * 256:(dc % 2 + 1) * 256]
            for h in range(8):
                k.mm(po, wo[:, h, dc * 128:(dc + 1) * 128], OnT[:, h, t0:t0 + 256], h == 0, h == 7, [wo, OnT], [psO[dc // 2]])
        k.actf(hT[:].rearrange("p c t -> p (c t)"), hT[:].rearrange("p c t -> p (c t)"), AF.Copy, [hT], [hT], scale=ALPHA)
        for c in range(8):
            k.stt(rr[:, c, :], psO[c // 2][:, (c % 2) * 256:(c % 2 + 1) * 256], G.modT[:, li, seg(2, c), row:row + 1],
                  hT[:, c, :], ALU.mult, ALU.add, [psO[c // 2], G.modT, hT], [rr])
        layer_norm_fm(k, G, rr, 256, li, 0, rr, psL, tmp)
        k.dma("act", Hout[b, :, :, t0:t0 + 256], rr[:], reads=[rr], writes=[Hout])
    k.pop()


def tp_blocks(dr, step=512):
    out = []
    los = [(0, 256)] if step >= 256 else [(i, step) for i in range(0, 256, step)]
    los = los + [(CTX + i, step) for i in range(0, LAT, step)]
    for lo, N in los:
        if dr == 0:
            out.append((lo, N, lo, lo + N))
        else:
            a0, a1 = rev_tprime(lo, N)
            out.append((lo, N, a0, a1))
    return out


def hgrn2_layer(k, G, li, Hin, Hout, need_ctx):
    NB = G.NB
    CH = 64
    NCH = NT // CH
    cast_bf16(k, G.WhgB, G.hg_w_in.h, 1024, 5120)
    cast_bf16(k, G.WhgoB, G.hg_w_out.h, 1024, 1024)
    k.push()
    lbe = k.sb("lbe", [128, 8, 4])
    lbs = k.sb("lbs", [128, 8])
    lb = k.sb("lb", [128, 8])
    oml = k.sb("oml", [128, 8])
    ng = k.sb("ng", [128, 8])
    k.dma("sp", lbe[:], G.hg_lb[:, :, :], reads=[G.hg_lb], writes=[lbe])
    k.dma("sp", ng[:], G.hg_norm[:, :], reads=[G.hg_norm], writes=[ng])
    k.actf(lbe[:], lbe[:], AF.Exp, [lbe], [lbe])
    k.op("dve", lambda e: e.tensor_reduce(lbs[:], lbe[:], AX.X, ALU.add), [lbe], [lbs])
    k.op("dve", lambda e: e.reciprocal(lbs[:], lbs[:]), [lbs], [lbs])
    k.op("dve", lambda e: e.tensor_reduce(lb[:], lbe[:, :, 1:li + 1], AX.X, ALU.add), [lbe], [lb])
    k.tt("dve", lb[:], lb[:], lbs[:], ALU.mult, [lb, lbs], [lb])
    k.ts("dve", oml[:], lb[:], -1.0, 1.0, ALU.mult, ALU.add, reads=[lb], writes=[oml])
    msk = k.sb("msk", [128, 64 + NT], BF16)
    k.dma("sp", msk[:], G.hgmask_d[:, :], reads=[G.hgmask_d], writes=[msk])
    ones128 = k.sb("ones128", [128, 128])
    k.memset("dve", ones128[:], 1.0 / 128, writes=[ones128])
    identb = k.sb("identb", [128, 128], BF16)
    k.copy("dve", identb[:], G.ident[:], [G.ident], [identb])
    Jb = k.sb("Jb", [128, 128], BF16)
    k.copy("dve", Jb[:], G.ident[:, ::-1], [G.ident], [Jb])
    WV = G.WhgB.h.rearrange("(k p) n -> p k n", p=128)
    for b in range(NB):
        k.push()
        OgT = k.sb("OgT", [128, 8, NT], BF16)
        k.push()
        uTb = k.sb("uTb", [128, 8, NT], BF16)
        hblk = [k.sb("hblk", [128, 8, 256]) for _ in range(2)]
        for blk in range(NT // 256):
            hb = hblk[blk % 2]
            row = NB if blk == 0 else b
            k.dma("sp", hb[:], Hin[b, :, :, blk * 256:(blk + 1) * 256], reads=[Hin], writes=[hb])
            for c in range(8):
                k.actf(uTb[:, c, blk * 256:(blk + 1) * 256], hb[:, c, :], AF.Identity, [hb, G.modT, G.mod1], [uTb],
                       scale=G.mod1[:, li, seg(1, c), row:row + 1], bias=G.modT[:, li, seg(0, c), row:row + 1])
        wch = [k.sb("wch", [128, 8, 128], BF16) for _ in range(2)]
        orow = [k.sb("orow", [128, NT]) for _ in range(2)]
        psa = [k.ps("psa", [128, 512]) for _ in range(2)]
        psv = [k.ps("psv", [128, 512]) for _ in range(2)]
        n = 0
        nw = 0
        for kind, cbase in ((0, 0), (1, 1024), (2, 2048), (3, 4096)):
            for h in range(8):
                w_ = wch[nw % 2]
                orw = orow[nw % 2]
                nw += 1
                k.dma("sp", w_[:], WV[:, :, cbase + h * 128:cbase + (h + 1) * 128], reads=[G.WhgB], writes=[w_])
                for (t0, N) in [(0, 256)] + [(CTX + i * 512, 512) for i in range(4)]:
                    pa = psa[n % 2]
                    n += 1
                    for kc in range(8):
                        k.mm(pa[:, 0:N], w_[:, kc, :], uTb[:, kc, t0:t0 + N], kc == 0, kc == 7, [w_, uTb], [pa])
                    k.copy("act" if n % 2 == 0 else "dve", orw[:, t0:t0 + N], pa[:, 0:N], [pa], [orw])
                k.dma("act", G.PQ_d[b, kind, h, :, :], orw[:], reads=[orw], writes=[G.PQ_d])
        wv = k.sb("wv", [128, 8, 1024], BF16)
        k.dma("sp", wv[:], WV[:, :, 3072:4096], reads=[G.WhgB], writes=[wv])
        vrow = [k.sb("vrow", [128, 1024], BF16) for _ in range(2)]
        nv = 0
        for dr in range(1):
            for (lo, N, a0, a1) in tp_blocks(dr, 128):
                vr = vrow[nv % 2]
                nv += 1
                for cb in range(2):
                    pv = psv[cb]
                    for kc in range(8):
                        lhs = uTb[:, kc, a0:a1] if dr == 0 else uTb[:, kc, a0:a1][:, ::-1]
                        k.mm(pv[:], lhs, wv[:, kc, cb * 512:(cb + 1) * 512], kc == 0, kc == 7, [uTb, wv], [pv])
                    k.copy("act" if cb == 0 else "dve", vr[:, cb * 512:(cb + 1) * 512], pv[:], [pv], [vr])
                dst = G.V_d if dr == 0 else G.Vr_d
                k.dma("act", dst[b, lo:lo + 128, :], vr[:], reads=[vr], writes=[dst])
        k.pop()
        k.push()
        qs = k.sb("qs", [128, NT])
        raw = k.sb("raw", [128, NT])
        tA = k.sb("tA", [128, NT])
        tB = k.sb("tB", [128, NT])
        tC = k.sb("tC", [128, NT])
        oacc = k.sb("oacc", [128, NT])
        qtb = [k.sb("qtb", [128, NT], BF16) for _ in range(2)]
        ktb = [k.sb("ktb", [128, NT], BF16) for _ in range(2)]
        khb = [k.sb("khb", [128, NT], BF16) for _ in range(2)]
        ecl = [k.sb("ecl", [128, NCH]) for _ in range(2)]
        khtok = [k.sb("khtok", [128, NT // 128, 128], BF16) for _ in range(2)]
        Vh = [k.sb("Vh", [128, NT // 128, 128], BF16) for _ in range(2)]
        S = [k.sb("S", [128, 128]) for _ in range(2)]
        Sb = [k.sb("Sb", [128, 128], BF16) for _ in range(2)]
        Am = [k.sb("Am", [128, 64], BF16) for _ in range(4)]
        psT = k.ps("psT", [128, 512], BF16)
        psAd = [k.ps("psA", [128, 512]) for _ in range(2)]
        psOo = [k.ps("psOo", [128, 512]) for _ in range(2)]
        psSd = [k.ps("psS", [128, 512]) for _ in range(2)]
        psM = k.ps("psM", [128, 512])
        sqb = k.sb("sqb", [128, 512])
        rinv = k.sb("rinv", [128, 512])
        psA_r = psAd
        psS_r = psSd
        for h in range(8):
            k.dma("sp", raw[:], G.PQ_d[b, 0, h, :, :], reads=[G.PQ_d], writes=[raw])
            k.actf(qs[:], raw[:], AF.Silu, [raw], [qs])
            k.memset("pool", oacc[:], 0.0, writes=[oacc])
            for dr in range(2):
                if dr == 0:
                    k.dma("sp", Vh[dr][:], G.V_d[b, :, h * 128:(h + 1) * 128].rearrange("(t p) e -> p t e", p=128),
                          reads=[G.V_d], writes=[Vh[dr]])
                else:
                    for tg in range(0, 18, 4):
                        nn = min(4, 18 - tg)
                        for j in range(nn):
                            tau = tg + j
                            src = 1 - tau if tau < 2 else 19 - tau
                            k.mm(psM[:, j * 128:(j + 1) * 128], Jb[:], Vh[0][:, src, :], True, True, [Jb, Vh[0]], [psM])
                        k.copy("act", Vh[1][:, tg:tg + nn, :].rearrange("p a b -> p (a b)"), psM[:, 0:nn * 128], [psM], [Vh[1]])
                k.dma("sp", raw[:], G.PQ_d[b, 1 + dr, h, :, :], reads=[G.PQ_d], writes=[raw])
                if dr == 0:
                    k.actf(tA[:], raw[:], AF.Sigmoid, [raw], [tA])
                else:
                    k.actf(tA[:, 0:CTX], raw[:, 0:CTX][:, ::-1], AF.Sigmoid, [raw], [tA])
                    k.actf(tA[:, CTX:NT], raw[:, CTX:NT][:, ::-1], AF.Sigmoid, [raw], [tA])
                k.ts("dve", tA[:], tA[:], oml[:, h:h + 1], lb[:, h:h + 1], ALU.mult, ALU.add, reads=[tA, oml, lb], writes=[tA])
                k.actf(tB[:], tA[:], AF.Ln, [tA], [tB])
                k.ts("dve", tA[:], tA[:], -1.0, 1.0, ALU.mult, ALU.add, reads=[tA], writes=[tA])
                k.op("dve", lambda e: e.tensor_tensor_scan(tC[:], msk[:, 64:64 + NT], tB[:], 0.0, ALU.mult, ALU.add), [msk, tB], [tC])
                k.actf(tB[:], tC[:], AF.Exp, [tC], [tB])
                k.actf(tC[:], tC[:], AF.Exp, [tC], [tC], scale=-1.0)
                if dr == 0:
                    k.tt("dve", qtb[dr][:], qs[:], tB[:], ALU.mult, [qs, tB], [qtb[dr]])
                else:
                    k.tt("dve", qtb[dr][:, 0:CTX], qs[:, 0:CTX][:, ::-1], tB[:, 0:CTX], ALU.mult, [qs, tB], [qtb[dr]])
                    k.tt("dve", qtb[dr][:, CTX:NT], qs[:, CTX:NT][:, ::-1], tB[:, CTX:NT], ALU.mult, [qs, tB], [qtb[dr]])
                k.copy("act", ecl[dr][:], tB[:, CH - 1::CH], [tB], [ecl[dr]])
                k.tt("dve", tA[:], tA[:], tC[:], ALU.mult, [tA, tC], [tA])
                k.copy("pool", ktb[dr][:], tA[:], [tA], [ktb[dr]])
                k.tt("dve", khb[dr][:].rearrange("p (c t) -> p c t", t=CH), tA[:].rearrange("p (c t) -> p c t", t=CH),
                     ecl[dr][:].unsqueeze(2).to_broadcast([128, NCH, CH]), ALU.mult, [tA, ecl[dr]], [khb[dr]])
                for tl in range(NT // 128):
                    k.tr(psT[:, (tl % 4) * 128:(tl % 4 + 1) * 128], khb[dr][:, tl * 128:(tl + 1) * 128], identb[:], [khb[dr], identb], [psT])
                    if tl % 4 == 3 or tl == NT // 128 - 1:
                        t0_ = (tl // 4) * 4
                        nn = tl - t0_ + 1
                        k.copy("act", khtok[dr][:, t0_:tl + 1, :].rearrange("p a b -> p (a b)"), psT[:, 0:nn * 128], [psT], [khtok[dr]])
            for ci in range(NCH):
                par = ci % 2
                tl = ci // 2
                pr = slice(64 * par, 64 * par + 64)
                tsl = slice(ci * CH, (ci + 1) * CH)
                bank_pos = ci % 8
                for dr in range(2):
                    am = Am[(ci % 2) * 2 + dr]
                    pa = psAd[dr][pr, (ci % 8) * 64:(ci % 8 + 1) * 64]
                    k.mm(pa, ktb[dr][:, tsl], qtb[dr][:, tsl], True, True, [ktb[dr], qtb[dr]], [psA_r[dr]])
                    k.tt("dve", am[pr, :], pa, msk[pr, 0:64], ALU.mult, [psA_r[dr], msk], [am])
                    po = psOo[dr][:, bank_pos * 64:(bank_pos + 1) * 64]
                    k.mm(po, Vh[dr][pr, tl, :], am[pr, :], True, ci == 0, [Vh[dr], am], [psOo[dr]])
                    if ci > 0:
                        k.mm(po, Sb[dr][:], qtb[dr][:, tsl], False, True, [Sb[dr], qtb[dr]], [psOo[dr]])
                    ps_ = psSd[dr][:, (ci % 4) * 128:(ci % 4 + 1) * 128]
                    k.mm(ps_, khtok[dr][pr, tl, :], Vh[dr][pr, tl, :], True, True, [khtok[dr], Vh[dr]], [psS_r[dr]])
                    if ci == 0:
                        k.copy("dve", S[dr][:], ps_, [psS_r[dr]], [S[dr]])
                    else:
                        k.stt(S[dr][:], S[dr][:], ecl[dr][:, ci:ci + 1], ps_, ALU.mult, ALU.add, [S[dr], ecl[dr], psS_r[dr]], [S[dr]])
                    k.copy("act", Sb[dr][:], S[dr][:], [S[dr]], [Sb[dr]])
                    if bank_pos == 7 or (ci == 3):
                        pass
                done = None
                if ci == 3:
                    done = (0, 256)
                elif ci > 3 and (ci - 4) % 8 == 7:
                    done = ((ci - 7) * CH, 512)
                if done is not None:
                    lo, N = done
                    for dr in range(2):
                        c0 = ((lo // CH) % 8) * 64
                        if c0 + N <= 512:
                            srcs = [(psOo[dr][:, c0:c0 + N], lo, N)]
                        else:
                            n1 = 512 - c0
                            srcs = [(psOo[dr][:, c0:512], lo, n1), (psOo[dr][:, 0:N - n1], lo + n1, N - n1)]
                        for (sp_, l0, n_) in srcs:
                            if dr == 0:
                                dst = oacc[:, l0:l0 + n_]
                            else:
                                a0, a1 = rev_tprime(l0, n_)
                                dst = oacc[:, a0:a1][:, ::-1]
                            k.tt("dve", dst, sp_, dst, ALU.add, [psOo[dr], oacc], [oacc])
            k.dma("sp", raw[:], G.PQ_d[b, 3, h, :, :], reads=[G.PQ_d], writes=[raw])
            k.actf(tA[:], raw[:], AF.Silu, [raw], [tA])
            for (t0, N) in [(0, 256)] + [(CTX + i * 512, 512) for i in range(4)]:
                k.actf(sqb[:, 0:N], oacc[:, t0:t0 + N], AF.Square, [oacc], [sqb])
                k.mm(psM[:, 0:N], ones128[:], sqb[:, 0:N], True, True, [ones128, sqb], [psM])
                k.actf(rinv[:, 0:N], psM[:, 0:N], AF.Sqrt, [psM], [rinv], bias=G.epsrms[:, 0:1])
                k.op("dve", lambda e: e.reciprocal(rinv[:, 0:N], rinv[:, 0:N]), [rinv], [rinv])
                k.tt("dve", rinv[:, 0:N], rinv[:, 0:N], oacc[:, t0:t0 + N], ALU.mult, [rinv, oacc], [rinv])
                k.stt(OgT[:, h, t0:t0 + N], rinv[:, 0:N], ng[:, h:h + 1], tA[:, t0:t0 + N], ALU.mult, ALU.mult, [rinv, ng, tA], [OgT])
        k.pop()
        out_proj_ln(k, G, li, b, Hin, Hout, OgT, G.WhgoB, need_ctx)
        k.pop()
    k.pop()


def build_program(NB):
    nc = bass.Bass("TRN2", target_bir_lowering=False)
    k = K(nc)
    G = setup_globals(k, NB)
    phase_adaln(k, G)
    cur = G.H[0]
    for li in range(DEPTH):
        need_ctx = li < DEPTH - 1
        mid = G.H[1]
        nxt = G.H[2]
        kind = li % 3
        if kind == 0:
            attention(k, G, li, cur, mid, need_ctx)
        elif kind == 1:
            s5_layer(k, G, li, cur, mid, need_ctx)
        else:
            hgrn2_layer(k, G, li, cur, mid, need_ctx)
        peer_cast(k, G, li)
        peer_route(k, G, li, mid, need_ctx)
        peer_experts2(k, G, li, mid, nxt, need_ctx, out_final=(li == DEPTH - 1))
        cur = nxt
    k.finish()
    return nc, k


def kernel(**inp):
    inp = {k_: np.asarray(v) for k_, v in inp.items()}
    NCORES = 8
    B = inp["x"].shape[0]
    NB = B // NCORES
    nc, k = build_program(NB)
    w = host_weights(inp)
    w.update(host_weights_attn(inp))
    w.update(host_weights_s5(inp))
    w.update(host_weights_hg(inp))
    in_maps = []
    for c in range(NCORES):
        m = dict(w)
        m.update(host_core_inputs(inp, c * NB, NB))
        in_maps.append(m)
    res = run_bass_kernel_spmd(nc, in_maps, core_ids=list(range(NCORES)))
    outs = [unfm(r["outT"]) for r in res.results]
    return np.ascontiguousarray(np.concatenate(outs, axis=0)).astype(np.float32)


def peer_experts2(k, G, li, Hin, Hout, need_ctx, out_final=False):
    k.push()
    NBUF = 3
    NA = 3
    SB_ = 8
    hT = k.sb("hT", [128, 8, 256])
    uTb = k.sb("uTb", [128, 8, 256], BF16)
    rr = k.sb("rr", [128, 8, 256])
    tmp = dict(sq=hT, mean=k.sb("mean", [128, 256]), rstd=k.sb("rstd", [128, 256]))
    Rt1 = [k.sb("Rt", [128, 384]) for _ in range(2)]
    Rt = [Rt1, Rt1]
    RTt = [k.sb("RTt", [128, 3, 256]) for _ in range(2)]
    Q = k.sb("Q", [128, SB_, 128], BF16)
    Pg = k.sb("Pg", [128, SB_, 128], BF16)
    GT = [k.sb("GT", [128, 256, 128], BF16) for _ in range(2)]
    ut = [k.sb("ut", [128, 2, 8, 128], BF16) for _ in range(NBUF)]
    vt = [k.sb("vt", [128, 2, 1024], BF16) for _ in range(NBUF)]
    ga = [k.sb("ga", [128, 256], BF16) for _ in range(NA)]
    gam = [k.sb("gam", [128, 256], BF16) for _ in range(NA)]
    psO = [k.ps("psO", [128, 512]) for _ in range(4)]
    psA2 = [k.ps("psA", [128, 512]) for _ in range(NA)]
    psGb = k.ps("psGb", [128, 512])
    psL = psA2[0]
    blocks = list(blocks_of(G, need_ctx))
    NBLK = len(blocks)
    st = dict(nev=0)

    def prologue_R(n):
        b, blk, tok0, row = blocks[n]
        for tt in range(2):
            t0 = tok0 + tt * 128
            r_ = Rt[n % 2][tt]
            k.dma("sp", r_[:], G.RT[b, t0:t0 + 128, :], reads=[G.RT], writes=[r_])
            for a in range(3):
                k.tr(psGb[:, a * 128:(a + 1) * 128], r_[:, a * 128:(a + 1) * 128], G.ident[:], [r_, G.ident], [psGb])
            k.copy("act", RTt[n % 2][:, :, tt * 128:(tt + 1) * 128], psGb[:, 0:384].rearrange("p (a t) -> p a t", a=3), [psGb], [RTt[n % 2]])

    def gbuild_steps(n):
        R_ = RTt[n % 2]
        G_ = GT[n % 2]
        steps = []
        for sub in range(256 // SB_):
            def step(sub=sub):
                tsl = slice(sub * SB_, (sub + 1) * SB_)
                k.tt("dve", Q[:], G.iota3[:, 0:SB_, :], R_[:, 1, tsl].unsqueeze(2).to_broadcast([128, SB_, 128]),
                     ALU.is_equal, [G.iota3, R_], [Q])
                for t in range(SB_):
                    tok = sub * SB_ + t
                    k.ts("dve", Pg[:, t, :], G.iota3[:, 0, :], R_[:, 0, tok:tok + 1], R_[:, 2, tok:tok + 1],
                         ALU.is_equal, ALU.mult, reads=[G.iota3, R_], writes=[Pg])
                for t in range(SB_):
                    tok = sub * SB_ + t
                    k.mm(psGb[:, (t % 4) * 128:(t % 4 + 1) * 128], Q[:, t, :], Pg[:, t, :], True, True, [Q, Pg], [psGb])
                    if t % 4 == 3:
                        k.copy("act" if st["nev"] % 2 == 0 else "dve", G_[:, tok - 3:tok + 1, :].rearrange("p t i -> p (t i)"),
                               psGb[:], [psGb], [G_])
                        st["nev"] += 1
            steps.append(step)
        return steps

    def modulate(n):
        b, blk, tok0, row = blocks[n]
        k.dma("sp", hT[:], Hin[b, :, :, tok0:tok0 + 256], reads=[Hin], writes=[hT])
        for c in range(8):
            k.actf(uTb[:, c, :], hT[:, c, :], AF.Identity, [hT, G.mod1, G.modT], [uTb],
                   scale=G.mod1[:, li, seg(4, c), row:row + 1], bias=G.modT[:, li, seg(3, c), row:row + 1])

    def load_tab(ii):
        k.dma("sp", ut[ii % NBUF][:].rearrange("p a c j -> p a (c j)"), G.UB.h[:, 2 * ii:2 * ii + 2, :],
              reads=[G.UB.sub[ii // 4]], writes=[ut[ii % NBUF]])
        k.dma("sp", vt[ii % NBUF][:], G.VB.h[:, 2 * ii:2 * ii + 2, :], reads=[G.VB.sub[ii // 4]], writes=[vt[ii % NBUF]])

    def amm(i):
        u_ = ut[(i // 2) % NBUF]
        pa = psA2[i % NA][:, 0:256]
        for c in range(8):
            k.mm(pa, u_[:, i % 2, c, :], uTb[:, c, :], c == 0, c == 7, [u_, uTb], [psA2[i % NA]])

    def expert_loop(n, steps):
        G_ = GT[n % 2]
        for ii in range(NBUF - 1):
            load_tab(ii)
        amm(0)
        amm(1)
        every = 128 // len(steps) if steps else 0
        si = 0
        for i in range(128):
            if i % 2 == 0 and i // 2 + NBUF - 1 < 64:
                load_tab(i // 2 + NBUF - 1)
            if i + 2 < 128:
                amm(i + 2)
            v_ = vt[(i // 2) % NBUF]
            pa = psA2[i % NA][:, 0:256]
            k.actf(ga[i % NA][:], pa, AF.Gelu_apprx_tanh, [psA2[i % NA]], [ga[i % NA]])
            k.tt("dve", gam[i % NA][:], ga[i % NA][:], G_[:, :, i], ALU.mult, [ga[i % NA], G_], [gam[i % NA]])
            for dc in range(8):
                k.mm(psO[dc // 2][:, (dc % 2) * 256:(dc % 2 + 1) * 256], v_[:, i % 2, dc * 128:(dc + 1) * 128], gam[i % NA][:],
                     (i == 0 and dc % 2 == 0), i == 127, [v_, gam[i % NA]], [psO[dc // 2]], skip_group_check=True)
            if steps and i % every == every - 1 and si < len(steps):
                steps[si]()
                si += 1
        while si < len(steps):
            steps[si]()
            si += 1

    def epilogue(n):
        b, blk, tok0, row = blocks[n]
        k.actf(hT[:].rearrange("p c t -> p (c t)"), hT[:].rearrange("p c t -> p (c t)"), AF.Copy, [hT], [hT], scale=ALPHA)
        for c in range(8):
            k.stt(rr[:, c, :], psO[c // 2][:, (c % 2) * 256:(c % 2 + 1) * 256], G.modT[:, li, seg(5, c), row:row + 1],
                  hT[:, c, :], ALU.mult, ALU.add, [psO[c // 2], G.modT, hT], [rr])
        layer_norm_fm(k, G, rr, 256, li, 1, rr, psL, tmp)
        if out_final:
            k.dma("act", G.out[b, :, :, tok0 - CTX:tok0 - CTX + 256], rr[:], reads=[rr], writes=[G.out])
        else:
            k.dma("act", Hout[b, :, :, tok0:tok0 + 256], rr[:], reads=[rr], writes=[Hout])

    prologue_R(0)
    for s_ in gbuild_steps(0):
        s_()
    modulate(0)
    for n in range(NBLK):
        nxt = n + 1 < NBLK
        steps = []
        if nxt:
            prologue_R(n + 1)
            steps = gbuild_steps(n + 1)
        expert_loop(n, steps)
        epilogue(n)
        if nxt:
            modulate(n + 1)
    k.pop()
```

```python
import numpy as np
from contextlib import ExitStack
import concourse.bass as bass
import concourse.mybir as mybir
from concourse.bass_utils import run_bass_kernel_spmd

F32 = mybir.dt.float32
BF16 = mybir.dt.bfloat16
U32 = mybir.dt.uint32
I32 = mybir.dt.int32
AF = mybir.ActivationFunctionType
ALU = mybir.AluOpType
AX = mybir.AxisListType


class Res:
    __slots__ = ("name", "w", "rd")

    def __init__(self, name):
        self.name = name
        self.w = None
        self.rd = []


class Tl:
    def __init__(self, h, name):
        self.h = h
        self.res = Res(name)
        self.name = name

    def __getitem__(self, idx):
        return self.h[idx]


class K:
    ENG = ("pe", "dve", "act", "pool", "sp")
    GEN = 30000
    NDS = 6
    SKIP_SAME = True
    SKIP_RAW = False

    def __init__(self, nc):
        self.nc = nc
        self.root = ExitStack()
        self.stacks = [self.root]
        self.eng = dict(pe=nc.tensor, dve=nc.vector, act=nc.scalar, pool=nc.gpsimd, sp=nc.sync)
        self.allsems = []
        self.cur = {}
        self.cnt = {}
        self.seen = {e: {} for e in self.ENG}
        for e in self.ENG:
            self.cur[e] = self._newsem("c_" + e)
            self.cnt[e] = 0
        self.dq = {}
        for q in ("sp", "act", "pool"):
            self.dq[q] = dict(idx=[self._newsem(f"d_{q}{i}") for i in range(self.NDS)],
                              val=[0] * self.NDS, n=0)
        self.uid = 0
        self.ninst = 0

    def _newsem(self, name):
        s = self.root.enter_context(self.nc.semaphore(name + f"_{len(self.allsems)}"))
        self.allsems.append(s)
        return len(self.allsems) - 1

    def push(self):
        self.stacks.append(ExitStack())

    def pop(self):
        self.barrier()
        self.stacks.pop().close()

    def sb(self, name, shape, dtype=F32):
        self.uid += 1
        nm = f"{name}_{self.uid}"
        h = self.stacks[-1].enter_context(self.nc.sbuf_tensor(nm, list(shape), dtype))
        return Tl(h, nm)

    def ps(self, name, shape, dtype=F32):
        self.uid += 1
        nm = f"{name}_{self.uid}"
        h = self.stacks[-1].enter_context(self.nc.psum_tensor(nm, list(shape), dtype))
        return Tl(h, nm)

    def dram(self, name, shape, dtype=F32, kind="Internal"):
        t = self.nc.dram_tensor(name, list(shape), dtype, kind=kind)
        tl = Tl(t.ap(), name)
        return tl

    def _wait(self, e, tk):
        idx, v, src = tk
        if src == e and e == "pe":
            return
        if self.seen[e].get(idx, 0) >= v:
            return
        self.eng[e].wait_ge(self.allsems[idx], v)
        self.seen[e][idx] = v

    @staticmethod
    def _r(x):
        return x.res if isinstance(x, Tl) else x

    def _deps(self, e, reads, writes):
        for r in reads:
            r = self._r(r)
            if r.w is not None and (r.w[2] != e or not self.SKIP_RAW):
                self._wait(e, r.w)
        for w in writes:
            w = self._r(w)
            if w.w is not None and (w.w[2] != e or not self.SKIP_SAME):
                self._wait(e, w.w)
            for tk in w.rd:
                if tk[2] != e or not self.SKIP_SAME:
                    self._wait(e, tk)

    def _commit(self, tk, reads, writes):
        for r in reads:
            r = self._r(r)
            r.rd.append(tk)
            if len(r.rd) > 64:
                best = {}
                for t in r.rd:
                    if t[0] not in best or best[t[0]][1] < t[1]:
                        best[t[0]] = t
                r.rd = list(best.values())
        for w in writes:
            w = self._r(w)
            w.w = tk
            w.rd = []

    def op(self, e, fn, reads=(), writes=()):
        self._deps(e, reads, writes)
        ins = fn(self.eng[e])
        if self.cnt[e] >= self.GEN:
            self.cur[e] = self._newsem("c_" + e)
            self.cnt[e] = 0
        self.cnt[e] += 1
        ins.then_inc(self.allsems[self.cur[e]], 1)
        tk = (self.cur[e], self.cnt[e], e)
        self._commit(tk, reads, writes)
        self.ninst += 1
        return tk

    def dma(self, q, out, in_, reads=(), writes=(), **kw):
        self._deps(q, reads, writes)
        d = self.dq[q]
        k = d["n"] % self.NDS
        d["n"] += 1
        if d["val"][k] > 0:
            self._wait(q, (d["idx"][k], d["val"][k], "dma"))
        ins = self.eng[q].dma_start(out=out, in_=in_, **kw)
        d["val"][k] += 16
        ins.then_inc(self.allsems[d["idx"][k]], 16)
        tk = (d["idx"][k], d["val"][k], "dma")
        self._commit(tk, reads, writes)
        self.ninst += 1
        return tk

    def barrier(self):
        tks = []
        for e in self.ENG:
            if self.cnt[e] > 0:
                tks.append((self.cur[e], self.cnt[e], "x"))
        for q, d in self.dq.items():
            for k in range(self.NDS):
                if d["val"][k] > 0:
                    tks.append((d["idx"][k], d["val"][k], "dma"))
        for e in self.ENG:
            for tk in tks:
                self._wait(e, tk)

    def finish(self):
        self.barrier()
        while len(self.stacks) > 1:
            self.stacks.pop().close()
        self.root.close()

    def mm(self, out, lhsT, rhs, start, stop, reads, writes, **kw):
        return self.op("pe", lambda e: e.matmul(out, lhsT, rhs, start=start, stop=stop, **kw), reads, writes)

    def tr(self, out, in_, ident, reads, writes):
        return self.op("pe", lambda e: e.transpose(out, in_, ident), reads, writes)

    def actf(self, out, in_, func, reads, writes, bias=None, scale=1.0, accum_out=None, e="act"):
        kw = {}
        if bias is not None:
            kw["bias"] = bias
        if accum_out is not None:
            kw["accum_out"] = accum_out
        return self.op(e, lambda en: en.activation(out, in_, func, scale=scale, **kw), reads, writes)

    def ts(self, e, out, in0, s1, s2, op0, op1=None, reads=(), writes=(), accum_out=None):
        kw = {}
        if op1 is not None:
            kw["op1"] = op1
        if accum_out is not None:
            kw["accum_out"] = accum_out
        return self.op(e, lambda en: en.tensor_scalar(out, in0, s1, s2, op0, **kw), reads, writes)

    def tt(self, e, out, in0, in1, op, reads=(), writes=()):
        return self.op(e, lambda en: en.tensor_tensor(out, in0, in1, op), reads, writes)

    def stt(self, out, in0, scalar, in1, op0, op1, reads=(), writes=(), e="dve"):
        return self.op(e, lambda en: en.scalar_tensor_tensor(out, in0, scalar, in1, op0, op1), reads, writes)

    def copy(self, e, out, in_, reads=(), writes=()):
        if e == "act":
            return self.op(e, lambda en: en.copy(out, in_), reads, writes)
        return self.op(e, lambda en: en.tensor_copy(out, in_), reads, writes)

    def memset(self, e, ap, val, writes=()):
        return self.op(e, lambda en: en.memset(ap, val), (), writes)


D = 1024
CTX = 256
LAT = 2048
NT = CTX + LAT
DEPTH = 4
ALPHA = (2 * DEPTH) ** 0.25
LN_EPS = 1e-5
RMS_EPS = 1e-6
NRP = 8


class GG:
    pass


def seg(s, c):
    return s * 8 + c


def setup_globals(k, NB, dbg=False):
    G = GG()
    skind = "ExternalOutput" if dbg else "Internal"
    G.NB = NB
    ext = lambda n, s, d=F32: k.dram(n, s, d, kind="ExternalInput")
    G.hT0 = ext("hT0", [NB, 128, 8, NT])
    G.cT = ext("cT", [128, 8, NRP])
    G.ada_w = ext("ada_w", [4, 8, 128, 6144])
    G.ada_b = ext("ada_b", [128, 4, 48])
    G.ln_g = ext("ln_g", [128, 4, 2, 8])
    G.ln_b = ext("ln_b", [128, 4, 2, 8])
    G.peer_wq = ext("peer_wq", [4, 1024, 2048])
    G.peer_keysT = ext("peer_keysT", [4, 2, 128, 128])
    G.peer_u = ext("peer_u", [4, 128 * 128, 1024])
    G.peer_v = ext("peer_v", [4, 128 * 128, 1024])
    G.ident_d = ext("ident", [128, 128])
    G.iota3_d = ext("iota3", [128, 16 * 128], BF16)
    G.iota16_d = ext("iota16", [128, 16])
    G.H = [G.hT0, k.dram("H1", [NB, 128, 8, NT], kind=skind), k.dram("H2", [NB, 128, 8, NT], kind=skind)]
    G.RT = k.dram("RT", [NB, NT, 384], kind=skind)
    G.UB = k.dram("UB", [128, 128, 1024], BF16)
    G.VB = k.dram("VB", [128, 128, 1024], BF16)
    G.UB.sub = [Res(f"UB{i}") for i in range(16)]
    G.VB.sub = [Res(f"VB{i}") for i in range(16)]
    G.out = k.dram("outT", [NB, 128, 8, LAT], kind="ExternalOutput")
    G.da_w_in = ext("da_w_in", [2, 1024, 3072])
    G.da_w_sw = ext("da_w_sw", [2, 1024, 2048])
    G.da_w_out = ext("da_w_out", [2, 1024, 1024])
    G.da_lam_q = ext("da_lam_q", [2, 128])
    G.da_lam_k = ext("da_lam_k", [2, 128])
    G.da_subln = ext("da_subln", [2, 128, 1])
    G.rope = ext("rope", [4, 128, NT])
    def wscr(n, r, c):
        t = k.dram(n, [r, c], BF16)
        t.sub = [t.res]
        return t
    G.WinB = wscr("WinB", 1024, 3072)
    G.WswB = wscr("WswB", 1024, 2048)
    G.WoutB = wscr("WoutB", 1024, 1024)
    G.hg_w_in = ext("hg_w_in", [1024, 5120])
    G.hg_w_out = ext("hg_w_out", [1024, 1024])
    G.hg_norm = ext("hg_norm", [128, 8])
    G.hg_lb = ext("hg_lb", [128, 8, 4])
    G.hgmask_d = ext("hgmask", [128, 64 + NT], BF16)
    G.WhgB = wscr("WhgB", 1024, 5120)
    G.WhgoB = wscr("WhgoB", 1024, 1024)
    G.PQ_d = k.dram("PQ_d", [NB, 4, 8, 128, NT])
    G.Vr_d = k.dram("Vr_d", [NB, NT, 1024], BF16)
    G.s5p = ext("s5p", [128, 2, 32, 3])
    G.s5bc = ext("s5bc", [128, 2, 32, 4, 16])
    G.s5_d = ext("s5_d", [128, 8])
    G.s5_w_glu = ext("s5_w_glu", [1024, 2048])
    G.tT_d = ext("tT", [128, NT])
    G.WgluB = wscr("WgluB", 1024, 2048)
    G.UT_d = k.dram("UT_d", [NB, 8, 128, NT], BF16)
    G.ZT_d = k.dram("ZT_d", [NB, 8, 128, NT], BF16)
    G.QT_d = k.dram("QT_d", [NB, 8, 128, NT], BF16)
    G.KT_d = k.dram("KT_d", [NB, 8, 128, NT], BF16)
    G.V_d = k.dram("V_d", [NB, NT, 1024], BF16)
    G.modT = k.sb("modT", [128, 4, 48, NRP])
    G.mod1 = k.sb("mod1", [128, 4, 48, NRP])
    G.lng = k.sb("lng", [128, 4, 2, 8])
    G.lnb = k.sb("lnb", [128, 4, 2, 8])
    G.ident = k.sb("ident", [128, 128])
    G.onesD = k.sb("onesD", [128, 128])
    G.iota3 = k.sb("iota3", [128, 16, 128], BF16)
    G.iota16 = k.sb("iota16", [128, 16])
    k.dma("sp", G.lng[:], G.ln_g[:, :, :, :], reads=[G.ln_g], writes=[G.lng])
    k.dma("sp", G.lnb[:], G.ln_b[:, :, :, :], reads=[G.ln_b], writes=[G.lnb])
    k.dma("sp", G.ident[:], G.ident_d[:, :], reads=[G.ident_d], writes=[G.ident])
    k.dma("sp", G.iota3[:].rearrange("p a b -> p (a b)"), G.iota3_d[:, :], reads=[G.iota3_d], writes=[G.iota3])
    k.dma("sp", G.iota16[:], G.iota16_d[:, :], reads=[G.iota16_d], writes=[G.iota16])
    k.memset("dve", G.onesD[:], 1.0 / D, writes=[G.onesD])
    G.epsln = k.sb("epsln", [128, 1])
    G.epsrms = k.sb("epsrms", [128, 1])
    k.memset("dve", G.epsln[:], LN_EPS, writes=[G.epsln])
    k.memset("dve", G.epsrms[:], RMS_EPS, writes=[G.epsrms])
    return G


def phase_adaln(k, G):
    k.push()
    cT = k.sb("cT", [128, 8, NRP])
    sT = k.sb("sT", [128, 8, NRP])
    adab = k.sb("adab", [128, 4, 48])
    k.dma("sp", cT[:], G.cT[:, :, :], reads=[G.cT], writes=[cT])
    k.dma("sp", adab[:], G.ada_b[:, :, :], reads=[G.ada_b], writes=[adab])
    k.actf(sT[:], cT[:], AF.Silu, [cT], [sT])
    ps = k.ps("adaps", [128, 48 * NRP])
    wb = [k.sb("adaw", [128, 6144]) for _ in range(2)]
    n = 0
    for i in range(4):
        first = True
        for kc in range(8):
            w = wb[n % 2]
            n += 1
            k.dma("sp", w[:], G.ada_w[i, kc, :, :], reads=[G.ada_w], writes=[w])
            for j in range(48):
                k.mm(ps[:, j * NRP:(j + 1) * NRP], w[:, j * 128:(j + 1) * 128], sT[:, kc, :],
                     first, (kc == 7 and j == 47), [w, sT], [ps], skip_group_check=True)
                first = False
        k.tt("dve", G.modT[:, i, :, :], ps[:].rearrange("p (j r) -> p j r", r=NRP),
             adab[:, i, :].unsqueeze(2).to_broadcast([128, 48, NRP]), ALU.add, [ps, adab], [G.modT])
    k.ts("dve", G.mod1[:].rearrange("p a b c -> p (a b c)"), G.modT[:].rearrange("p a b c -> p (a b c)"),
         1.0, None, ALU.add, reads=[G.modT], writes=[G.mod1])
    k.pop()


def cast_bf16(k, dst, src, rows, cols, step=1024):
    n = 0
    for r0 in range(0, rows, step):
        for c0 in range(0, cols, 1024):
            k.dma("pool", dst.h[r0:r0 + step, c0:c0 + 1024], src[r0:r0 + step, c0:c0 + 1024],
                  reads=[], writes=[dst.sub[n] if len(dst.sub) > 1 else dst])
            n += 1


def layer_norm_fm(k, G, r, T, li, which, out, psL, tmp):
    sq, mean, rstd = tmp["sq"], tmp["mean"], tmp["rstd"]
    k.actf(sq[:].rearrange("p c t -> p (c t)"), r[:].rearrange("p c t -> p (c t)"), AF.Square, [r], [sq])
    for c in range(8):
        k.mm(psL[:, 0:T], G.onesD[:], r[:, c, :], c == 0, c == 7, [G.onesD, r], [psL])
    k.copy("act", mean[:, 0:T], psL[:, 0:T], [psL], [mean])
    for c in range(8):
        k.mm(psL[:, T:2 * T], G.onesD[:], sq[:, c, :], c == 0, c == 7, [G.onesD, sq], [psL])
    k.tt("dve", rstd[:, 0:T], mean[:, 0:T], mean[:, 0:T], ALU.mult, [mean], [rstd])
    k.tt("dve", rstd[:, 0:T], psL[:, T:2 * T], rstd[:, 0:T], ALU.subtract, [psL, rstd], [rstd])
    k.actf(rstd[:, 0:T], rstd[:, 0:T], AF.Sqrt, [rstd], [rstd], bias=G.epsln[:, 0:1])
    k.op("dve", lambda e: e.reciprocal(rstd[:, 0:T], rstd[:, 0:T]), [rstd], [rstd])
    for c in range(8):
        k.tt("dve", sq[:, c, :], r[:, c, :], mean[:, 0:T], ALU.subtract, [r, mean], [sq])
        k.tt("dve", sq[:, c, :], sq[:, c, :], rstd[:, 0:T], ALU.mult, [sq, rstd], [sq])
        k.actf(out[:, c, :], sq[:, c, :], AF.Identity, [sq, G.lng, G.lnb], [out],
               scale=G.lng[:, li, which, c:c + 1], bias=G.lnb[:, li, which, c:c + 1])


def blocks_of(G, need_ctx):
    for b in range(G.NB):
        for blk in range(NT // 256):
            if blk == 0 and not need_ctx:
                continue
            yield b, blk, blk * 256, (G.NB if blk == 0 else b)


def peer_route(k, G, li, Hin, need_ctx):
    k.push()
    wq = k.sb("wq", [128, 8, 2048])
    kT = k.sb("kT", [128, 2, 128])
    k.dma("sp", wq[:], G.peer_wq[li].rearrange("(k p) n -> p k n", p=128), reads=[G.peer_wq], writes=[wq])
    k.dma("sp", kT[:], G.peer_keysT[li].rearrange("c d k -> d c k"), reads=[G.peer_keysT], writes=[kT])
    hT = [k.sb("hT", [128, 8, 256]) for _ in range(2)]
    uT2 = [k.sb("uT", [128, 8, 256]) for _ in range(2)]
    qT2 = [k.sb("qT", [128, 16, 256]) for _ in range(2)]
    psq = [k.ps("psq", [128, 512]) for _ in range(2)]
    pss = k.ps("pss", [128, 2048])
    sc2_ = [k.sb("sc", [128, 2048]) for _ in range(2)]
    nsc = 0
    scr = k.sb("scr", [128, 16, 128])
    V = k.sb("V", [128, 16, 16])
    V2 = k.sb("V2", [128, 16, 8])
    TS2 = k.sb("TS2", [128, 8, 8])
    I = k.sb("I", [128, 16, 16], U32)
    If = k.sb("If", [128, 16, 16])
    cand = k.sb("cand", [128, 8, 256])
    scr2 = k.sb("scr2", [128, 8, 256])
    TS = k.sb("TS", [128, 8, 16])
    PI = k.sb("PI", [128, 8, 16], U32)
    PA = k.sb("PA", [128, 8, 16], U32)
    PB = k.sb("PB", [128, 8, 16], U32)
    PAf = k.sb("PAf", [128, 8, 16])
    PBf = k.sb("PBf", [128, 8, 16])
    E = k.sb("E", [128, 8, 16, 16])
    ssum = k.sb("ssum", [128, 8])
    Rt = [k.sb("Rt", [128, 3, 128]) for _ in range(2)]
    Ifv = If[:].rearrange("p (h c) k -> p h c k", c=2)
    Vv = V[:].rearrange("p (h c) k -> p h c k", c=2)
    nb = 0
    for b, blk, tok0, row in blocks_of(G, need_ctx):
        h = hT[nb % 2]
        uT = uT2[nb % 2]
        qT = qT2[nb % 2]
        nb += 1
        k.dma("sp", h[:], Hin[b, :, :, tok0:tok0 + 256], reads=[Hin], writes=[h])
        for c in range(8):
            k.actf(uT[:, c, :], h[:, c, :], AF.Identity, [h, G.mod1, G.modT], [uT],
                   scale=G.mod1[:, li, seg(4, c), row:row + 1], bias=G.modT[:, li, seg(3, c), row:row + 1])
        for hc in range(16):
            ps = psq[(hc // 2) % 2]
            half = hc % 2
            for kc in range(8):
                k.mm(ps[:, half * 256:(half + 1) * 256], wq[:, kc, hc * 128:(hc + 1) * 128], uT[:, kc, :],
                     kc == 0, kc == 7, [wq, uT], [ps])
            if half == 1:
                k.copy("act", qT[:, hc - 1:hc + 1, :].rearrange("p a t -> p (a t)"), ps[:], [ps], [qT])
        for tt in range(2):
            for hc in range(16):
                k.mm(pss[:, hc * 128:(hc + 1) * 128], qT[:, hc, tt * 128:(tt + 1) * 128], kT[:, hc % 2, :],
                     True, True, [qT, kT], [pss])
            sc = sc2_[nsc % 2]
            nsc += 1
            k.copy("act", sc[:], pss[:], [pss], [sc])
            sl_ = lambda hc: sc[:, hc * 128:(hc + 1) * 128]
            for hc in range(16):
                k.op("dve", lambda e: e.max(V[:, hc, 0:8], sl_(hc)), [sc], [V])
            for hc in range(16):
                k.op("dve", lambda e: e.max_index(I[:, hc, 0:8], V[:, hc, 0:8], sl_(hc)), [sc, V], [I])
            for hc in range(16):
                k.op("dve", lambda e: e.match_replace(scr[:, hc, :], V[:, hc, 0:8], sl_(hc), -1e30), [sc, V], [scr])
            for hc in range(16):
                k.op("dve", lambda e: e.max(V2[:, hc, :], scr[:, hc, :]), [scr], [V2])
            for hc in range(16):
                k.op("dve", lambda e: e.max_index(I[:, hc, 8:16], V2[:, hc, :], scr[:, hc, :]), [scr, V2], [I])
            k.copy("dve", V[:, :, 8:16], V2[:], [V2], [V])
            k.copy("dve", If[:], I[:], [I], [If])
            k.tt("dve", cand[:].rearrange("p h (a b) -> p h a b", b=16),
                 Vv[:, :, 0, :].unsqueeze(3).to_broadcast([128, 8, 16, 16]),
                 Vv[:, :, 1, :].unsqueeze(2).to_broadcast([128, 8, 16, 16]), ALU.add, [V], [cand])
            for hh in range(8):
                k.op("dve", lambda e: e.max(TS[:, hh, 0:8], cand[:, hh, :]), [cand], [TS])
            for hh in range(8):
                k.op("dve", lambda e: e.max_index(PI[:, hh, 0:8], TS[:, hh, 0:8], cand[:, hh, :]), [cand, TS], [PI])
            for hh in range(8):
                k.op("dve", lambda e: e.match_replace(scr2[:, hh, :], TS[:, hh, 0:8], cand[:, hh, :], -1e30), [cand, TS], [scr2])
            for hh in range(8):
                k.op("dve", lambda e: e.max(TS2[:, hh, :], scr2[:, hh, :]), [scr2], [TS2])
            for hh in range(8):
                k.op("dve", lambda e: e.max_index(PI[:, hh, 8:16], TS2[:, hh, :], scr2[:, hh, :]), [scr2, TS2], [PI])
            k.copy("dve", TS[:, :, 8:16], TS2[:], [TS2], [TS])
            k.op("dve", lambda e: e.tensor_single_scalar(PA[:], PI[:], 4, ALU.logical_shift_right), [PI], [PA])
            k.op("dve", lambda e: e.tensor_single_scalar(PB[:], PI[:], 15, ALU.bitwise_and), [PI], [PB])
            k.copy("dve", PAf[:], PA[:], [PA], [PAf])
            k.copy("dve", PBf[:], PB[:], [PB], [PBf])
            R = Rt[tt]
            for which, Pf in ((0, PAf), (1, PBf)):
                k.tt("dve", E[:], G.iota16[:].unsqueeze(1).unsqueeze(1).to_broadcast([128, 8, 16, 16]),
                     Pf[:].unsqueeze(3).to_broadcast([128, 8, 16, 16]), ALU.is_equal, [G.iota16, Pf], [E])
                k.tt("dve", E[:], E[:], Ifv[:, :, which, :].unsqueeze(2).to_broadcast([128, 8, 16, 16]),
                     ALU.mult, [E, If], [E])
                k.op("dve", lambda e: e.tensor_reduce(R[:, which, :], E[:].rearrange("p h k a -> p (h k) a"),
                                                        AX.X, ALU.add), [E], [R])
            g3 = R[:, 2, :].rearrange("p (h k) -> p h k", k=16)
            k.tt("dve", g3, TS[:], TS[:, :, 0:1].to_broadcast([128, 8, 16]), ALU.subtract, [TS], [R])
            k.actf(g3, g3, AF.Exp, [R], [R])
            k.op("dve", lambda e: e.tensor_reduce(ssum[:], g3, AX.X, ALU.add), [R], [ssum])
            k.op("dve", lambda e: e.reciprocal(ssum[:], ssum[:]), [ssum], [ssum])
            k.tt("dve", g3, g3, ssum[:].unsqueeze(2).to_broadcast([128, 8, 16]), ALU.mult, [R, ssum], [R])
            t0 = tok0 + tt * 128
            k.dma("act", G.RT[b, t0:t0 + 128, :], R[:].rearrange("p a k -> p (a k)"), reads=[R], writes=[G.RT])
    k.pop()


def peer_experts(k, G, li, Hin, Hout, need_ctx, out_final=False):
    k.push()
    NBUF = 4
    hT2 = [k.sb("hT", [128, 8, 256]) for _ in range(2)]
    uTb2 = [k.sb("uTb", [128, 8, 256], BF16) for _ in range(2)]
    rr = k.sb("rr", [128, 8, 256])
    oo = rr
    nblk_ = 0
    tmp = dict(sq=k.sb("sq", [128, 8, 256]), mean=k.sb("mean", [128, 256]), rstd=k.sb("rstd", [128, 256]))
    Rt = [k.sb("Rt", [128, 384]) for _ in range(2)]
    RTt = k.sb("RTt", [128, 3, 256])
    SB_ = 16
    Q = [k.sb("Q", [128, SB_, 128], BF16) for _ in range(2)]
    Pg = [k.sb("Pg", [128, SB_, 128], BF16) for _ in range(2)]
    GT = k.sb("GT", [128, 256, 128], BF16)
    ut = [k.sb("ut", [128, 2, 8, 128], BF16) for _ in range(NBUF)]
    vt = [k.sb("vt", [128, 2, 1024], BF16) for _ in range(NBUF)]
    ga = [k.sb("ga", [128, 256], BF16) for _ in range(3)]
    gam = [k.sb("gam", [128, 256], BF16) for _ in range(3)]
    psO = [k.ps("psO", [128, 512]) for _ in range(4)]
    psG = [k.ps("psG", [128, 512]) for _ in range(2)]
    psA2 = [psG[0], psG[1], k.ps("psA", [128, 512])]
    psA_r = psA2
    psL = psG[0]
    NA = 3
    ng = 0
    nev = 0
    for b, blk, tok0, row in blocks_of(G, need_ctx):
        hT = hT2[0]
        uTb = uTb2[0]
        nblk_ += 1
        k.dma("sp", hT[:], Hin[b, :, :, tok0:tok0 + 256], reads=[Hin], writes=[hT])
        for c in range(8):
            k.actf(uTb[:, c, :], hT[:, c, :], AF.Identity, [hT, G.mod1, G.modT], [uTb],
                   scale=G.mod1[:, li, seg(4, c), row:row + 1], bias=G.modT[:, li, seg(3, c), row:row + 1])
        for tt in range(2):
            t0 = tok0 + tt * 128
            k.dma("sp", Rt[tt][:], G.RT[b, t0:t0 + 128, :], reads=[G.RT], writes=[Rt[tt]])
            pg = psG[ng % 2]
            ng += 1
            for a in range(3):
                k.tr(pg[:, a * 128:(a + 1) * 128], Rt[tt][:, a * 128:(a + 1) * 128], G.ident[:], [Rt[tt], G.ident], [pg])
            k.copy("act", RTt[:, :, tt * 128:(tt + 1) * 128], pg[:, 0:384].rearrange("p (a t) -> p a t", a=3), [pg], [RTt])
        def load_tab(ii):
            k.dma("sp", ut[ii % NBUF][:].rearrange("p a c j -> p a (c j)"), G.UB.h[:, 2 * ii:2 * ii + 2, :],
                  reads=[G.UB.sub[ii // 4]], writes=[ut[ii % NBUF]])
            k.dma("sp", vt[ii % NBUF][:], G.VB.h[:, 2 * ii:2 * ii + 2, :], reads=[G.VB.sub[ii // 4]], writes=[vt[ii % NBUF]])
        for ii in range(NBUF - 1):
            load_tab(ii)
        for sub in range(256 // SB_):
            s = sub % 2
            tsl = slice(sub * SB_, (sub + 1) * SB_)
            k.tt("dve", Q[s][:], G.iota3[:], RTt[:, 1, tsl].unsqueeze(2).to_broadcast([128, SB_, 128]),
                 ALU.is_equal, [G.iota3, RTt], [Q[s]])
            for t in range(SB_):
                tok = sub * SB_ + t
                k.ts("dve", Pg[s][:, t, :], G.iota3[:, 0, :], RTt[:, 0, tok:tok + 1], RTt[:, 2, tok:tok + 1],
                     ALU.is_equal, ALU.mult, reads=[G.iota3, RTt], writes=[Pg[s]])
            for t in range(SB_):
                tok = sub * SB_ + t
                pg = psG[ng % 2]
                k.mm(pg[:, (t % 4) * 128:(t % 4 + 1) * 128], Q[s][:, t, :], Pg[s][:, t, :], True, True,
                     [Q[s], Pg[s]], [pg])
                if t % 4 == 3:
                    ng += 1
                    k.copy("act" if nev % 2 == 0 else "dve", GT[:, tok - 3:tok + 1, :].rearrange("p t i -> p (t i)"),
                           pg[:], [pg], [GT])
                    nev += 1
        def amm(i):
            u_ = ut[(i // 2) % NBUF]
            pa = psA2[i % NA][:, 0:256]
            for c in range(8):
                k.mm(pa, u_[:, i % 2, c, :], uTb[:, c, :], c == 0, c == 7, [u_, uTb], [psA_r[i % NA]])
        amm(0)
        amm(1)
        for i in range(128):
            if i % 2 == 0 and i // 2 + NBUF - 1 < 64:
                load_tab(i // 2 + NBUF - 1)
            if i + 2 < 128:
                amm(i + 2)
            v_ = vt[(i // 2) % NBUF]
            pa = psA2[i % NA][:, 0:256]
            par = psA_r[i % NA]
            k.actf(ga[i % NA][:], pa, AF.Gelu_apprx_tanh, [par], [ga[i % NA]])
            k.tt("dve", gam[i % NA][:], ga[i % NA][:], GT[:, :, i], ALU.mult, [ga[i % NA], GT], [gam[i % NA]])
            for dc in range(8):
                k.mm(psO[dc // 2][:, (dc % 2) * 256:(dc % 2 + 1) * 256], v_[:, i % 2, dc * 128:(dc + 1) * 128], gam[i % NA][:],
                     (i == 0 and dc % 2 == 0), i == 127, [v_, gam[i % NA]], [psO[dc // 2]], skip_group_check=True)
        k.actf(hT[:].rearrange("p c t -> p (c t)"), hT[:].rearrange("p c t -> p (c t)"), AF.Copy, [hT], [hT], scale=ALPHA)
        for c in range(8):
            k.stt(rr[:, c, :], psO[c // 2][:, (c % 2) * 256:(c % 2 + 1) * 256], G.modT[:, li, seg(5, c), row:row + 1],
                  hT[:, c, :], ALU.mult, ALU.add, [psO[c // 2], G.modT, hT], [rr])
        layer_norm_fm(k, G, rr, 256, li, 1, oo, psL, tmp)
        if out_final:
            k.dma("act", G.out[b, :, :, tok0 - CTX:tok0 - CTX + 256], oo[:], reads=[oo], writes=[G.out])
        else:
            k.dma("act", Hout[b, :, :, tok0:tok0 + 256], oo[:], reads=[oo], writes=[Hout])
    k.pop()


def peer_cast(k, G, li):
    for dst, src in ((G.UB, G.peer_u), (G.VB, G.peer_v)):
        sv = src[li].rearrange("(i p) n -> p i n", p=128)
        for n in range(16):
            k.dma("pool", dst.h[:, 8 * n:8 * n + 8, :], sv[:, 8 * n:8 * n + 8, :], reads=[], writes=[dst.sub[n]])


import ml_dtypes


def fm(a):
    T = a.shape[-2]
    x = a.reshape(a.shape[:-2] + (T, 8, 128))
    nd = x.ndim
    perm = tuple(range(nd - 3)) + (nd - 1, nd - 2, nd - 3)
    return np.ascontiguousarray(np.transpose(x, perm))


def unfm(a):
    nd = a.ndim
    perm = tuple(range(nd - 3)) + (nd - 1, nd - 2, nd - 3)
    x = np.transpose(a, perm)
    return np.ascontiguousarray(x).reshape(x.shape[:-2] + (D,))


def host_consts():
    ident = np.eye(128, dtype=np.float32)
    iota3 = np.tile(np.arange(128, dtype=np.float32)[None, None, :], (128, 16, 1)).reshape(128, 16 * 128)
    iota16 = np.tile(np.arange(16, dtype=np.float32)[None, :], (128, 1))
    return dict(ident=ident, iota3=iota3.astype(ml_dtypes.bfloat16), iota16=iota16)


def host_weights(inp):
    w = {}
    w["ada_w"] = np.ascontiguousarray(inp["ada_w"]).reshape(4, 8, 128, 6144)
    w["ada_b"] = np.ascontiguousarray(inp["ada_b"].reshape(4, 48, 128).transpose(2, 0, 1))
    w["ln_g"] = np.ascontiguousarray(inp["ln_g"].reshape(4, 2, 8, 128).transpose(3, 0, 1, 2))
    w["ln_b"] = np.ascontiguousarray(inp["ln_b"].reshape(4, 2, 8, 128).transpose(3, 0, 1, 2))
    w["peer_wq"] = np.ascontiguousarray(inp["peer_wq"])
    w["peer_keysT"] = np.ascontiguousarray(inp["peer_keys"].transpose(0, 1, 3, 2))
    pu = inp["peer_u"].reshape(4, 128, 128, 8, 128).transpose(0, 1, 4, 3, 2)
    w["peer_u"] = np.ascontiguousarray(pu).reshape(4, 128 * 128, 1024)
    w["peer_v"] = np.ascontiguousarray(inp["peer_v"])
    w.update(host_consts())
    return w


def host_core_inputs(inp, b0, NB):
    cT = np.zeros((NRP, D), np.float32)
    cT[:NB] = inp["c"][b0:b0 + NB]
    cT[NB] = inp["c_ctx"]
    cT = np.ascontiguousarray(cT.reshape(NRP, 8, 128).transpose(2, 1, 0))
    tok = np.concatenate([inp["ctx"][b0:b0 + NB], inp["x"][b0:b0 + NB]], axis=1)
    return dict(cT=cT, hT0=fm(tok))


def rope_tables():
    rows = LAT // 64
    row = np.repeat(np.arange(rows, dtype=np.float32), 64)
    col = np.tile(np.arange(64, dtype=np.float32), rows)
    inv = (np.float32(10000.0) ** (-np.arange(16, dtype=np.float32) / np.float32(16))).astype(np.float32)
    ang = np.concatenate([row[:, None] * inv, col[:, None] * inv], axis=-1).astype(np.float32)
    cos, sin = np.cos(ang).astype(np.float32), np.sin(ang).astype(np.float32)
    p = np.arange(128)
    r = p % 64
    i = r // 2
    sign = np.where(r % 2 == 0, -1.0, 1.0).astype(np.float32)
    C = np.ones((128, NT), np.float32)
    S = np.zeros((128, NT), np.float32)
    C[:, CTX:] = cos[:, i].T
    S[:, CTX:] = sin[:, i].T * sign[:, None]
    return np.stack([C * np.float32(0.125), S * np.float32(0.125), C, S]).astype(np.float32)


def host_weights_attn(inp):
    w = {}
    w["da_w_in"] = np.ascontiguousarray(inp["da_w_in"])
    qk = inp["da_w_in"][:, :, :2048]
    w["da_w_sw"] = np.ascontiguousarray(qk.reshape(2, 1024, 1024, 2)[:, :, :, ::-1].reshape(2, 1024, 2048))
    w["da_w_out"] = np.ascontiguousarray(inp["da_w_out"])
    w["da_lam_q"] = np.ascontiguousarray(inp["da_lam_q"].reshape(2, 128))
    w["da_lam_k"] = np.ascontiguousarray(inp["da_lam_k"].reshape(2, 128))
    w["da_subln"] = np.ascontiguousarray(inp["da_subln"].reshape(2, 128, 1))
    w["rope"] = rope_tables()
    return w


def qblocks(need_ctx):
    out = []
    if need_ctx:
        out.append((0, 256, 2))
    for i in range(4):
        out.append((CTX + i * 512, 512, 18))
    return out


def attention(k, G, li, Hin, Hout, need_ctx):
    import math
    slot = li // 3
    lam_init = 0.8 - 0.6 * math.exp(-0.3 * li)
    NB = G.NB
    cast_bf16(k, G.WinB, G.da_w_in[slot], 1024, 3072)
    cast_bf16(k, G.WswB, G.da_w_sw[slot], 1024, 2048)
    cast_bf16(k, G.WoutB, G.da_w_out[slot], 1024, 1024)
    k.push()
    lq = k.sb("lq", [128, 128])
    lk = k.sb("lk", [128, 128])
    l2 = k.sb("l2", [128, 2])
    neglam = k.sb("neglam", [128, 1])
    subg = k.sb("subg", [128, 1])
    ones_b = k.sb("ones_b", [128, 128], BF16)
    ones128 = k.sb("ones128", [128, 128])
    k.memset("dve", ones_b[:], 1.0, writes=[ones_b])
    k.memset("dve", ones128[:], 1.0 / 128, writes=[ones128])
    k.dma("sp", lq[:], G.da_lam_q.h[slot:slot + 1, :].to_broadcast([128, 128]), reads=[G.da_lam_q], writes=[lq])
    k.dma("sp", lk[:], G.da_lam_k.h[slot:slot + 1, :].to_broadcast([128, 128]), reads=[G.da_lam_k], writes=[lk])
    k.dma("sp", subg[:], G.da_subln[slot], reads=[G.da_subln], writes=[subg])
    k.tt("dve", lq[:], lq[:], lk[:], ALU.mult, [lq, lk], [lq])
    k.op("dve", lambda e: e.tensor_reduce(l2[:], lq[:].rearrange("p (c d) -> p c d", c=2), AX.X, ALU.add), [lq], [l2])
    k.actf(l2[:], l2[:], AF.Exp, [l2], [l2])
    k.tt("dve", neglam[:], l2[:, 1:2], l2[:, 0:1], ALU.subtract, [l2], [neglam])
    k.ts("dve", neglam[:], neglam[:], -lam_init, None, ALU.add, reads=[neglam], writes=[neglam])
    k.ts("dve", subg[:], subg[:], 1.0 - lam_init, None, ALU.mult, reads=[subg], writes=[subg])
    OnT = k.sb("OnT", [128, 8, NT], BF16)
    for b in range(NB):
        k.push()
        uTb = k.sb("uTb", [128, 8, NT], BF16)
        hblk = [k.sb("hblk", [128, 8, 256]) for _ in range(2)]
        for blk in range(NT // 256):
            hb = hblk[blk % 2]
            row = NB if blk == 0 else b
            k.dma("sp", hb[:], Hin[b, :, :, blk * 256:(blk + 1) * 256], reads=[Hin], writes=[hb])
            for c in range(8):
                k.actf(uTb[:, c, blk * 256:(blk + 1) * 256], hb[:, c, :], AF.Identity, [hb, G.modT, G.mod1], [uTb],
                       scale=G.mod1[:, li, seg(1, c), row:row + 1], bias=G.modT[:, li, seg(0, c), row:row + 1])
        Ct = k.sb("Ct", [128, NT])
        St = k.sb("St", [128, NT])
        wch = [k.sb("wch", [128, 8, 128], BF16) for _ in range(2)]
        wsc = [k.sb("wsc", [128, 8, 128], BF16) for _ in range(2)]
        orow = [k.sb("orow", [128, NT], BF16) for _ in range(2)]
        t1 = [k.sb("t1", [128, 512]) for _ in range(2)]
        t2 = [k.sb("t2", [128, 512]) for _ in range(2)]
        psa = [k.ps("psa", [128, 512]) for _ in range(2)]
        psb = [k.ps("psb", [128, 512]) for _ in range(2)]
        psv = [k.ps("psv", [128, 512]) for _ in range(2)]
        WinV = G.WinB.h.rearrange("(k p) n -> p k n", p=128)
        WswV = G.WswB.h.rearrange("(k p) n -> p k n", p=128)
        n = 0
        for grp in range(2):
            k.dma("sp", Ct[:], G.rope[2 * grp], reads=[G.rope], writes=[Ct])
            k.dma("sp", St[:], G.rope[2 * grp + 1], reads=[G.rope], writes=[St])
            for ch in range(8):
                col0 = grp * 1024 + ch * 128
                w_, ws_ = wch[ch % 2], wsc[ch % 2]
                k.dma("sp", w_[:], WinV[:, :, col0:col0 + 128], reads=[G.WinB], writes=[w_])
                k.dma("sp", ws_[:], WswV[:, :, col0:col0 + 128], reads=[G.WswB], writes=[ws_])
                orw = orow[ch % 2]
                for (t0, N) in [(0, 256)] + [(CTX + i * 512, 512) for i in range(4)]:
                    pa, pb = psa[n % 2], psb[n % 2]
                    a1, a2 = t1[n % 2], t2[n % 2]
                    n += 1
                    for kc in range(8):
                        k.mm(pa[:, 0:N], w_[:, kc, :], uTb[:, kc, t0:t0 + N], kc == 0, kc == 7, [w_, uTb], [pa])
                    for kc in range(8):
                        k.mm(pb[:, 0:N], ws_[:, kc, :], uTb[:, kc, t0:t0 + N], kc == 0, kc == 7, [ws_, uTb], [pb])
                    k.tt("dve", a1[:, 0:N], pa[:, 0:N], Ct[:, t0:t0 + N], ALU.mult, [pa, Ct], [a1])
                    k.tt("dve", a2[:, 0:N], pb[:, 0:N], St[:, t0:t0 + N], ALU.mult, [pb, St], [a2])
                    k.tt("pool", orw[:, t0:t0 + N], a1[:, 0:N], a2[:, 0:N], ALU.add, [a1, a2], [orw])
                dst = G.QT_d if grp == 0 else G.KT_d
                k.dma("act", dst[b, ch, :, :], orw[:], reads=[orw], writes=[dst])
        wv = k.sb("wv", [128, 8, 1024], BF16)
        k.dma("sp", wv[:], WinV[:, :, 2048:3072], reads=[G.WinB], writes=[wv])
        vrow = [k.sb("vrow", [128, 1024], BF16) for _ in range(2)]
        for tl in range(NT // 128):
            vr = vrow[tl % 2]
            for cb in range(2):
                pv = psv[cb]
                for kc in range(8):
                    k.mm(pv[:], uTb[:, kc, tl * 128:(tl + 1) * 128], wv[:, kc, cb * 512:(cb + 1) * 512], kc == 0, kc == 7,
                         [uTb, wv], [pv])
                k.copy("act" if cb == 0 else "dve", vr[:, cb * 512:(cb + 1) * 512], pv[:], [pv], [vr])
            k.dma("act", G.V_d[b, tl * 128:(tl + 1) * 128, :], vr[:], reads=[vr], writes=[G.V_d])
        k.pop()
        k.push()
        Qh = [k.sb("Qh", [128, NT], BF16) for _ in range(2)]
        Kh = [k.sb("Kh", [128, NT], BF16) for _ in range(2)]
        Vh = [k.sb("Vh", [128, 18, 128], BF16) for _ in range(2)]
        PT = [k.sb("PT", [128, 512], BF16) for _ in range(3)]
        psS = [k.ps("psS", [128, 512]) for _ in range(2)]
        psOc = [k.ps("psOc", [128, 512]) for _ in range(2)]
        psZ = [k.ps("psZ", [128, 512]) for _ in range(2)]
        psM = k.ps("psM", [128, 512])
        rz = [k.sb("rz", [128, 512]) for _ in range(2)]
        tO = [k.sb("tO", [128, 512]) for _ in range(2)]
        osb = k.sb("osb", [128, 512])
        sqb = k.sb("sqb", [128, 512])
        rinv = k.sb("rinv", [128, 512])
        ns = 0
        for h in range(8):
            q_, k_, v_ = Qh[h % 2], Kh[h % 2], Vh[h % 2]
            k.dma("sp", q_[:], G.QT_d[b, h, :, :], reads=[G.QT_d], writes=[q_])
            k.dma("sp", k_[:], G.KT_d[b, h, :, :], reads=[G.KT_d], writes=[k_])
            k.dma("sp", v_[:], G.V_d[b, :, h * 128:(h + 1) * 128].rearrange("(t p) e -> p t e", p=128),
                  reads=[G.V_d], writes=[v_])
            for (t0, N, nkt) in qblocks(need_ctx):
                for c in range(2):
                    def smm(kt_, n_):
                        k.mm(psS[n_ % 2][:, 0:N], k_[c * 64:(c + 1) * 64, kt_ * 128:(kt_ + 1) * 128], q_[c * 64:(c + 1) * 64, t0:t0 + N],
                             True, True, [k_, q_], [psS[n_ % 2]])
                    smm(0, ns)
                    for kt in range(nkt):
                        pS = psS[ns % 2]
                        pt = PT[ns % 3]
                        ns += 1
                        if kt + 1 < nkt:
                            smm(kt + 1, ns)
                        k.actf(pt[:, 0:N], pS[:, 0:N], AF.Exp, [pS], [pt])
                        k.mm(psOc[c][:, 0:N], v_[:, kt, :], pt[:, 0:N], kt == 0, kt == nkt - 1, [v_, pt], [psOc[c]])
                        k.mm(psZ[c][:, 0:N], ones_b[:], pt[:, 0:N], kt == 0, kt == nkt - 1, [ones_b, pt], [psZ[c]])
                    k.op("dve", lambda e: e.reciprocal(rz[c][:, 0:N], psZ[c][:, 0:N]), [psZ[c]], [rz[c]])
                    k.tt("dve", tO[c][:, 0:N], psOc[c][:, 0:N], rz[c][:, 0:N], ALU.mult, [psOc[c], rz[c]], [tO[c]])
                k.stt(osb[:, 0:N], tO[1][:, 0:N], neglam[:, 0:1], tO[0][:, 0:N], ALU.mult, ALU.add, [tO[0], tO[1], neglam], [osb])
                k.actf(sqb[:, 0:N], osb[:, 0:N], AF.Square, [osb], [sqb])
                k.mm(psM[:, 0:N], ones128[:], sqb[:, 0:N], True, True, [ones128, sqb], [psM])
                k.actf(rinv[:, 0:N], psM[:, 0:N], AF.Sqrt, [psM], [rinv], bias=G.epsrms[:, 0:1])
                k.op("dve", lambda e: e.reciprocal(rinv[:, 0:N], rinv[:, 0:N]), [rinv], [rinv])
                k.tt("dve", osb[:, 0:N], osb[:, 0:N], rinv[:, 0:N], ALU.mult, [osb, rinv], [osb])
                k.ts("dve", OnT[:, h, t0:t0 + N], osb[:, 0:N], subg[:, 0:1], None, ALU.mult, reads=[osb, subg], writes=[OnT])
        k.pop()
        k.push()
        wo = k.sb("wo", [128, 8, 1024], BF16)
        k.dma("sp", wo[:], G.WoutB.h.rearrange("(k p) n -> p k n", p=128), reads=[G.WoutB], writes=[wo])
        hT = k.sb("hT", [128, 8, 256])
        rr = k.sb("rr", [128, 8, 256])
        tmp = dict(sq=k.sb("sq", [128, 8, 256]), mean=k.sb("mean", [128, 256]), rstd=k.sb("rstd", [128, 256]))
        psO = [k.ps("psO", [128, 512]) for _ in range(4)]
        psL = k.ps("psL", [128, 512])
        for blk in range(NT // 256):
            if blk == 0 and not need_ctx:
                continue
            row = NB if blk == 0 else b
            t0 = blk * 256
            k.dma("sp", hT[:], Hin[b, :, :, t0:t0 + 256], reads=[Hin], writes=[hT])
            for dc in range(8):
                po = psO[dc // 2][:, (dc % 2) * 256:(dc % 2 + 1) * 256]
                for h in range(8):
                    k.mm(po, wo[:, h, dc * 128:(dc + 1) * 128], OnT[:, h, t0:t0 + 256], h == 0, h == 7, [wo, OnT], [psO[dc // 2]])
            k.actf(hT[:].rearrange("p c t -> p (c t)"), hT[:].rearrange("p c t -> p (c t)"), AF.Copy, [hT], [hT], scale=ALPHA)
            for c in range(8):
                k.stt(rr[:, c, :], psO[c // 2][:, (c % 2) * 256:(c % 2 + 1) * 256], G.modT[:, li, seg(2, c), row:row + 1],
                      hT[:, c, :], ALU.mult, ALU.add, [psO[c // 2], G.modT, hT], [rr])
            layer_norm_fm(k, G, rr, 256, li, 0, rr, psL, tmp)
            k.dma("act", Hout[b, :, :, t0:t0 + 256], rr[:], reads=[rr], writes=[Hout])
        k.pop()
    k.pop()


def host_weights_s5(inp):
    w = {}

    def t_gp(a):
        return a.reshape(2, 32, 2, 64).transpose(2, 3, 0, 1).reshape(128, 2, 32)
    lr = t_gp(inp["s5_lam_re"][0])
    li_ = t_gp(inp["s5_lam_im"][0])
    ld = inp["s5_log_dt"][0].reshape(2, 32, 2).transpose(2, 0, 1)
    ld = np.broadcast_to(ld[:, None], (2, 64, 2, 32)).reshape(128, 2, 32)
    w["s5p"] = np.ascontiguousarray(np.stack([lr, li_, ld], axis=-1)).astype(np.float32)

    def t_b(a):
        return a.reshape(2, 32, 2, 64, 16).transpose(2, 3, 0, 1, 4).reshape(128, 2, 32, 16)

    def t_c(a):
        return a.reshape(2, 32, 2, 16, 64).transpose(2, 4, 0, 1, 3).reshape(128, 2, 32, 16)
    w["s5bc"] = np.ascontiguousarray(np.stack([t_b(inp["s5_b_re"][0]), t_b(inp["s5_b_im"][0]),
                                               t_c(inp["s5_c_re"][0]), t_c(inp["s5_c_im"][0])], axis=3)).astype(np.float32)
    w["s5_d"] = np.ascontiguousarray(inp["s5_d"][0].reshape(8, 128).T)
    w["s5_w_glu"] = np.ascontiguousarray(inp["s5_w_glu"][0])
    w["tT"] = np.tile(np.arange(NT, dtype=np.float32)[None, :], (128, 1))
    return w


TWO_PI = 6.283185307179586
CW1 = 6.28125
CW2 = TWO_PI - CW1
MAGIC = 12582912.0


def range_reduce(k, out, ang, nt, reads, e="dve"):
    k.ts(e, nt, ang, 1.0 / TWO_PI, MAGIC, ALU.mult, ALU.add, reads=reads[0], writes=reads[1])
    k.ts(e, nt, nt, MAGIC, None, ALU.subtract, reads=reads[1], writes=reads[1])
    k.stt(out, nt, -CW1, ang, ALU.mult, ALU.add, reads[0] + reads[1], reads[2])
    k.stt(out, nt, -CW2, out, ALU.mult, ALU.add, reads[1] + reads[2], reads[2])


def rev_tprime(lo, n):
    if lo < CTX:
        return (CTX - lo - n, CTX - lo)
    return (2 * CTX + LAT - lo - n, 2 * CTX + LAT - lo)


def s5_layer(k, G, li, Hin, Hout, need_ctx):
    NB = G.NB
    cast_bf16(k, G.WgluB, G.s5_w_glu.h, 1024, 2048)
    k.push()
    hb = [k.sb("hb", [128, 8, 256]) for _ in range(2)]
    ub = [k.sb("ub", [128, 8, 256], BF16) for _ in range(2)]
    n = 0
    for b in range(NB):
        for blk in range(NT // 256):
            row = NB if blk == 0 else b
            h_, u_ = hb[n % 2], ub[n % 2]
            n += 1
            k.dma("sp", h_[:], Hin[b, :, :, blk * 256:(blk + 1) * 256], reads=[Hin], writes=[h_])
            for c in range(8):
                k.actf(u_[:, c, :], h_[:, c, :], AF.Identity, [h_, G.modT, G.mod1], [u_],
                       scale=G.mod1[:, li, seg(1, c), row:row + 1], bias=G.modT[:, li, seg(0, c), row:row + 1])
            k.dma("act", G.UT_d[b, :, :, blk * 256:(blk + 1) * 256].rearrange("c p t -> p c t"), u_[:], reads=[u_], writes=[G.UT_d])
    k.pop()
    k.push()
    P3 = k.sb("P3", [128, 2, 32, 3])
    BC = k.sb("BC", [128, 2, 32, 4, 16])
    k.dma("sp", P3[:], G.s5p[:, :, :, :], reads=[G.s5p], writes=[P3])
    k.dma("sp", BC[:], G.s5bc[:, :, :, :, :], reads=[G.s5bc], writes=[BC])
    sd = k.sb("sd", [128, 8])
    k.dma("sp", sd[:], G.s5_d[:, :], reads=[G.s5_d], writes=[sd])
    tT = k.sb("tT", [128, NT])
    k.dma("sp", tT[:], G.tT_d[:, :], reads=[G.tT_d], writes=[tT])
    halfpi = k.sb("halfpi", [128, 1])
    k.memset("dve", halfpi[:], TWO_PI / 4, writes=[halfpi])
    sh = [128, 2, 32]
    nm = ["dt", "mag", "th", "nt", "thp", "sn", "cs", "are", "aim", "nr", "den", "kre", "kim", "t1", "t2"]
    V = {x: k.sb("p_" + x, sh) for x in nm}
    lr, lim, ld = P3[:, :, :, 0], P3[:, :, :, 1], P3[:, :, :, 2]
    k.actf(V["dt"][:], ld, AF.Exp, [P3], [V["dt"]])
    k.tt("dve", V["t1"][:], lr, V["dt"][:], ALU.mult, [P3, V["dt"]], [V["t1"]])
    k.actf(V["mag"][:], V["t1"][:], AF.Exp, [V["t1"]], [V["mag"]])
    k.tt("dve", V["th"][:], lim, V["dt"][:], ALU.mult, [P3, V["dt"]], [V["th"]])
    range_reduce(k, V["thp"][:], V["th"][:], V["nt"][:], ([V["th"]], [V["nt"]], [V["thp"]]))
    k.actf(V["sn"][:], V["thp"][:], AF.Sin, [V["thp"]], [V["sn"]])
    k.actf(V["t2"][:], V["thp"][:], AF.Abs, [V["thp"]], [V["t2"]])
    k.actf(V["cs"][:], V["t2"][:], AF.Sin, [V["t2"], halfpi], [V["cs"]], scale=-1.0, bias=halfpi[:, 0:1])
    k.tt("dve", V["are"][:], V["mag"][:], V["cs"][:], ALU.mult, [V["mag"], V["cs"]], [V["are"]])
    k.tt("dve", V["aim"][:], V["mag"][:], V["sn"][:], ALU.mult, [V["mag"], V["sn"]], [V["aim"]])
    k.ts("dve", V["nr"][:], V["are"][:], -1.0, None, ALU.add, reads=[V["are"]], writes=[V["nr"]])
    k.tt("dve", V["den"][:], lr, lr, ALU.mult, [P3], [V["den"]])
    k.tt("dve", V["t1"][:], lim, lim, ALU.mult, [P3], [V["t1"]])
    k.tt("dve", V["den"][:], V["den"][:], V["t1"][:], ALU.add, [V["den"], V["t1"]], [V["den"]])
    k.op("dve", lambda e: e.reciprocal(V["den"][:], V["den"][:]), [V["den"]], [V["den"]])
    k.tt("dve", V["t1"][:], V["nr"][:], lr, ALU.mult, [V["nr"], P3], [V["t1"]])
    k.tt("dve", V["t2"][:], V["aim"][:], lim, ALU.mult, [V["aim"], P3], [V["t2"]])
    k.tt("dve", V["kre"][:], V["t1"][:], V["t2"][:], ALU.add, [V["t1"], V["t2"]], [V["kre"]])
    k.tt("dve", V["kre"][:], V["kre"][:], V["den"][:], ALU.mult, [V["kre"], V["den"]], [V["kre"]])
    k.tt("dve", V["t1"][:], V["aim"][:], lr, ALU.mult, [V["aim"], P3], [V["t1"]])
    k.tt("dve", V["t2"][:], V["nr"][:], lim, ALU.mult, [V["nr"], P3], [V["t2"]])
    k.tt("dve", V["kim"][:], V["t1"][:], V["t2"][:], ALU.subtract, [V["t1"], V["t2"]], [V["kim"]])
    k.tt("dve", V["kim"][:], V["kim"][:], V["den"][:], ALU.mult, [V["kim"], V["den"]], [V["kim"]])
    sh4 = [128, 2, 32, 16]
    Bre = k.sb("Bre", sh4)
    Bim = k.sb("Bim", sh4)
    t4a = k.sb("t4a", sh4)
    nCim = k.sb("nCim", sh4)
    kreb = V["kre"][:].unsqueeze(3).to_broadcast(sh4)
    kimb = V["kim"][:].unsqueeze(3).to_broadcast(sh4)
    bre, bim, cre, cim = BC[:, :, :, 0, :], BC[:, :, :, 1, :], BC[:, :, :, 2, :], BC[:, :, :, 3, :]
    k.tt("dve", Bre[:], bre, kreb, ALU.mult, [BC, V["kre"]], [Bre])
    k.tt("dve", t4a[:], bim, kimb, ALU.mult, [BC, V["kim"]], [t4a])
    k.tt("dve", Bre[:], Bre[:], t4a[:], ALU.subtract, [Bre, t4a], [Bre])
    k.tt("dve", Bim[:], bim, kreb, ALU.mult, [BC, V["kre"]], [Bim])
    k.tt("dve", t4a[:], bre, kimb, ALU.mult, [BC, V["kim"]], [t4a])
    k.tt("dve", Bim[:], Bim[:], t4a[:], ALU.add, [Bim, t4a], [Bim])
    k.ts("dve", nCim[:], cim, -1.0, None, ALU.mult, reads=[BC], writes=[nCim])
    BD = [k.sb("BD", [128, 128]) for _ in range(2)]
    WB = [k.sb("WB", [128, 128], BF16) for _ in range(2)]
    WC = [k.sb("WC", [128, 128], BF16) for _ in range(2)]
    cosT = k.sb("cosT", [128, NT])
    sinT = k.sb("sinT", [128, NT])
    ntT = k.sb("ntT", [128, NT])
    rT = k.sb("rT", [128, 512])
    uch = [k.sb("uch", [128, NT], BF16) for _ in range(NB)]
    yacc = [k.sb("yacc", [128, NT]) for _ in range(NB)]
    zout = k.sb("zout", [128, NT], BF16)
    xb = [[k.sb("xre", [128, 512], BF16), k.sb("xim", [128, 512], BF16)] for _ in range(NB)]
    Wab = [{x: k.sb("w5" + x, [128, 512]) for x in ("a", "b")} for _ in range(2)]
    Wst = [{x: k.sb("w5" + x, [128, 512]) for x in ("cr", "ci", "zr", "zi")} for _ in range(NB)]
    stt_ = [{x: k.sb("st" + x, [128, 1]) for x in ("zr", "zi")} for _ in range(NB)]
    psB = [[k.ps("psBr", [128, 512]), k.ps("psBi", [128, 512])] for _ in range(2)]
    psY = [k.ps("psY", [128, 512]) for _ in range(2)]
    blocks = [(0, 256)] + [(CTX + i * 512, 512) for i in range(4)]
    nblk = 0
    for c in range(8):
        for b in range(NB):
            k.dma("sp", uch[b][:], G.UT_d[b, c, :, :], reads=[G.UT_d], writes=[uch[b]])
            k.memset("pool", yacc[b][:], 0.0, writes=[yacc[b]])
        for s_ in range(4):
            unit = 4 * c + s_
            for dr in range(2):
                for ri, Bsrc in ((0, Bre), (1, Bim)):
                    bd = BD[ri]
                    k.memset("pool", bd[:], 0.0, writes=[bd])
                    k.copy("pool", bd[0:64, 32 * s_:32 * s_ + 16], Bsrc[0:64, dr, unit, :], [Bsrc], [bd])
                    k.copy("pool", bd[64:128, 32 * s_ + 16:32 * s_ + 32], Bsrc[64:128, dr, unit, :], [Bsrc], [bd])
                    pt = psY[ri]
                    k.tr(pt[:, 0:128], bd[:], G.ident[:], [bd, G.ident], [pt])
                    k.copy("act", WB[ri][:], pt[:, 0:128], [pt], [WB[ri]])
                for ri, Csrc, Cap in ((0, BC, cre), (1, nCim, nCim[:])):
                    wc = WC[ri]
                    k.memset("pool", wc[:], 0.0, writes=[wc])
                    k.copy("pool", wc[0:64, 32 * s_:32 * s_ + 16], Cap[0:64, dr, unit, :], [Csrc], [wc])
                    k.copy("pool", wc[64:128, 32 * s_ + 16:32 * s_ + 32], Cap[64:128, dr, unit, :], [Csrc], [wc])
                k.ts("dve", cosT[:], tT[:], V["thp"][:, dr, unit:unit + 1], None, ALU.mult, reads=[tT, V["thp"]], writes=[cosT])
                range_reduce(k, sinT[:], cosT[:], ntT[:], ([cosT], [ntT], [sinT]))
                k.actf(ntT[:], sinT[:], AF.Abs, [sinT], [ntT])
                k.actf(cosT[:], ntT[:], AF.Sin, [ntT, halfpi], [cosT], scale=-1.0, bias=halfpi[:, 0:1])
                k.actf(sinT[:], sinT[:], AF.Sin, [sinT], [sinT])
                k.ts("dve", rT[:], tT[:, 0:512], 0.0, V["mag"][:, dr, unit:unit + 1], ALU.mult, ALU.add, reads=[tT, V["mag"]], writes=[rT])
                for bi_, (lo, N) in enumerate(blocks):
                    for b in range(NB):
                        w = dict(Wst[b])
                        w.update(Wab[nblk % 2])
                        pB = psB[nblk % 2]
                        pY = psY[nblk % 2]
                        xre, xim = xb[b]
                        nblk += 1
                        if dr == 0:
                            a0, a1 = lo, lo + N
                            usl = uch[b][:, a0:a1]
                            ysl = yacc[b][:, a0:a1]
                        else:
                            a0, a1 = rev_tprime(lo, N)
                            usl = uch[b][:, a0:a1][:, ::-1]
                            ysl = yacc[b][:, a0:a1][:, ::-1]
                        k.mm(pB[0][:, 0:N], WB[0][:], usl, True, True, [WB[0], uch[b]], [pB[0]])
                        k.mm(pB[1][:, 0:N], WB[1][:], usl, True, True, [WB[1], uch[b]], [pB[1]])
                        cs_, sn_ = cosT[:, lo:lo + N], sinT[:, lo:lo + N]
                        k.tt("dve", w["a"][:, 0:N], pB[0][:, 0:N], cs_, ALU.mult, [pB[0], cosT], [w["a"]])
                        k.tt("dve", w["b"][:, 0:N], pB[1][:, 0:N], sn_, ALU.mult, [pB[1], sinT], [w["b"]])
                        k.tt("pool", w["cr"][:, 0:N], w["a"][:, 0:N], w["b"][:, 0:N], ALU.add, [w["a"], w["b"]], [w["cr"]])
                        k.tt("dve", w["a"][:, 0:N], pB[1][:, 0:N], cs_, ALU.mult, [pB[1], cosT], [w["a"]])
                        k.tt("dve", w["b"][:, 0:N], pB[0][:, 0:N], sn_, ALU.mult, [pB[0], sinT], [w["b"]])
                        k.tt("pool", w["ci"][:, 0:N], w["a"][:, 0:N], w["b"][:, 0:N], ALU.subtract, [w["a"], w["b"]], [w["ci"]])
                        for src, dst in (("cr", "zr"), ("ci", "zi")):
                            st = stt_[b][dst]
                            init = 0.0 if bi_ == 0 else st[:, 0:1]
                            rd = [rT, w[src]] + ([] if bi_ == 0 else [st])
                            k.op("dve", lambda e: e.tensor_tensor_scan(w[dst][:, 0:N], rT[:, 0:N], w[src][:, 0:N], init,
                                                                         ALU.mult, ALU.add), rd, [w[dst]])
                            k.copy("act", st[:, 0:1], w[dst][:, N - 1:N], [w[dst]], [st])
                        k.tt("dve", w["a"][:, 0:N], w["zr"][:, 0:N], cs_, ALU.mult, [w["zr"], cosT], [w["a"]])
                        k.tt("dve", w["b"][:, 0:N], w["zi"][:, 0:N], sn_, ALU.mult, [w["zi"], sinT], [w["b"]])
                        k.tt("pool", xre[:, 0:N], w["a"][:, 0:N], w["b"][:, 0:N], ALU.subtract, [w["a"], w["b"]], [xre])
                        k.tt("dve", w["a"][:, 0:N], w["zr"][:, 0:N], sn_, ALU.mult, [w["zr"], sinT], [w["a"]])
                        k.tt("dve", w["b"][:, 0:N], w["zi"][:, 0:N], cs_, ALU.mult, [w["zi"], cosT], [w["b"]])
                        k.tt("pool", xim[:, 0:N], w["a"][:, 0:N], w["b"][:, 0:N], ALU.add, [w["a"], w["b"]], [xim])
                        k.mm(pY[:, 0:N], WC[0][:], xre[:, 0:N], True, False, [WC[0], xre], [pY])
                        k.mm(pY[:, 0:N], WC[1][:], xim[:, 0:N], False, True, [WC[1], xim], [pY])
                        k.tt("dve", ysl, pY[:, 0:N], ysl, ALU.add, [pY, yacc[b]], [yacc[b]])
        for b in range(NB):
            k.stt(yacc[b][:], uch[b][:], sd[:, c:c + 1], yacc[b][:], ALU.mult, ALU.add, [uch[b], sd, yacc[b]], [yacc[b]])
            k.actf(zout[:], yacc[b][:], AF.Gelu_apprx_tanh, [yacc[b]], [zout])
            k.dma("act", G.ZT_d[b, c, :, :], zout[:], reads=[zout], writes=[G.ZT_d])
    k.pop()
    k.push()
    wg = k.sb("wg", [128, 8, 2048], BF16)
    k.dma("sp", wg[:], G.WgluB.h.rearrange("(k p) n -> p k n", p=128), reads=[G.WgluB], writes=[wg])
    zT = [k.sb("zT", [128, 8, 256], BF16) for _ in range(2)]
    hT = k.sb("hT", [128, 8, 256])
    rr = k.sb("rr", [128, 8, 256])
    sg = [k.sb("sg", [128, 256]) for _ in range(2)]
    oc = [k.sb("oc", [128, 256]) for _ in range(2)]
    tmp = dict(sq=k.sb("sq", [128, 8, 256]), mean=k.sb("mean", [128, 256]), rstd=k.sb("rstd", [128, 256]))
    psV = [k.ps("psV", [128, 512]) for _ in range(2)]
    psL = k.ps("psL", [128, 512])
    n = 0
    for b, blk, t0, row in blocks_of(G, need_ctx):
        z_ = zT[n % 2]
        n += 1
        k.dma("sp", z_[:], G.ZT_d[b, :, :, t0:t0 + 256].rearrange("c p t -> p c t"), reads=[G.ZT_d], writes=[z_])
        k.dma("sp", hT[:], Hin[b, :, :, t0:t0 + 256], reads=[Hin], writes=[hT])
        k.actf(hT[:].rearrange("p c t -> p (c t)"), hT[:].rearrange("p c t -> p (c t)"), AF.Copy, [hT], [hT], scale=ALPHA)
        for dc in range(8):
            pv = psV[dc % 2]
            for kc in range(8):
                k.mm(pv[:, 0:256], wg[:, kc, dc * 128:(dc + 1) * 128], z_[:, kc, :], kc == 0, kc == 7, [wg, z_], [pv])
            for kc in range(8):
                k.mm(pv[:, 256:512], wg[:, kc, 1024 + dc * 128:1024 + (dc + 1) * 128], z_[:, kc, :], kc == 0, kc == 7, [wg, z_], [pv])
            s_, o_ = sg[dc % 2], oc[dc % 2]
            k.actf(s_[:], pv[:, 256:512], AF.Sigmoid, [pv], [s_])
            k.tt("dve", o_[:], pv[:, 0:256], s_[:], ALU.mult, [pv, s_], [o_])
            k.stt(rr[:, dc, :], o_[:], G.modT[:, li, seg(2, dc), row:row + 1], hT[:, dc, :], ALU.mult, ALU.add,
                  [o_, G.modT, hT], [rr])
        layer_norm_fm(k, G, rr, 256, li, 0, rr, psL, tmp)
        k.dma("act", Hout[b, :, :, t0:t0 + 256], rr[:], reads=[rr], writes=[Hout])
    k.pop()


def host_weights_hg(inp):
    w = {}
    w["hg_w_in"] = np.ascontiguousarray(inp["hg_w_in"][0])
    w["hg_w_out"] = np.ascontiguousarray(inp["hg_w_out"][0])
    w["hg_norm"] = np.ascontiguousarray(inp["hg_norm"][0].reshape(8, 128).T)
    w["hg_lb"] = np.ascontiguousarray(inp["hg_lb"].reshape(4, 8, 128).transpose(2, 1, 0))
    m = np.zeros((128, 64 + NT), np.float32)
    s_ = (np.arange(128) % 64)[:, None]
    t_ = np.arange(64)[None, :]
    m[:, 0:64] = (s_ <= t_).astype(np.float32)
    m[:, 64:] = (np.arange(NT) % 64 != 0).astype(np.float32)[None, :]
    w["hgmask"] = m.astype(ml_dtypes.bfloat16)
    return w


def out_proj_ln(k, G, li, b, Hin, Hout, OnT, WB_, need_ctx):
    NB = G.NB
    k.push()
    wo = k.sb("wo", [128, 8, 1024], BF16)
    k.dma("sp", wo[:], WB_.h.rearrange("(k p) n -> p k n", p=128), reads=[WB_], writes=[wo])
    hT = k.sb("hT", [128, 8, 256])
    rr = k.sb("rr", [128, 8, 256])
    tmp = dict(sq=k.sb("sq", [128, 8, 256]), mean=k.sb("mean", [128, 256]), rstd=k.sb("rstd", [128, 256]))
    psO = [k.ps("psO", [128, 512]) for _ in range(4)]
    psL = k.ps("psL", [128, 512])
    for blk in range(NT // 256):
        if blk == 0 and not need_ctx:
            continue
        row = NB if blk == 0 else b
        t0 = blk * 256
        k.dma("sp", hT[:], Hin[b, :, :, t0:t0 + 256], reads=[Hin], writes=[hT])
        for dc in range(8):
            po = psO[dc // 2][:, (dc % 2) * 256:(dc % 2 + 1) * 256]
            for h in range(8):
                k.mm(po, wo[:, h, dc * 128:(dc + 1) * 128], OnT[:, h, t0:t0 + 256], h == 0, h == 7, [wo, OnT], [psO[dc // 2]])
        k.actf(hT[:].rearrange("p c t -> p (c t)"), hT[:].rearrange("p c t -> p (c t)"), AF.Copy, [hT], [hT], scale=ALPHA)
        for c in range(8):
            k.stt(rr[:, c, :], psO[c // 2][:, (c % 2) * 256:(c % 2 + 1) * 256], G.modT[:, li, seg(2, c), row:row + 1],
                  hT[:, c, :], ALU.mult, ALU.add, [psO[c // 2], G.modT, hT], [rr])
        layer_norm_fm(k, G, rr, 256, li, 0, rr, psL, tmp)
        k.dma("act", Hout[b, :, :, t0:t0 + 256], rr[:], reads=[rr], writes=[Hout])
    k.pop()


def tp_blocks(dr, step=512):
    out = []
    los = [(0, 256)] if step >= 256 else [(i, step) for i in range(0, 256, step)]
    los = los + [(CTX + i, step) for i in range(0, LAT, step)]
    for lo, N in los:
        if dr == 0:
            out.append((lo, N, lo, lo + N))
        else:
            a0, a1 = rev_tprime(lo, N)
            out.append((lo, N, a0, a1))
    return out


def hgrn2_layer(k, G, li, Hin, Hout, need_ctx):
    NB = G.NB
    CH = 64
    NCH = NT // CH
    cast_bf16(k, G.WhgB, G.hg_w_in.h, 1024, 5120)
    cast_bf16(k, G.WhgoB, G.hg_w_out.h, 1024, 1024)
    k.push()
    lbe = k.sb("lbe", [128, 8, 4])
    lbs = k.sb("lbs", [128, 8])
    lb = k.sb("lb", [128, 8])
    oml = k.sb("oml", [128, 8])
    ng = k.sb("ng", [128, 8])
    k.dma("sp", lbe[:], G.hg_lb[:, :, :], reads=[G.hg_lb], writes=[lbe])
    k.dma("sp", ng[:], G.hg_norm[:, :], reads=[G.hg_norm], writes=[ng])
    k.actf(lbe[:], lbe[:], AF.Exp, [lbe], [lbe])
    k.op("dve", lambda e: e.tensor_reduce(lbs[:], lbe[:], AX.X, ALU.add), [lbe], [lbs])
    k.op("dve", lambda e: e.reciprocal(lbs[:], lbs[:]), [lbs], [lbs])
    k.op("dve", lambda e: e.tensor_reduce(lb[:], lbe[:, :, 1:li + 1], AX.X, ALU.add), [lbe], [lb])
    k.tt("dve", lb[:], lb[:], lbs[:], ALU.mult, [lb, lbs], [lb])
    k.ts("dve", oml[:], lb[:], -1.0, 1.0, ALU.mult, ALU.add, reads=[lb], writes=[oml])
    msk = k.sb("msk", [128, 64 + NT], BF16)
    k.dma("sp", msk[:], G.hgmask_d[:, :], reads=[G.hgmask_d], writes=[msk])
    ones128 = k.sb("ones128", [128, 128])
    k.memset("dve", ones128[:], 1.0 / 128, writes=[ones128])
    identb = k.sb("identb", [128, 128], BF16)
    k.copy("dve", identb[:], G.ident[:], [G.ident], [identb])
    Jb = k.sb("Jb", [128, 128], BF16)
    k.copy("dve", Jb[:], G.ident[:, ::-1], [G.ident], [Jb])
    WV = G.WhgB.h.rearrange("(k p) n -> p k n", p=128)
    for b in range(NB):
        k.push()
        OgT = k.sb("OgT", [128, 8, NT], BF16)
        k.push()
        uTb = k.sb("uTb", [128, 8, NT], BF16)
        hblk = [k.sb("hblk", [128, 8, 256]) for _ in range(2)]
        for blk in range(NT // 256):
            hb = hblk[blk % 2]
            row = NB if blk == 0 else b
            k.dma("sp", hb[:], Hin[b, :, :, blk * 256:(blk + 1) * 256], reads=[Hin], writes=[hb])
            for c in range(8):
                k.actf(uTb[:, c, blk * 256:(blk + 1) * 256], hb[:, c, :], AF.Identity, [hb, G.modT, G.mod1], [uTb],
                       scale=G.mod1[:, li, seg(1, c), row:row + 1], bias=G.modT[:, li, seg(0, c), row:row + 1])
        wch = [k.sb("wch", [128, 8, 128], BF16) for _ in range(2)]
        orow = [k.sb("orow", [128, NT]) for _ in range(2)]
        psa = [k.ps("psa", [128, 512]) for _ in range(2)]
        psv = [k.ps("psv", [128, 512]) for _ in range(2)]
        n = 0
        nw = 0
        for kind, cbase in ((0, 0), (1, 1024), (2, 2048), (3, 4096)):
            for h in range(8):
                w_ = wch[nw % 2]
                orw = orow[nw % 2]
                nw += 1
                k.dma("sp", w_[:], WV[:, :, cbase + h * 128:cbase + (h + 1) * 128], reads=[G.WhgB], writes=[w_])
                for (t0, N) in [(0, 256)] + [(CTX + i * 512, 512) for i in range(4)]:
                    pa = psa[n % 2]
                    n += 1
                    for kc in range(8):
                        k.mm(pa[:, 0:N], w_[:, kc, :], uTb[:, kc, t0:t0 + N], kc == 0, kc == 7, [w_, uTb], [pa])
                    k.copy("act" if n % 2 == 0 else "dve", orw[:, t0:t0 + N], pa[:, 0:N], [pa], [orw])
                k.dma("act", G.PQ_d[b, kind, h, :, :], orw[:], reads=[orw], writes=[G.PQ_d])
        wv = k.sb("wv", [128, 8, 1024], BF16)
        k.dma("sp", wv[:], WV[:, :, 3072:4096], reads=[G.WhgB], writes=[wv])
        vrow = [k.sb("vrow", [128, 1024], BF16) for _ in range(2)]
        nv = 0
        for dr in range(1):
            for (lo, N, a0, a1) in tp_blocks(dr, 128):
                vr = vrow[nv % 2]
                nv += 1
                for cb in range(2):
                    pv = psv[cb]
                    for kc in range(8):
                        lhs = uTb[:, kc, a0:a1] if dr == 0 else uTb[:, kc, a0:a1][:, ::-1]
                        k.mm(pv[:], lhs, wv[:, kc, cb * 512:(cb + 1) * 512], kc == 0, kc == 7, [uTb, wv], [pv])
                    k.copy("act" if cb == 0 else "dve", vr[:, cb * 512:(cb + 1) * 512], pv[:], [pv], [vr])
                dst = G.V_d if dr == 0 else G.Vr_d
                k.dma("act", dst[b, lo:lo + 128, :], vr[:], reads=[vr], writes=[dst])
        k.pop()
        k.push()
        qs = k.sb("qs", [128, NT])
        raw = k.sb("raw", [128, NT])
        tA = k.sb("tA", [128, NT])
        tB = k.sb("tB", [128, NT])
        tC = k.sb("tC", [128, NT])
        oacc = k.sb("oacc", [128, NT])
        qtb = [k.sb("qtb", [128, NT], BF16) for _ in range(2)]
        ktb = [k.sb("ktb", [128, NT], BF16) for _ in range(2)]
        khb = [k.sb("khb", [128, NT], BF16) for _ in range(2)]
        ecl = [k.sb("ecl", [128, NCH]) for _ in range(2)]
        khtok = [k.sb("khtok", [128, NT // 128, 128], BF16) for _ in range(2)]
        Vh = [k.sb("Vh", [128, NT // 128, 128], BF16) for _ in range(2)]
        S = [k.sb("S", [128, 128]) for _ in range(2)]
        Sb = [k.sb("Sb", [128, 128], BF16) for _ in range(2)]
        Am = [k.sb("Am", [128, 64], BF16) for _ in range(4)]
        psT = k.ps("psT", [128, 512], BF16)
        psAd = [k.ps("psA", [128, 512]) for _ in range(2)]
        psOo = [k.ps("psOo", [128, 512]) for _ in range(2)]
        psSd = [k.ps("psS", [128, 512]) for _ in range(2)]
        psM = k.ps("psM", [128, 512])
        sqb = k.sb("sqb", [128, 512])
        rinv = k.sb("rinv", [128, 512])
        psA_r = psAd
        psS_r = psSd
        for h in range(8):
            k.dma("sp", raw[:], G.PQ_d[b, 0, h, :, :], reads=[G.PQ_d], writes=[raw])
            k.actf(qs[:], raw[:], AF.Silu, [raw], [qs])
            k.memset("pool", oacc[:], 0.0, writes=[oacc])
            for dr in range(2):
                if dr == 0:
                    k.dma("sp", Vh[dr][:], G.V_d[b, :, h * 128:(h + 1) * 128].rearrange("(t p) e -> p t e", p=128),
                          reads=[G.V_d], writes=[Vh[dr]])
                else:
                    for tg in range(0, 18, 4):
                        nn = min(4, 18 - tg)
                        for j in range(nn):
                            tau = tg + j
                            src = 1 - tau if tau < 2 else 19 - tau
                            k.mm(psM[:, j * 128:(j + 1) * 128], Jb[:], Vh[0][:, src, :], True, True, [Jb, Vh[0]], [psM])
                        k.copy("act", Vh[1][:, tg:tg + nn, :].rearrange("p a b -> p (a b)"), psM[:, 0:nn * 128], [psM], [Vh[1]])
                k.dma("sp", raw[:], G.PQ_d[b, 1 + dr, h, :, :], reads=[G.PQ_d], writes=[raw])
                if dr == 0:
                    k.actf(tA[:], raw[:], AF.Sigmoid, [raw], [tA])
                else:
                    k.actf(tA[:, 0:CTX], raw[:, 0:CTX][:, ::-1], AF.Sigmoid, [raw], [tA])
                    k.actf(tA[:, CTX:NT], raw[:, CTX:NT][:, ::-1], AF.Sigmoid, [raw], [tA])
                k.ts("dve", tA[:], tA[:], oml[:, h:h + 1], lb[:, h:h + 1], ALU.mult, ALU.add, reads=[tA, oml, lb], writes=[tA])
                k.actf(tB[:], tA[:], AF.Ln, [tA], [tB])
                k.ts("dve", tA[:], tA[:], -1.0, 1.0, ALU.mult, ALU.add, reads=[tA], writes=[tA])
                k.op("dve", lambda e: e.tensor_tensor_scan(tC[:], msk[:, 64:64 + NT], tB[:], 0.0, ALU.mult, ALU.add), [msk, tB], [tC])
                k.actf(tB[:], tC[:], AF.Exp, [tC], [tB])
                k.actf(tC[:], tC[:], AF.Exp, [tC], [tC], scale=-1.0)
                if dr == 0:
                    k.tt("dve", qtb[dr][:], qs[:], tB[:], ALU.mult, [qs, tB], [qtb[dr]])
                else:
                    k.tt("dve", qtb[dr][:, 0:CTX], qs[:, 0:CTX][:, ::-1], tB[:, 0:CTX], ALU.mult, [qs, tB], [qtb[dr]])
                    k.tt("dve", qtb[dr][:, CTX:NT], qs[:, CTX:NT][:, ::-1], tB[:, CTX:NT], ALU.mult, [qs, tB], [qtb[dr]])
                k.copy("act", ecl[dr][:], tB[:, CH - 1::CH], [tB], [ecl[dr]])
                k.tt("dve", tA[:], tA[:], tC[:], ALU.mult, [tA, tC], [tA])
                k.copy("pool", ktb[dr][:], tA[:], [tA], [ktb[dr]])
                k.tt("dve", khb[dr][:].rearrange("p (c t) -> p c t", t=CH), tA[:].rearrange("p (c t) -> p c t", t=CH),
                     ecl[dr][:].unsqueeze(2).to_broadcast([128, NCH, CH]), ALU.mult, [tA, ecl[dr]], [khb[dr]])
                for tl in range(NT // 128):
                    k.tr(psT[:, (tl % 4) * 128:(tl % 4 + 1) * 128], khb[dr][:, tl * 128:(tl + 1) * 128], identb[:], [khb[dr], identb], [psT])
                    if tl % 4 == 3 or tl == NT // 128 - 1:
                        t0_ = (tl // 4) * 4
                        nn = tl - t0_ + 1
                        k.copy("act", khtok[dr][:, t0_:tl + 1, :].rearrange("p a b -> p (a b)"), psT[:, 0:nn * 128], [psT], [khtok[dr]])
            for ci in range(NCH):
                par = ci % 2
                tl = ci // 2
                pr = slice(64 * par, 64 * par + 64)
                tsl = slice(ci * CH, (ci + 1) * CH)
                bank_pos = ci % 8
                for dr in range(2):
                    am = Am[(ci % 2) * 2 + dr]
                    pa = psAd[dr][pr, (ci % 8) * 64:(ci % 8 + 1) * 64]
                    k.mm(pa, ktb[dr][:, tsl], qtb[dr][:, tsl], True, True, [ktb[dr], qtb[dr]], [psA_r[dr]])
                    k.tt("dve", am[pr, :], pa, msk[pr, 0:64], ALU.mult, [psA_r[dr], msk], [am])
                    po = psOo[dr][:, bank_pos * 64:(bank_pos + 1) * 64]
                    k.mm(po, Vh[dr][pr, tl, :], am[pr, :], True, ci == 0, [Vh[dr], am], [psOo[dr]])
                    if ci > 0:
                        k.mm(po, Sb[dr][:], qtb[dr][:, tsl], False, True, [Sb[dr], qtb[dr]], [psOo[dr]])
                    ps_ = psSd[dr][:, (ci % 4) * 128:(ci % 4 + 1) * 128]
                    k.mm(ps_, khtok[dr][pr, tl, :], Vh[dr][pr, tl, :], True, True, [khtok[dr], Vh[dr]], [psS_r[dr]])
                    if ci == 0:
                        k.copy("dve", S[dr][:], ps_, [psS_r[dr]], [S[dr]])
                    else:
                        k.stt(S[dr][:], S[dr][:], ecl[dr][:, ci:ci + 1], ps_, ALU.mult, ALU.add, [S[dr], ecl[dr], psS_r[dr]], [S[dr]])
                    k.copy("act", Sb[dr][:], S[dr][:], [S[dr]], [Sb[dr]])
                    if bank_pos == 7 or (ci == 3):
                        pass
                done = None
                if ci == 3:
                    done = (0, 256)
                elif ci > 3 and (ci - 4) % 8 == 7:
                    done = ((ci - 7) * CH, 512)
                if done is not None:
                    lo, N = done
                    for dr in range(2):
                        c0 = ((lo // CH) % 8) * 64
                        if c0 + N <= 512:
                            srcs = [(psOo[dr][:, c0:c0 + N], lo, N)]
                        else:
                            n1 = 512 - c0
                            srcs = [(psOo[dr][:, c0:512], lo, n1), (psOo[dr][:, 0:N - n1], lo + n1, N - n1)]
                        for (sp_, l0, n_) in srcs:
                            if dr == 0:
                                dst = oacc[:, l0:l0 + n_]
                            else:
                                a0, a1 = rev_tprime(l0, n_)
                                dst = oacc[:, a0:a1][:, ::-1]
                            k.tt("dve", dst, sp_, dst, ALU.add, [psOo[dr], oacc], [oacc])
            k.dma("sp", raw[:], G.PQ_d[b, 3, h, :, :], reads=[G.PQ_d], writes=[raw])
            k.actf(tA[:], raw[:], AF.Silu, [raw], [tA])
            for (t0, N) in [(0, 256)] + [(CTX + i * 512, 512) for i in range(4)]:
                k.actf(sqb[:, 0:N], oacc[:, t0:t0 + N], AF.Square, [oacc], [sqb])
                k.mm(psM[:, 0:N], ones128[:], sqb[:, 0:N], True, True, [ones128, sqb], [psM])
                k.actf(rinv[:, 0:N], psM[:, 0:N], AF.Sqrt, [psM], [rinv], bias=G.epsrms[:, 0:1])
                k.op("dve", lambda e: e.reciprocal(rinv[:, 0:N], rinv[:, 0:N]), [rinv], [rinv])
                k.tt("dve", rinv[:, 0:N], rinv[:, 0:N], oacc[:, t0:t0 + N], ALU.mult, [rinv, oacc], [rinv])
                k.stt(OgT[:, h, t0:t0 + N], rinv[:, 0:N], ng[:, h:h + 1], tA[:, t0:t0 + N], ALU.mult, ALU.mult, [rinv, ng, tA], [OgT])
        k.pop()
        out_proj_ln(k, G, li, b, Hin, Hout, OgT, G.WhgoB, need_ctx)
        k.pop()
    k.pop()


def build_program(NB):
    nc = bass.Bass("TRN2", target_bir_lowering=False)
    k = K(nc)
    G = setup_globals(k, NB)
    phase_adaln(k, G)
    cur = G.H[0]
    for li in range(DEPTH):
        need_ctx = li < DEPTH - 1
        mid = G.H[1]
        nxt = G.H[2]
        kind = li % 3
        if kind == 0:
            attention(k, G, li, cur, mid, need_ctx)
        elif kind == 1:
            s5_layer(k, G, li, cur, mid, need_ctx)
        else:
            hgrn2_layer(k, G, li, cur, mid, need_ctx)
        peer_cast(k, G, li)
        peer_route(k, G, li, mid, need_ctx)
        peer_experts2(k, G, li, mid, nxt, need_ctx, out_final=(li == DEPTH - 1))
        cur = nxt
    k.finish()
    return nc, k


def kernel(**inp):
    inp = {k_: np.asarray(v) for k_, v in inp.items()}
    NCORES = 8
    B = inp["x"].shape[0]
    NB = B // NCORES
    nc, k = build_program(NB)
    w = host_weights(inp)
    w.update(host_weights_attn(inp))
    w.update(host_weights_s5(inp))
    w.update(host_weights_hg(inp))
    in_maps = []
    for c in range(NCORES):
        m = dict(w)
        m.update(host_core_inputs(inp, c * NB, NB))
        in_maps.append(m)
    res = run_bass_kernel_spmd(nc, in_maps, core_ids=list(range(NCORES)))
    outs = [unfm(r["outT"]) for r in res.results]
    return np.ascontiguousarray(np.concatenate(outs, axis=0)).astype(np.float32)


def peer_experts2(k, G, li, Hin, Hout, need_ctx, out_final=False):
    k.push()
    NBUF = 3
    NA = 3
    SB_ = 8
    hT = k.sb("hT", [128, 8, 256])
    uTb = k.sb("uTb", [128, 8, 256], BF16)
    rr = k.sb("rr", [128, 8, 256])
    tmp = dict(sq=hT, mean=k.sb("mean", [128, 256]), rstd=k.sb("rstd", [128, 256]))
    Rt1 = [k.sb("Rt", [128, 384]) for _ in range(2)]
    Rt = [Rt1, Rt1]
    RTt = [k.sb("RTt", [128, 3, 256]) for _ in range(2)]
    Q = [k.sb("Q", [128, 128], BF16) for _ in range(SB_)]
    Pg = [k.sb("Pg", [128, 128], BF16) for _ in range(SB_)]
    GT = [k.sb("GT", [128, 256, 128], BF16) for _ in range(2)]
    ut = [k.sb("ut", [128, 2, 8, 128], BF16) for _ in range(NBUF)]
    vt = [k.sb("vt", [128, 2, 1024], BF16) for _ in range(NBUF)]
    ga = [k.sb("ga", [128, 256], BF16) for _ in range(NA)]
    gam = [k.sb("gam", [128, 256], BF16) for _ in range(NA)]
    psO = [k.ps("psO", [128, 512]) for _ in range(4)]
    psA2 = [k.ps("psA", [128, 512]) for _ in range(NA)]
    psGb = k.ps("psGb", [128, 512])
    psL = psA2[0]
    blocks = list(blocks_of(G, need_ctx))
    NBLK = len(blocks)
    st = dict(nev=0)

    def prologue_R(n):
        b, blk, tok0, row = blocks[n]
        for tt in range(2):
            t0 = tok0 + tt * 128
            r_ = Rt[n % 2][tt]
            k.dma("sp", r_[:], G.RT[b, t0:t0 + 128, :], reads=[G.RT], writes=[r_])
            for a in range(3):
                k.tr(psGb[:, a * 128:(a + 1) * 128], r_[:, a * 128:(a + 1) * 128], G.ident[:], [r_, G.ident], [psGb])
            k.copy("act", RTt[n % 2][:, :, tt * 128:(tt + 1) * 128], psGb[:, 0:384].rearrange("p (a t) -> p a t", a=3), [psGb], [RTt[n % 2]])

    def gbuild_steps(n):
        R_ = RTt[n % 2]
        G_ = GT[n % 2]
        steps = []
        for sub in range(128):
            def dve_part(sub=sub):
                for tok in (2 * sub, 2 * sub + 1):
                    q_, p_ = Q[tok % SB_], Pg[tok % SB_]
                    k.ts("dve", q_[:], G.iota3[:, 0, :], R_[:, 1, tok:tok + 1], None,
                         ALU.is_equal, reads=[G.iota3, R_], writes=[q_])
                    k.ts("dve", p_[:], G.iota3[:, 0, :], R_[:, 0, tok:tok + 1], R_[:, 2, tok:tok + 1],
                         ALU.is_equal, ALU.mult, reads=[G.iota3, R_], writes=[p_])

            def pe_part(sub=sub):
                for tok in (2 * sub, 2 * sub + 1):
                    q_, p_ = Q[tok % SB_], Pg[tok % SB_]
                    k.mm(psGb[:, (tok % 4) * 128:(tok % 4 + 1) * 128], q_[:], p_[:], True, True, [q_, p_], [psGb])

            def evac_part(sub=sub):
                tok = 2 * sub + 1
                if tok % 4 == 3:
                    k.copy("act", G_[:, tok - 3:tok + 1, :].rearrange("p t i -> p (t i)"), psGb[:], [psGb], [G_])
            steps.append((dve_part, pe_part, evac_part))
        return steps

    def modulate(n):
        b, blk, tok0, row = blocks[n]
        k.dma("sp", hT[:], Hin[b, :, :, tok0:tok0 + 256], reads=[Hin], writes=[hT])
        for c in range(8):
            k.actf(uTb[:, c, :], hT[:, c, :], AF.Identity, [hT, G.mod1, G.modT], [uTb],
                   scale=G.mod1[:, li, seg(4, c), row:row + 1], bias=G.modT[:, li, seg(3, c), row:row + 1])

    def load_tab(ii):
        k.dma("sp", ut[ii % NBUF][:].rearrange("p a c j -> p a (c j)"), G.UB.h[:, 2 * ii:2 * ii + 2, :],
              reads=[G.UB.sub[ii // 4]], writes=[ut[ii % NBUF]])
        k.dma("sp", vt[ii % NBUF][:], G.VB.h[:, 2 * ii:2 * ii + 2, :], reads=[G.VB.sub[ii // 4]], writes=[vt[ii % NBUF]])

    def amm(i):
        u_ = ut[(i // 2) % NBUF]
        pa = psA2[i % NA][:, 0:256]
        for c in range(8):
            k.mm(pa, u_[:, i % 2, c, :], uTb[:, c, :], c == 0, c == 7, [u_, uTb], [psA2[i % NA]])

    def expert_loop(n, steps):
        G_ = GT[n % 2]
        for ii in range(NBUF - 1):
            load_tab(ii)
        amm(0)
        amm(1)
        every = 128 // len(steps) if steps else 0
        si = 0
        for i in range(128):
            if i % 2 == 0 and i // 2 + NBUF - 1 < 64:
                load_tab(i // 2 + NBUF - 1)
            if i + 2 < 128:
                amm(i + 2)
            cur = steps[i] if steps else None
            if cur:
                cur[0]()
                cur[1]()
            v_ = vt[(i // 2) % NBUF]
            pa = psA2[i % NA][:, 0:256]
            k.actf(ga[i % NA][:], pa, AF.Gelu_apprx_tanh, [psA2[i % NA]], [ga[i % NA]])
            k.tt("dve", gam[i % NA][:], ga[i % NA][:], G_[:, :, i], ALU.mult, [ga[i % NA], G_], [gam[i % NA]])
            for dc in range(8):
                k.mm(psO[dc // 2][:, (dc % 2) * 256:(dc % 2 + 1) * 256], v_[:, i % 2, dc * 128:(dc + 1) * 128], gam[i % NA][:],
                     (i == 0 and dc % 2 == 0), i == 127, [v_, gam[i % NA]], [psO[dc // 2]], skip_group_check=True)
            if cur:
                cur[2]()

    def epilogue(n):
        b, blk, tok0, row = blocks[n]
        k.actf(hT[:].rearrange("p c t -> p (c t)"), hT[:].rearrange("p c t -> p (c t)"), AF.Copy, [hT], [hT], scale=ALPHA)
        for c in range(8):
            k.stt(rr[:, c, :], psO[c // 2][:, (c % 2) * 256:(c % 2 + 1) * 256], G.modT[:, li, seg(5, c), row:row + 1],
                  hT[:, c, :], ALU.mult, ALU.add, [psO[c // 2], G.modT, hT], [rr])
        layer_norm_fm(k, G, rr, 256, li, 1, rr, psL, tmp)
        if out_final:
            k.dma("act", G.out[b, :, :, tok0 - CTX:tok0 - CTX + 256], rr[:], reads=[rr], writes=[G.out])
        else:
            k.dma("act", Hout[b, :, :, tok0:tok0 + 256], rr[:], reads=[rr], writes=[Hout])

    prologue_R(0)
    for s_ in gbuild_steps(0):
        s_[0]()
        s_[1]()
        s_[2]()
    modulate(0)
    for n in range(NBLK):
        nxt = n + 1 < NBLK
        steps = []
        if nxt:
            prologue_R(n + 1)
            steps = gbuild_steps(n + 1)
        expert_loop(n, steps)
        epilogue(n)
        if nxt:
            modulate(n + 1)
    k.pop()
```
